# Optimizing a Trainium2 kernel written in Bass

```python
import math
import jax, jax.numpy as jnp
from jax import lax
import numpy as np

D_MODEL = 1024
BATCH = 2
SEQ = 16384
DEPTH = 2

N_MEM = 256
N_MIXERS = 2
N_RET_LAYERS = (DEPTH + 1) // 2
N_DIFF_LAYERS = DEPTH // 2

RET_HEADS = 8
RET_QK_DIM = D_MODEL // RET_HEADS
RET_V_DIM = 2 * RET_QK_DIM
RET_WIDTH = RET_HEADS * RET_V_DIM
RET_CHUNK = 128
RET_THETA = 10000.0

DIFF_HEADS = 8
DIFF_QK_DIM = D_MODEL // DIFF_HEADS
DIFF_V_DIM = 2 * DIFF_QK_DIM
DIFF_WIDTH = DIFF_HEADS * DIFF_V_DIM
DIFF_QBLOCK = 128
ROPE_THETA = 500000.0
ROPE_DIM = DIFF_QK_DIM // 4

MEM_HEADS = 4
MEM_HEAD_DIM = D_MODEL // MEM_HEADS
MEM_WIDTH = MEM_HEADS * MEM_HEAD_DIM

RET_IN = 2 * RET_HEADS * RET_QK_DIM + 2 * RET_WIDTH + 2 * MEM_WIDTH
DIFF_IN = 2 * (2 * DIFF_HEADS * DIFF_QK_DIM) + 2 * DIFF_WIDTH + 2 * MEM_WIDTH
OUT_WIDTH = RET_WIDTH + MEM_WIDTH
EPS = 1e-6

kernel_name = "hybrid_retention_diffattn_memory_encoder"


def rms_norm(x, g=None):
    xf = x.astype(jnp.float32)
    y = xf * lax.rsqrt(jnp.mean(xf * xf, axis=-1, keepdims=True) + EPS)
    if g is not None:
        y = y * g.astype(jnp.float32)
    return y.astype(x.dtype)


def split_cols(x, sizes):
    out, start = [], 0
    for n in sizes:
        out.append(x[..., start:start + n])
        start += n
    return out


def rope_tables(seq, dim, theta):
    inv = 1.0 / (theta ** (jnp.arange(0, dim, 2, dtype=jnp.float32) / dim))
    ang = jnp.arange(seq, dtype=jnp.float32)[:, None] * inv[None, :]
    return jnp.cos(ang), jnp.sin(ang)


def apply_rotary(x, cos, sin):
    half = x.shape[-1] // 2
    x1, x2 = x[..., :half], x[..., half:]
    c = cos.astype(x.dtype)
    s = sin.astype(x.dtype)
    return jnp.concatenate([x1 * c - x2 * s, x1 * s + x2 * c], axis=-1)


def partial_rotary(x, cos, sin):
    return jnp.concatenate([apply_rotary(x[..., :ROPE_DIM], cos, sin), x[..., ROPE_DIM:]], axis=-1)


def retention_one_direction(q, k, v, log_gamma, strict):
    b, h, s, dk = q.shape
    dv = v.shape[-1]
    c = RET_CHUNK
    n = s // c
    qc = q.reshape(b, h, n, c, dk)
    kc = k.reshape(b, h, n, c, dk)
    vc = v.reshape(b, h, n, c, dv)
    pos = jnp.arange(c, dtype=jnp.float32)
    lg = log_gamma[:, None]
    rel = pos[:, None] - pos[None, :]
    mask = (rel > 0) if strict else (rel >= 0)
    decay = jnp.exp(lg[:, :, None] * jnp.where(mask, rel, 0.0)[None]) * mask[None]
    scores = jnp.einsum('bhncd,bhnmd->bhncm', qc, kc) * decay[:, None].astype(q.dtype)
    intra = jnp.einsum('bhncm,bhnme->bhnce', scores, vc)
    q_decay = jnp.exp(lg * (pos + 1.0)).astype(q.dtype)
    k_decay = jnp.exp(lg * (c - 1.0 - pos)).astype(k.dtype)
    kv = jnp.einsum('bhncd,bhnce->nbhde', kc * k_decay[None, :, None, :, None], vc)
    qs = (qc * q_decay[None, :, None, :, None]).transpose(2, 0, 1, 3, 4)
    chunk_decay = jnp.exp(log_gamma * c).astype(kv.dtype)[None, :, None, None]

    def step(state, inp):
        q_n, kv_n = inp
        out = jnp.einsum('bhcd,bhde->bhce', q_n, state)
        return state * chunk_decay + kv_n, out

    init = jnp.zeros((b, h, dk, dv), dtype=kv.dtype)
    _, cross = lax.scan(step, init, (qs, kv))
    cross = cross.transpose(1, 2, 0, 3, 4)
    return (intra + cross).reshape(b, h, s, dv)


def memory_cross_attention(qm, mem_k, mem_v):
    b, s, _ = qm.shape
    q = qm.reshape(b, s, MEM_HEADS, MEM_HEAD_DIM)
    sc = jnp.einsum('bshd,bmhd->bhsm', q, mem_k).astype(jnp.float32) * (MEM_HEAD_DIM ** -0.5)
    p = jax.nn.softmax(sc, axis=-1).astype(qm.dtype)
    o = jnp.einsum('bhsm,bmhd->bshd', p, mem_v)
    return o.reshape(b, s, MEM_WIDTH)


def retention_layer(hn, mem_k, mem_v, w_in, w_out, decay_fwd, decay_bwd, rcos, rsin):
    b, s, _ = hn.shape
    proj = hn @ w_in
    q, k, v, g, qm, gm = split_cols(proj, [RET_HEADS * RET_QK_DIM, RET_HEADS * RET_QK_DIM,
                                           RET_WIDTH, RET_WIDTH, MEM_WIDTH, MEM_WIDTH])
    q = q.reshape(b, s, RET_HEADS, RET_QK_DIM).transpose(0, 2, 1, 3)
    k = k.reshape(b, s, RET_HEADS, RET_QK_DIM).transpose(0, 2, 1, 3) * (RET_QK_DIM ** -0.5)
    v = v.reshape(b, s, RET_HEADS, RET_V_DIM).transpose(0, 2, 1, 3)
    q = apply_rotary(q, rcos, rsin)
    k = apply_rotary(k, rcos, rsin)
    lg_f = jax.nn.log_sigmoid(decay_fwd.astype(jnp.float32))
    lg_b = jax.nn.log_sigmoid(decay_bwd.astype(jnp.float32))
    y_f = retention_one_direction(q, k, v, lg_f, strict=False)
    flip = lambda t: jnp.flip(t, axis=2)
    y_b = flip(retention_one_direction(flip(q), flip(k), flip(v), lg_b, strict=True))
    y = rms_norm(y_f + y_b)
    y = y.transpose(0, 2, 1, 3).reshape(b, s, RET_WIDTH) * jax.nn.silu(g)
    m = memory_cross_attention(qm, mem_k, mem_v) * jax.nn.silu(gm)
    return jnp.concatenate([y, m], axis=-1) @ w_out


def diff_layer(hn, mem_k, mem_v, w_in, w_out, lq1, lk1, lq2, lk2, subln_g, lambda_init, dcos, dsin):
    b, s, _ = hn.shape
    proj = hn @ w_in
    qk_w = 2 * DIFF_HEADS * DIFF_QK_DIM
    q, k, v, g, qm, gm = split_cols(proj, [qk_w, qk_w, DIFF_WIDTH, DIFF_WIDTH, MEM_WIDTH, MEM_WIDTH])
    q = q.reshape(b, s, DIFF_HEADS, 2, DIFF_QK_DIM).transpose(0, 2, 3, 1, 4)
    k = k.reshape(b, s, DIFF_HEADS, 2, DIFF_QK_DIM).transpose(0, 2, 3, 1, 4)
    v = v.reshape(b, s, DIFF_HEADS, DIFF_V_DIM).transpose(0, 2, 1, 3)
    q = partial_rotary(q, dcos, dsin)
    k = partial_rotary(k, dcos, dsin)
    f32 = jnp.float32
    lam = (jnp.exp(jnp.sum(lq1.astype(f32) * lk1.astype(f32)))
           - jnp.exp(jnp.sum(lq2.astype(f32) * lk2.astype(f32))) + lambda_init)
    nb = s // DIFF_QBLOCK
    qb = q.reshape(b, DIFF_HEADS, 2, nb, DIFF_QBLOCK, DIFF_QK_DIM).transpose(3, 0, 1, 2, 4, 5)
    scale = DIFF_QK_DIM ** -0.5

    def block(qblk):
        sc = jnp.einsum('bhcqd,bhckd->bhcqk', qblk, k).astype(f32) * scale
        p = jax.nn.softmax(sc, axis=-1)
        a = (p[:, :, 0] - lam * p[:, :, 1]).astype(v.dtype)
        return jnp.einsum('bhqk,bhke->bhqe', a, v)

    o = lax.map(block, qb)
    o = o.transpose(1, 2, 0, 3, 4).reshape(b, DIFF_HEADS, s, DIFF_V_DIM)
    o = rms_norm(o, subln_g) * (1.0 - lambda_init)
    o = o.transpose(0, 2, 1, 3).reshape(b, s, DIFF_WIDTH) * jax.nn.silu(g)
    m = memory_cross_attention(qm, mem_k, mem_v) * jax.nn.silu(gm)
    return jnp.concatenate([o, m], axis=-1) @ w_out


def setup_inputs(seed: int = 0) -> dict:
    key = jax.random.key(seed)
    ks = jax.random.split(key, 20)
    f32 = jnp.float32
    nrm = lambda k, shape, scale: jax.random.normal(k, shape, f32) * scale
    base_logit = jnp.log(2.0 ** (5.0 + jnp.arange(RET_HEADS, dtype=f32)) - 1.0)
    return {
        "x": nrm(ks[0], (BATCH, SEQ, D_MODEL), 1.0),
        "mem": nrm(ks[1], (BATCH, N_MEM, D_MODEL), 1.0),
        "norm_g": 1.0 + nrm(ks[2], (DEPTH, D_MODEL), 0.02),
        "mem_norm_g": 1.0 + nrm(ks[3], (DEPTH, D_MODEL), 0.02),
        "mem_w_kv": nrm(ks[4], (DEPTH, D_MODEL, 2 * MEM_WIDTH), D_MODEL ** -0.5),
        "ret_w_in": nrm(ks[5], (N_RET_LAYERS, D_MODEL, RET_IN), D_MODEL ** -0.5),
        "ret_w_out": nrm(ks[6], (N_RET_LAYERS, OUT_WIDTH, D_MODEL), OUT_WIDTH ** -0.5),
        "ret_decay_fwd": base_logit[None] + nrm(ks[7], (N_RET_LAYERS, RET_HEADS), 0.05),
        "ret_decay_bwd": base_logit[None] + nrm(ks[8], (N_RET_LAYERS, RET_HEADS), 0.05),
        "diff_w_in": nrm(ks[9], (N_DIFF_LAYERS, D_MODEL, DIFF_IN), D_MODEL ** -0.5),
        "diff_w_out": nrm(ks[10], (N_DIFF_LAYERS, OUT_WIDTH, D_MODEL), OUT_WIDTH ** -0.5),
        "diff_lambda_q1": nrm(ks[11], (N_DIFF_LAYERS, DIFF_QK_DIM), 0.1),
        "diff_lambda_k1": nrm(ks[12], (N_DIFF_LAYERS, DIFF_QK_DIM), 0.1),
        "diff_lambda_q2": nrm(ks[13], (N_DIFF_LAYERS, DIFF_QK_DIM), 0.1),
        "diff_lambda_k2": nrm(ks[14], (N_DIFF_LAYERS, DIFF_QK_DIM), 0.1),
        "diff_subln_g": 1.0 + nrm(ks[15], (N_DIFF_LAYERS, DIFF_V_DIM), 0.02),
        "final_norm_g": 1.0 + nrm(ks[16], (D_MODEL,), 0.02),
    }


def reference(x, mem, norm_g, mem_norm_g, mem_w_kv, ret_w_in, ret_w_out, ret_decay_fwd,
              ret_decay_bwd, diff_w_in, diff_w_out, diff_lambda_q1, diff_lambda_k1,
              diff_lambda_q2, diff_lambda_k2, diff_subln_g, final_norm_g):
    b, s, _ = x.shape
    rcos, rsin = rope_tables(s, RET_QK_DIM, RET_THETA)
    dcos, dsin = rope_tables(s, ROPE_DIM, ROPE_THETA)
    h = x
    for i in range(DEPTH):
        hn = rms_norm(h, norm_g[i])
        mn = rms_norm(mem, mem_norm_g[i])
        mkv = mn @ mem_w_kv[i]
        mem_k = mkv[..., :MEM_WIDTH].reshape(b, N_MEM, MEM_HEADS, MEM_HEAD_DIM)
        mem_v = mkv[..., MEM_WIDTH:].reshape(b, N_MEM, MEM_HEADS, MEM_HEAD_DIM)
        j = i // N_MIXERS
        if i % N_MIXERS == 0:
            delta = retention_layer(hn, mem_k, mem_v, ret_w_in[j], ret_w_out[j],
                                    ret_decay_fwd[j], ret_decay_bwd[j], rcos, rsin)
        else:
            lambda_init = 0.8 - 0.6 * math.exp(-0.3 * i)
            delta = diff_layer(hn, mem_k, mem_v, diff_w_in[j], diff_w_out[j],
                               diff_lambda_q1[j], diff_lambda_k1[j], diff_lambda_q2[j],
                               diff_lambda_k2[j], diff_subln_g[j], lambda_init, dcos, dsin)
        h = h + delta
    return rms_norm(h, final_norm_g)
```

```python
import numpy as np
import concourse.bass as bass
import concourse.mybir as mybir
from concourse.bass_utils import run_bass_kernel_spmd

F32 = mybir.dt.float32
BF16 = mybir.dt.bfloat16
I32 = mybir.dt.int32
AF = mybir.ActivationFunctionType
ALU = mybir.AluOpType
AX = mybir.AxisListType

SEM_ROLL = 30000


class Buf:
    __slots__ = ("name", "w", "r", "excl")

    def __init__(self, name="", excl=False):
        self.name = name
        self.excl = excl
        self.w = None
        self.r = {}


class Eng:
    def __init__(self, K, name, e):
        self.K = K
        self.name = name
        self.e = e
        self.sem = K.nc.alloc_semaphore(f"s_{name}_0")
        self.nsem = 1
        self.cnt = 0
        self.waited = {}

    def roll(self):
        if self.cnt >= SEM_ROLL:
            self.sem = self.K.nc.alloc_semaphore(f"s_{self.name}_{self.nsem}")
            self.nsem += 1
            self.cnt = 0


class K:
    def __init__(self, nc, n_dma_sems=12):
        self.nc = nc
        self.eng = {
            "pe": Eng(self, "pe", nc.tensor),
            "act": Eng(self, "act", nc.scalar),
            "dve": Eng(self, "dve", nc.vector),
            "pool": Eng(self, "pool", nc.gpsimd),
            "sp": Eng(self, "sp", nc.sync),
        }
        self.dsems = {}
        for q in ("sp", "act", "pool"):
            self.dsems[q] = [[nc.alloc_semaphore(f"d_{q}_{i}"), 0] for i in range(n_dma_sems)]
        self.dnext = {"sp": 0, "act": 0, "pool": 0}
        self.out_tokens = []
        self.n_instr = 0

    def _wait(self, E, tok):
        sem, val = tok
        key = id(sem)
        if E.waited.get(key, 0) < val:
            E.e.wait_ge(sem, val)
            E.waited[key] = val

    def _deps(self, E, reads, writes, skip_own=False):
        toks = []
        for b in reads:
            if b.w is not None:
                toks.append(b.w)
        for b in writes:
            if b.w is not None:
                toks.append(b.w)
            toks.extend(b.r.values())
        for t in toks:
            if skip_own and t[0] is E.sem:
                continue
            self._wait(E, t)

    def _mark(self, tok, reads, writes):
        for b in reads:
            cur = b.r.get(id(tok[0]))
            if cur is None or cur[1] < tok[1]:
                b.r[id(tok[0])] = tok
        for b in writes:
            b.w = tok
            b.r = {}

    def op(self, eng, fn, reads=(), writes=(), inc=True):
        E = self.eng[eng]
        if any(b.excl for b in reads):
            writes = list(writes) + [b for b in reads if b.excl]
            reads = [b for b in reads if not b.excl]
        self._deps(E, reads, writes, skip_own=(eng == "pe"))
        ins = fn(E.e)
        self.n_instr += 1
        if inc:
            E.cnt += 1
            ins.then_inc(E.sem, 1)
            tok = (E.sem, E.cnt)
        else:
            tok = (E.sem, E.cnt + 1)
        self._mark(tok, reads, writes)
        if inc:
            E.roll()
        return tok

    def dma(self, q, out, in_, reads=(), writes=(), is_output=False, **kw):
        E = self.eng[q]
        self._deps(E, reads, writes)
        pool = self.dsems[q]
        i = self.dnext[q]
        self.dnext[q] = (i + 1) % len(pool)
        slot = pool[i]
        if slot[1] > 0:
            self._wait(E, (slot[0], slot[1]))
        ins = E.e.dma_start(out=out, in_=in_, **kw)
        slot[1] += 16
        ins.then_inc(slot[0], 16)
        tok = (slot[0], slot[1])
        self.n_instr += 1
        self._mark(tok, reads, writes)
        if is_output:
            self.out_tokens.append(tok)
        return tok

    def finish(self):
        E = self.eng["sp"]
        for q in self.dsems:
            for slot in self.dsems[q]:
                if slot[1] > 0:
                    self._wait(E, (slot[0], slot[1]))
        for n in ("pe", "act", "dve", "pool"):
            e2 = self.eng[n]
            if e2.cnt > 0:
                self._wait(E, (e2.sem, e2.cnt))


def _sb(self, name, shape, dt=F32):
    stack = getattr(self, "_stack", None)
    if stack is None:
        return self.nc.alloc_sbuf_tensor(name, list(shape), dt).ap()
    self._nsb = getattr(self, "_nsb", 0) + 1
    h = stack.enter_context(self.nc.sbuf_tensor(f"{name}_{self._nsb}", list(shape), dt))
    return h.ap()


def _phase_begin(self):
    import contextlib
    self._stack = contextlib.ExitStack()


def _barrier(self):
    toks = []
    for n in ("pe", "act", "dve", "pool", "sp"):
        e2 = self.eng[n]
        if e2.cnt > 0:
            toks.append((e2.sem, e2.cnt))
    for q in self.dsems:
        for slot in self.dsems[q]:
            if slot[1] > 0:
                toks.append((slot[0], slot[1]))
    toks.extend(getattr(self, "cc_tokens", []))
    for n in ("pe", "act", "dve", "pool", "sp"):
        E = self.eng[n]
        for t in toks:
            if t[0] is E.sem:
                continue
            self._wait(E, t)


def _phase_end(self):
    self.barrier()
    self._stack.close()
    self._stack = None


K.phase_begin = _phase_begin
K.phase_end = _phase_end
K.barrier = _barrier


def _ps(self, name, shape, dt=F32):
    return self.nc.alloc_psum_tensor(name, list(shape), dt).ap()


K.sb = _sb
K.ps = _ps


def _mm(self, out, lhsT, rhs, start, stop, reads, writes, inc=None, **kw):
    if inc is None:
        inc = stop
    return self.op("pe", lambda e: e.matmul(out, lhsT=lhsT, rhs=rhs, start=start, stop=stop, **kw),
                   reads, writes, inc=inc)


def _tr(self, out, in_, ident, reads, writes, inc=True):
    return self.op("pe", lambda e: e.transpose(out=out, in_=in_, identity=ident), reads, writes, inc=inc)


def _act(self, out, in_, func, reads, writes, **kw):
    return self.op("act", lambda e: e.activation(out=out, in_=in_, func=func, **kw), reads, writes)


def _tt(self, eng, out, in0, in1, op, reads, writes):
    return self.op(eng, lambda e: e.tensor_tensor(out=out, in0=in0, in1=in1, op=op), reads, writes)


def _ts(self, eng, out, in0, s1, s2, op0, op1, reads, writes):
    if op1 is None and eng == "pool" and op0 == ALU.mult:
        op1, s2 = ALU.mult, 1.0
    if op1 is None:
        return self.op(eng, lambda e: e.tensor_scalar(out=out, in0=in0, scalar1=s1, scalar2=None, op0=op0),
                       reads, writes)
    return self.op(eng, lambda e: e.tensor_scalar(out=out, in0=in0, scalar1=s1, scalar2=s2, op0=op0, op1=op1),
                   reads, writes)


def _stt(self, out, in0, scalar, in1, op0, op1, reads, writes):
    return self.op("dve", lambda e: e.scalar_tensor_tensor(out=out, in0=in0, scalar=scalar, in1=in1,
                                                           op0=op0, op1=op1), reads, writes)


def _cp(self, eng, out, in_, reads, writes):
    if eng == "act":
        return self.op("act", lambda e: e.copy(out=out, in_=in_), reads, writes)
    return self.op(eng, lambda e: e.tensor_copy(out=out, in_=in_), reads, writes)


def _recip(self, out, in_, reads, writes):
    return self.op("dve", lambda e: e.reciprocal(out=out, in_=in_), reads, writes)


def _memset(self, eng, ap, val, writes):
    return self.op(eng, lambda e: e.memset(ap, val), (), writes)


K.mm = _mm
K.tr = _tr
K.act = _act
K.tt = _tt
K.ts = _ts
K.stt = _stt
K.cp = _cp
K.recip = _recip
K.memset = _memset


class Ring:
    def __init__(self, k, name, shape, dt, n):
        self.aps = [k.sb(f"{name}{i}", shape, dt) for i in range(n)]
        self.bufs = [Buf(f"{name}{i}") for i in range(n)]
        self.n = n
        self.i = -1

    def next(self):
        self.i = (self.i + 1) % self.n
        return self.aps[self.i], self.bufs[self.i]

    def cur(self):
        return self.aps[self.i], self.bufs[self.i]


class Psum:
    def __init__(self, k):
        self.banks = [k.ps(f"bank{i}", [128, 512], F32) for i in range(8)]
        self.bufs = [Buf(f"bank{i}", excl=True) for i in range(8)]

    def f32(self, i):
        return self.banks[i]

    def bf16(self, i):
        return self.banks[i].bitcast(BF16)


def _collective(self, kind, in_ap, out_ap, groups, reads=(), writes=()):
    E = self.eng["pool"]
    self._deps(E, reads, writes)
    if not hasattr(self, "cc_sem"):
        self.cc_sem = self.nc.alloc_semaphore("cc_sem")
        self.cc_cnt = 0
        self.cc_tokens = []
    ins = E.e.collective_compute(kind, ALU.bypass, replica_groups=groups, ins=[in_ap.opt()], outs=[out_ap.opt()])
    ins.then_inc(self.cc_sem)
    self.cc_cnt += 1
    tok = (self.cc_sem, self.cc_cnt)
    self.n_instr += 1
    self._mark(tok, reads, writes)
    self.cc_tokens = [tok]
    return tok


K.collective = _collective


S = 16384
D = 1024
NT = S // 128
EPS = 1e-6


def norm_to_T(k, P, xt, xb, gt, gb, ident, identb, n_feat, scr, hnT, hnTb, pbank):
    junk, junkb, ss, ssb, rstd, rstdb, hn, hnb = scr
    k.act(junk, xt, AF.Square, [xb], [junkb, ssb], accum_out=ss)
    k.act(rstd, ss, AF.Sqrt, [ssb], [rstdb], scale=1.0 / n_feat, bias=EPS)
    k.recip(rstd, rstd, [rstdb], [rstdb])
    k.stt(hn, xt, rstd, gt, ALU.mult, ALU.mult, [xb, rstdb, gb], [hnb])
    nch = n_feat // 128
    pT = P.bf16(pbank)
    for c in range(nch):
        k.tr(pT[:, c * 128:(c + 1) * 128], hn[:, c * 128:(c + 1) * 128], ident, [hnb, identb], [P.bufs[pbank]],
             inc=(c == nch - 1))
    k.cp("act", hnT, pT[:, 0:nch * 128].rearrange("p (c t) -> p c t", c=nch), [P.bufs[pbank]], [hnTb])


def make_ident(k):
    io = k.sb("io_id", [128, 128], F32)
    iob = Buf("io")
    idf = k.sb("identf", [128, 128], F32)
    ident = k.sb("ident", [128, 128], BF16)
    identb = Buf("ident")
    k.op("pool", lambda e: e.iota(io, pattern=[[1, 128]], base=0, channel_multiplier=-1,
                                  allow_small_or_imprecise_dtypes=True), (), [iob])
    k.ts("dve", idf, io, 0.0, None, ALU.is_equal, None, [iob], [iob])
    k.cp("dve", ident, idf, [iob], [identb])
    return ident, identb, io, iob


def mem_prep(k, P, nc, mem, mng, wkv, nheads, ident, identb, scr, gt, gb):
    mkT = k.sb("mkT", [128, nheads, 2, 256], BF16)
    mkTb = Buf("mkT")
    mva = k.sb("mva", [128, nheads, 2, 258], BF16)
    mvab = Buf("mva")
    mt_x = k.sb("mem_x", [128, D], F32)
    mt_xb = Buf("mem_x")
    mg = k.sb("mem_g", [128, D], F32)
    mgb = Buf("mem_g")
    mnT = k.sb("mnT", [128, 8, 128], BF16)
    mnTb = Buf("mnT")
    wk = k.sb("wkv_sb", [128, 8, 512], BF16)
    wkb = Buf("wkv_sb")
    mkt = k.sb("mk_tok", [128, 256], BF16)
    mktb = Buf("mk_tok")
    k.dma("sp", mg, mng.partition_broadcast(128), (), [mgb])
    k.memset("dve", mva, 1.0, [mvab])
    for hd in range(nheads):
        k.dma("pool", wk, wkv[:, hd * 512:(hd + 1) * 512].rearrange("(c p) n -> p c n", p=128), (), [wkb])
        for mt in range(2):
            k.dma("sp", mt_x, mem[mt * 128:(mt + 1) * 128, :], (), [mt_xb])
            norm_to_T(k, P, mt_x, mt_xb, mg, mgb, ident, identb, D, scr, mnT, mnTb, 0)
            pp = P.f32(1)
            for c in range(8):
                k.mm(pp, mnT[:, c, :], wk[:, c, :], c == 0, c == 7, [mnTb, wkb], [P.bufs[1]])
            k.cp("act", mkt, pp[:, 0:256], [P.bufs[1]], [mktb])
            k.cp("dve", mva[:, hd, mt, 0:256], pp[:, 256:512], [P.bufs[1]], [mvab])
            pT = P.bf16(0)
            for dc in range(2):
                k.tr(pT[:, dc * 128:(dc + 1) * 128], mkt[:, dc * 128:(dc + 1) * 128], ident, [mktb, identb],
                     [P.bufs[0]], inc=(dc == 1))
            k.cp("act", mkT[:, hd, :, mt * 128:(mt + 1) * 128],
                 pT[:, 0:256].rearrange("p (c t) -> p c t", c=2), [P.bufs[0]], [mkTb])
    return mkT, mkTb, mva, mvab


def mem_attn(k, P, qm_src, qm_srcb, sgm, sgmb, mkT_h, mkTb, mva_h, mvab, ident, identb, tmp, banks, out, outb):
    qmT, qmTb, E, Eb, rs, rsb = tmp
    bT, bS, bO = banks
    pT = P.bf16(bT)
    for dc in range(2):
        k.tr(pT[:, dc * 128:(dc + 1) * 128], qm_src[:, dc * 128:(dc + 1) * 128], ident, [qm_srcb, identb],
             [P.bufs[bT]], inc=(dc == 1))
    k.cp("act", qmT, pT[:, 0:256].rearrange("p (c t) -> p c t", c=2), [P.bufs[bT]], [qmTb])
    pS = P.f32(bS)
    for mt in range(2):
        for dc in range(2):
            k.mm(pS[:, mt * 128:(mt + 1) * 128], mkT_h[:, dc, mt * 128:(mt + 1) * 128], qmT[:, dc, :],
                 dc == 0, dc == 1, [mkTb, qmTb], [P.bufs[bS]], inc=(mt == 1 and dc == 1))
    k.act(E, pS[:, 0:256].rearrange("p (c t) -> p c t", c=2), AF.Exp, [P.bufs[bS]], [Eb], scale=1.0 / 16.0)
    pO = P.f32(bO)
    for mt in range(2):
        k.mm(pO[:, 0:257], E[:, mt, :], mva_h[:, mt, 0:257], mt == 0, mt == 1, [Eb, mvab], [P.bufs[bO]])
    k.recip(rs, pO[:, 256:257], [P.bufs[bO]], [rsb])
    k.stt(out, pO[:, 0:256], rs, sgm, ALU.mult, ALU.mult, [P.bufs[bO], rsb, sgmb], [outb])


def phase_A(k, P, nc, ident, identb, io, iob, T, n_tiles):
    x, ng, w, mem, mng, wkv, dec, rcs = (T[n] for n in ("xA", "ngA", "wA", "memA", "mngA", "wkvA", "decA", "rcsA"))
    yg, scrd, sbs = T["yg"], T["scrA"], T["sbsA"]
    yg_mb, yg_yb = T["yg_mb"], T["yg_yb"]
    gather_yg = T["gather_yg"]
    scr_b = [Buf(f"scr{i}") for i in range(n_tiles)]
    sbs_b = [Buf(f"sbs{i}") for i in range(n_tiles)]
    stop = 99
    wt = k.sb("wt", [128, 8, 2048], BF16)
    wtb = Buf("wt")
    for c4 in range(4):
        k.dma("pool", wt[:, :, c4 * 512:(c4 + 1) * 512],
              w[:, c4 * 512:(c4 + 1) * 512].rearrange("(c p) n -> p c n", p=128), (), [wtb])
    gt = k.sb("gt", [128, D], F32)
    gb = Buf("gt")
    k.dma("sp", gt, ng.partition_broadcast(128), (), [gb])

    def mkscr(tag, nf):
        return (k.sb("junk" + tag, [128, nf], F32), Buf(), k.sb("ss" + tag, [128, 1], F32), Buf(),
                k.sb("rstd" + tag, [128, 1], F32), Buf(), k.sb("hn" + tag, [128, nf], BF16), Buf())

    scr = mkscr("a", D)
    mkT, mkTb, mva, mvab = mem_prep(k, P, nc, mem, mng, wkv, 1, ident, identb, scr, gt, gb)

    lg = k.sb("lg", [128, 4], F32)
    lgb = Buf("lg")
    k.dma("sp", lg, dec.partition_broadcast(128), (), [lgb])
    k.act(lg, lg, AF.Exp, [lgb], [lgb], scale=-1.0)
    k.act(lg, lg, AF.Ln, [lgb], [lgb], bias=1.0)
    k.ts("dve", lg, lg, -1.0, None, ALU.mult, None, [lgb], [lgb])
    cb = Buf("consts")
    tmpc = k.sb("tmpc", [128, 128], F32)
    tmpm = k.sb("tmpm", [128, 128], F32)
    tmpe = k.sb("tmpe", [128, 128], F32)
    DT = k.sb("DT", [128, 2, 128], F32)
    QDF = k.sb("QDF", [128, 2, 128], F32)
    QDB = k.sb("QDB", [128, 2, 128], F32)
    kd = k.sb("kd", [128, 4], F32)
    cd = k.sb("cd", [128, 4], F32)
    ci = k.sb("ci", [128, 128], F32)
    cbk = k.sb("cbk", [128, 128], F32)
    pi = k.sb("pi", [128, 2], F32)
    for h in range(2):
        k.ts("dve", tmpc, io, 0.0, None, ALU.max, None, [iob], [cb])
        k.act(tmpe, tmpc, AF.Exp, [cb, lgb], [cb], scale=lg[:, h:h + 1])
        k.ts("dve", tmpm, io, 0.0, None, ALU.is_ge, None, [iob], [cb])
        k.tt("dve", DT[:, h, :], tmpe, tmpm, ALU.mult, [cb], [cb])
        k.ts("dve", tmpc, io, -1.0, 0.0, ALU.mult, ALU.max, [iob], [cb])
        k.act(tmpe, tmpc, AF.Exp, [cb, lgb], [cb], scale=lg[:, 2 + h:3 + h])
        k.ts("dve", tmpm, io, 0.0, None, ALU.is_lt, None, [iob], [cb])
        k.tt("dve", tmpe, tmpe, tmpm, ALU.mult, [cb], [cb])
        k.tt("dve", DT[:, h, :], DT[:, h, :], tmpe, ALU.add, [cb], [cb])
    k.op("pool", lambda e: e.iota(ci, pattern=[[1, 128]], base=1, channel_multiplier=0,
                                  allow_small_or_imprecise_dtypes=True), (), [cb])
    k.op("pool", lambda e: e.iota(cbk, pattern=[[-1, 128]], base=128, channel_multiplier=0,
                                  allow_small_or_imprecise_dtypes=True), (), [cb])
    k.op("pool", lambda e: e.iota(pi[:, 0:1], pattern=[[0, 1]], base=127, channel_multiplier=-1,
                                  allow_small_or_imprecise_dtypes=True), (), [cb])
    k.op("pool", lambda e: e.iota(pi[:, 1:2], pattern=[[0, 1]], base=0, channel_multiplier=1,
                                  allow_small_or_imprecise_dtypes=True), (), [cb])
    for h in range(2):
        k.act(QDF[:, h, :], ci, AF.Exp, [cb, lgb], [cb], scale=lg[:, h:h + 1])
        k.act(QDB[:, h, :], cbk, AF.Exp, [cb, lgb], [cb], scale=lg[:, 2 + h:3 + h])
        k.act(kd[:, h:h + 1], pi[:, 0:1], AF.Exp, [cb, lgb], [cb], scale=lg[:, h:h + 1])
        k.act(kd[:, 2 + h:3 + h], pi[:, 1:2], AF.Exp, [cb, lgb], [cb], scale=lg[:, 2 + h:3 + h])
    k.act(cd, lg, AF.Exp, [lgb], [cb], scale=128.0)

    xr = Ring(k, "xt", [128, D], F32, 2)
    rcr = Ring(k, "rc", [128, 128], F32, 2)
    hnTr = Ring(k, "hnT", [128, 8, 128], BF16, 2)
    stg = Ring(k, "stg", [128, 2048], BF16, 2)
    def mkset(s):
        return dict(
            qk=k.sb("qk" + s, [128, 4, 128], F32), qkb=Buf("qk"),
            t1=k.sb("t1" + s, [128, 4, 64], F32), t2=k.sb("t2" + s, [128, 4, 64], F32), tb=Buf("t12"),
            qkr=k.sb("qkr" + s, [128, 4, 128], BF16), qkrb=Buf("qkr"),
            qmb16=k.sb("qmb16" + s, [128, 256], BF16), qmb16b=Buf("qmb16"),
            sgm=k.sb("sgm" + s, [128, 256], F32), sgmb=Buf("sgm"),
            matmp=(k.sb("qmT" + s, [128, 2, 128], BF16), Buf(), k.sb("Emem" + s, [128, 2, 128], BF16), Buf(),
                   k.sb("rsm" + s, [128, 1], F32), Buf()),
            mgt=k.sb("mgt" + s, [128, 256], BF16), mgtb=Buf("mgt"),
            scr=mkscr("s" + s, D))

    sets = [mkset("0"), mkset("1")]
    mgTr = Ring(k, "mgT", [128, 2, 128], BF16, 2)

    st8 = {}
    rcr4 = Ring(k, "rc4", [128, 128], F32, 4)

    def loadA(i):
        if i < n_tiles:
            xt, xb = xr.next()
            rc, rcb = rcr4.next()
            k.dma("sp", xt, x[i * 128:(i + 1) * 128, :], (), [xb])
            k.dma("sp", rc, rcs[i * 128:(i + 1) * 128, :], (), [rcb])
            st8[i] = dict(xt=xt, xb=xb, rc=rc, rcb=rcb)

    def s0(i):
        d = st8[i]
        junk, junkb, ss, ssb, rstd, rstdb, hn, hnb = sets[i % 2]["scr"]
        k.act(junk, d["xt"], AF.Square, [d["xb"]], [junkb, ssb], accum_out=ss)
        k.act(rstd, ss, AF.Sqrt, [ssb], [rstdb], scale=1.0 / D, bias=EPS)
        k.recip(rstd, rstd, [rstdb], [rstdb])
        k.stt(hn, d["xt"], rstd, gt, ALU.mult, ALU.mult, [d["xb"], rstdb, gb], [hnb])
        d["hn"], d["hnb"] = hn, hnb

    def s1(i):
        d = st8[i]
        hnT, hnTb = hnTr.next()
        pT = P.bf16(0)
        for c in range(8):
            k.tr(pT[:, c * 128:(c + 1) * 128], d["hn"][:, c * 128:(c + 1) * 128], ident, [d["hnb"], identb],
                 [P.bufs[0]], inc=(c == 7))
        k.cp("act", hnT, pT[:, 0:1024].rearrange("p (c t) -> p c t", c=8), [P.bufs[0]], [hnTb])
        for gi in range(4):
            pp = P.f32(1 + gi)
            for c in range(8):
                k.mm(pp, hnT[:, c, :], wt[:, c, gi * 512:(gi + 1) * 512], c == 0, c == 7, [hnTb, wtb],
                     [P.bufs[1 + gi]])

    def s2a(i):
        d = st8[i]
        S_ = sets[i % 2]
        st, stb = stg.next()
        d["st"], d["stb"] = st, stb
        qk, qkb = S_["qk"], S_["qkb"]
        p0 = P.f32(1)
        k.cp("act", qk[:, 0:2, :], p0[:, 0:256].rearrange("p (u d) -> p u d", u=2), [P.bufs[1]], [qkb])
        k.act(qk[:, 2:4, :], p0[:, 256:512].rearrange("p (u d) -> p u d", u=2), AF.Copy, [P.bufs[1]], [qkb],
              scale=128.0 ** -0.5)
        k.cp("act", st[:, 1024:1536], P.f32(2), [P.bufs[2]], [stb])
        k.act(st[:, 1536:2048], P.f32(3), AF.Silu, [P.bufs[3]], [stb])
        p3 = P.f32(4)
        k.cp("dve", S_["qmb16"], p3[:, 0:256], [P.bufs[4]], [S_["qmb16b"]])
        k.act(S_["sgm"], p3[:, 256:512], AF.Silu, [P.bufs[4]], [S_["sgmb"]])

    def s2b(i):
        d = st8.pop(i)
        S_ = sets[i % 2]
        st, stb, rc, rcb = d["st"], d["stb"], d["rc"], d["rcb"]
        qk, qkb, t1, t2, tb, qkr, qkrb = S_["qk"], S_["qkb"], S_["t1"], S_["t2"], S_["tb"], S_["qkr"], S_["qkrb"]
        qmb16, qmb16b, sgm, sgmb, matmp, mgt, mgtb = (S_["qmb16"], S_["qmb16b"], S_["sgm"], S_["sgmb"], S_["matmp"],
                                                      S_["mgt"], S_["mgtb"])
        x1 = qk[:, :, 0:64]
        x2 = qk[:, :, 64:128]
        cosb = rc[:, 0:64].unsqueeze(1).to_broadcast([128, 4, 64])
        sinb = rc[:, 64:128].unsqueeze(1).to_broadcast([128, 4, 64])
        k.tt("dve", t1, x1, cosb, ALU.mult, [qkb, rcb], [tb])
        k.tt("dve", t2, x2, sinb, ALU.mult, [qkb, rcb], [tb])
        k.tt("dve", qkr[:, :, 0:64], t1, t2, ALU.subtract, [tb], [qkrb])
        k.tt("dve", t1, x1, sinb, ALU.mult, [qkb, rcb], [tb])
        k.tt("dve", t2, x2, cosb, ALU.mult, [qkb, rcb], [tb])
        k.tt("dve", qkr[:, :, 64:128], t1, t2, ALU.add, [tb], [qkrb])
        for h in range(2):
            k.ts("pool", st[:, 512 + h * 128:512 + (h + 1) * 128], qkr[:, 2 + h, :], kd[:, h:h + 1], None,
                 ALU.mult, None, [qkrb, cb], [stb])
            k.ts("pool", st[:, 768 + h * 128:768 + (h + 1) * 128], qkr[:, 2 + h, :], kd[:, 2 + h:3 + h], None,
                 ALU.mult, None, [qkrb, cb], [stb])
        pT = P.bf16(6)
        for u in range(4):
            k.tr(pT[:, u * 128:(u + 1) * 128], qkr[:, u, :], ident, [qkrb, identb], [P.bufs[6]], inc=(u == 3))
        k.cp("act", st[:, 0:512], pT[:, 0:512], [P.bufs[6]], [stb])
        k.dma("sp", scrd[i], st, [stb], [scr_b[i]])
        mem_attn(k, P, qmb16, qmb16b, sgm, sgmb, mkT[:, 0], mkTb, mva[:, 0], mvab, ident, identb, matmp,
                 (6, 7, 5), mgt, mgtb)
        mgT, mgTb = mgTr.next()
        pT = P.bf16(6)
        for ec in range(2):
            k.tr(pT[:, 512 + ec * 128:512 + (ec + 1) * 128], mgt[:, ec * 128:(ec + 1) * 128], ident,
                 [mgtb, identb], [P.bufs[6]], inc=(ec == 1))
        k.cp("act", mgT, pT[:, 512:768].rearrange("p (c t) -> p c t", c=2), [P.bufs[6]], [mgTb])
        k.dma("sp", yg[i][:, 512:768], mgT.rearrange("p c t -> p (c t)"), [mgTb], [yg_mb[i]])

    loadA(0)
    for t in range(n_tiles + 2):
        loadA(t + 1)
        if 0 <= t - 2 < n_tiles:
            s2a(t - 2)
        if 0 <= t - 1 < n_tiles:
            s1(t - 1)
        if t < n_tiles:
            s0(t)
        if 0 <= t - 2 < n_tiles:
            s2b(t - 2)

    Sb = k.sb("Sb", [128, 2, 256], F32)
    Sbb = Buf("Sb")
    Sf = k.sb("Sf", [128, 2, 256], F32)
    Sfb = Buf("Sf")
    Sf16 = k.sb("Sf16", [128, 2, 256], BF16)
    Sf16b = Buf("Sf16")
    k.memset("dve", Sb, 0.0, [Sbb])
    k.memset("dve", Sf, 0.0, [Sfb])
    k.memset("dve", Sf16, 0.0, [Sf16b])
    kvr = Ring(k, "kvr", [128, 1024], BF16, 3)
    sbst = Ring(k, "sbst", [128, 512], BF16, 3)
    ldB = {}

    def loadBk(n):
        if n >= 0:
            kv, kvb = kvr.next()
            k.dma("sp", kv, scrd[n][:, 512:1536], [scr_b[n]], [kvb])
            ldB[n] = (kv, kvb)

    loadBk(n_tiles - 1)
    loadBk(n_tiles - 2)
    for n in range(n_tiles - 1, -1, -1):
        loadBk(n - 2)
        kv, kvb = ldB.pop(n)
        so, sob = sbst.next()
        k.cp("act", so, Sb.rearrange("p h e -> p (h e)"), [Sbb], [sob])
        k.dma("sp", sbs[n], so, [sob], [sbs_b[n]])
        if n == 0:
            break
        for h in range(2):
            bk = 1 + h
            k.mm(P.f32(bk)[:, 0:256], kv[:, 256 + h * 128:256 + (h + 1) * 128], kv[:, 512 + h * 256:512 + (h + 1) * 256],
                 True, True, [kvb], [P.bufs[bk]])
            k.stt(Sb[:, h, :], Sb[:, h, :], cd[:, 2 + h:3 + h], P.f32(bk)[:, 0:256], ALU.mult, ALU.add,
                  [Sbb, cb, P.bufs[bk]], [Sbb])

    chr_ = Ring(k, "chk", [128, 2048], BF16, 3)
    sbr = Ring(k, "sbin", [128, 512], BF16, 3)
    AT = Ring(k, "AT", [128, 128], BF16, 2)
    qsf = Ring(k, "qsf", [128, 128], BF16, 2)
    qsb = Ring(k, "qsb", [128, 128], BF16, 2)
    ygr = Ring(k, "ygt", [128, 256], BF16, 2)
    ygT = Ring(k, "ygT", [128, 4, 128], BF16, 2)
    ss2 = k.sb("ss2", [128, 1], F32)
    ss2b = Buf("ss2")
    junk2 = k.sb("junk2", [128, 256], F32)
    junk2b = Buf("junk2")
    ldF = {}

    def loadF(n):
        if n < n_tiles:
            ch, chb = chr_.next()
            sbn, sbnb = sbr.next()
            k.dma("sp", ch, scrd[n], [scr_b[n]], [chb])
            k.dma("sp", sbn, sbs[n], [sbs_b[n]], [sbnb])
            ldF[n] = (ch, chb, sbn, sbnb)

    loadF(0)
    loadF(1)
    for n in range(n_tiles):
        loadF(n + 2)
        ch, chb, sbn, sbnb = ldF.pop(n)
        yT, yTb = ygT.next()
        for h in range(2):
            qT = ch[:, h * 128:(h + 1) * 128]
            kT = ch[:, 256 + h * 128:256 + (h + 1) * 128]
            kf = ch[:, 512 + h * 128:512 + (h + 1) * 128]
            v = ch[:, 1024 + h * 256:1024 + (h + 1) * 256]
            sg = ch[:, 1536 + h * 256:1536 + (h + 1) * 256]
            bS, bY, bKV = 1 + h, 3 + h, 5 + h
            a, ab = AT.next()
            f, fb = qsf.next()
            bq, bqb = qsb.next()
            k.mm(P.f32(bS)[:, 0:128], kT, qT, True, True, [chb], [P.bufs[bS]])
            k.tt("dve", a, P.f32(bS)[:, 0:128], DT[:, h, :], ALU.mult, [P.bufs[bS], cb], [ab])
            k.tt("pool", f, qT, QDF[:, h, :], ALU.mult, [chb, cb], [fb])
            k.tt("pool", bq, qT, QDB[:, h, :], ALU.mult, [chb, cb], [bqb])
            pY = P.f32(bY)[:, 0:256]
            k.mm(pY, a, v, True, False, [ab, chb], [P.bufs[bY]])
            k.mm(pY, f, Sf16[:, h, :], False, False, [fb, Sf16b], [P.bufs[bY]])
            k.mm(pY, bq, sbn[:, h * 256:(h + 1) * 256], False, True, [bqb, sbnb], [P.bufs[bY]])
            if n < n_tiles - 1:
                pKV = P.f32(bKV)[:, 0:256]
                k.mm(pKV, kf, v, True, True, [chb], [P.bufs[bKV]])
                k.stt(Sf[:, h, :], Sf[:, h, :], cd[:, h:h + 1], pKV, ALU.mult, ALU.add,
                      [Sfb, cb, P.bufs[bKV]], [Sfb])
                k.cp("act", Sf16[:, h, :], Sf[:, h, :], [Sfb], [Sf16b])
            k.act(junk2, pY, AF.Square, [P.bufs[bY]], [junk2b, ss2b], accum_out=ss2)
            k.act(ss2, ss2, AF.Sqrt, [ss2b], [ss2b], scale=1.0 / 256, bias=EPS)
            k.recip(ss2, ss2, [ss2b], [ss2b])
            yg_t, ygb = ygr.next()
            k.stt(yg_t, pY, ss2, sg, ALU.mult, ALU.mult, [P.bufs[bY], ss2b, chb], [ygb])
            pT = P.bf16(7)
            for ec in range(2):
                k.tr(pT[:, (h * 2 + ec) * 128:(h * 2 + ec + 1) * 128], yg_t[:, ec * 128:(ec + 1) * 128], ident,
                     [ygb, identb], [P.bufs[7]], inc=(ec == 1))
            k.cp("act", yT[:, h * 2:(h + 1) * 2, :],
                 pT[:, h * 256:(h + 1) * 256].rearrange("p (c t) -> p c t", c=2), [P.bufs[7]], [yTb])
        k.dma("sp", yg[n][:, 0:512], yT.rearrange("p c t -> p (c t)"), [yTb], [yg_yb[n]])
        gather_yg(n)


import math

LAMBDA_INIT = 0.8 - 0.6 * math.exp(-0.3 * 1)
def sumsq_gather(k, P, nc, ssq, ssqb, dr_in, dr_in_b, dr_all, dr_all_b, groups, n_feat, tag):
    nt = ssq.shape[1]
    k.dma("sp", dr_in, ssq, [ssqb], [dr_in_b])
    k.collective("AllGather", dr_in, dr_all, groups, [dr_in_b], [dr_all_b])
    g4 = k.sb("ssq4" + tag, [128, 4, nt], F32)
    g4b = Buf("ssq4" + tag)
    k.dma("sp", g4, dr_all.rearrange("(r p) n -> p r n", p=128), [dr_all_b], [g4b])
    rstd = k.sb("rstd" + tag, [128, nt], F32)
    rstdb = Buf("rstd" + tag)
    k.tt("dve", rstd, g4[:, 0, :], g4[:, 1, :], ALU.add, [g4b], [rstdb])
    k.tt("dve", rstd, rstd, g4[:, 2, :], ALU.add, [g4b, rstdb], [rstdb])
    k.tt("dve", rstd, rstd, g4[:, 3, :], ALU.add, [g4b, rstdb], [rstdb])
    k.act(rstd, rstd, AF.Sqrt, [rstdb], [rstdb], scale=1.0 / n_feat, bias=EPS)
    k.recip(rstd, rstd, [rstdb], [rstdb])
    return rstd, rstdb


def colproj(k, P, src, srcb, wot, wotb, bank):
    pp = P.f32(bank)[:, 0:256]
    for g in range(4):
        for c in range(6):
            ch = g * 6 + c
            k.mm(pp, src[:, g, c * 128:(c + 1) * 128], wot[:, ch, :], ch == 0, ch == 23, [srcb, wotb],
                 [P.bufs[bank]])
    return pp


def phase_B(k, P, nc, ident, identb, T, n_tiles):
    ygall, ygall_cb = T["ygall"], T["ygall_cb"]
    xc, wo, ngc = T["xcB"], T["woB"], T["ngcB"]
    h1s, h1s_b = T["h1s"], T["h1s_b"]
    h1gt, h1gt_tb = T["h1gt"], T["h1gt_tb"]
    gather_h1 = T["gather_h1"]
    wot = k.sb("wotB", [128, 24, 256], BF16)
    wotb = Buf("wotB")
    for c in range(2):
        k.dma("pool", wot[:, c * 12:(c + 1) * 12, :],
              wo[c * 1536:(c + 1) * 1536, :].rearrange("(c p) n -> p c n", p=128), (), [wotb])
    gc = k.sb("gcB", [128, 256], F32)
    gcb = Buf("gcB")
    k.dma("sp", gc, ngc.partition_broadcast(128), (), [gcb])
    ssq = k.sb("ssqB", [128, n_tiles], F32)
    ssqb = Buf("ssqB")
    k.memset("dve", ssq, 0.0, [ssqb])
    ygr = Ring(k, "ygtB", [128, 4, 768], BF16, 3)
    xr = Ring(k, "xcB", [128, 256], F32, 3)
    hr = Ring(k, "h1sB", [128, 256], F32, 2)
    hgr = Ring(k, "h1gB", [128, 256], BF16, 2)
    hTr = Ring(k, "h1gTB", [128, 2, 128], BF16, 2)
    junk = k.sb("junkB", [128, 256], BF16)
    junkb = Buf("junkB")
    yv = ygall.rearrange("(c g ii) p n -> c ii p g n", g=4, ii=4)
    ld = {}

    def load(i):
        if i < n_tiles:
            yt, yb = ygr.next()
            xt, xb = xr.next()
            k.dma("sp", yt, yv[i // 4][i % 4], [ygall_cb[i // 4]], [yb])
            k.dma("sp", xt, xc[i * 128:(i + 1) * 128, :], (), [xb])
            ld[i] = (yt, yb, xt, xb)

    pps = {}

    def proj(i):
        if i < n_tiles:
            yt, yb, xt, xb = ld[i]
            bank = 1 + (i % 2)
            pps[i] = (colproj(k, P, yt, yb, wot, wotb, bank), bank)

    load(0)
    load(1)
    proj(0)
    for i in range(n_tiles):
        load(i + 2)
        proj(i + 1)
        yt, yb, xt, xb = ld.pop(i)
        pp, bank = pps.pop(i)
        h, hb = hr.next()
        k.tt("dve", h, pp, xt, ALU.add, [P.bufs[bank], xb], [hb])
        k.dma("sp", h1s[i], h, [hb], [h1s_b[i]])
        k.act(junk, h, AF.Square, [hb], [junkb, ssqb], accum_out=ssq[:, i:i + 1])
        hg, hgb = hgr.next()
        k.tt("dve", hg, h, gc, ALU.mult, [hb, gcb], [hgb])
        hT, hTb = hTr.next()
        pT = P.bf16(3 + (i % 2))
        for c in range(2):
            k.tr(pT[:, c * 128:(c + 1) * 128], hg[:, c * 128:(c + 1) * 128], ident, [hgb, identb],
                 [P.bufs[3 + (i % 2)]], inc=(c == 1))
        k.cp("act", hT, pT[:, 0:256].rearrange("p (c t) -> p c t", c=2), [P.bufs[3 + (i % 2)]], [hTb])
        k.dma("sp", h1gt[i], hT.rearrange("p c t -> p (c t)"), [hTb], [h1gt_tb[i]])
        gather_h1(i)
    return ssq, ssqb


def phase_B2(k, P, nc, ident, identb, T, n_tiles, rstd, rstdb):
    hall, hall_cb = T["h1gtall"], T["h1gtall_cb"]
    w, mem, mng, wkv, dcs = T["wB"], T["memB"], T["mngB"], T["wkvB"], T["dcsB"]
    scr1, scr1_b = T["scr1"], T["scr1_b"]
    ogt, ogt_mb = T["ogt"], T["ogt_mb"]
    wt = k.sb("wtB2", [128, 8, 2560], BF16)
    wtb = Buf("wtB2")
    for c in range(5):
        k.dma("pool", wt[:, :, c * 512:(c + 1) * 512],
              w[:, c * 512:(c + 1) * 512].rearrange("(c p) n -> p c n", p=128), (), [wtb])
    gt = k.sb("gtB2", [128, D], F32)
    gb = Buf("gtB2")
    scr = (k.sb("junkB2", [128, D], BF16), Buf(), k.sb("ssB2", [128, 1], F32), Buf(),
           k.sb("rstdB2", [128, 1], F32), Buf(), k.sb("hnB2", [128, D], BF16), Buf())
    mkT, mkTb, mva, mvab = mem_prep(k, P, nc, mem, mng, wkv, 1, ident, identb, scr, gt, gb)
    hnr = Ring(k, "hnTB2", [128, 4, 256], BF16, 4)
    rpr = Ring(k, "ropeB2", [128, 32], F32, 4)
    stg = Ring(k, "stgB2", [128, 2048], BF16, 2)
    def mkset(s):
        return dict(
            rot=k.sb("rotB2" + s, [128, 8, 32], F32), rotb=Buf("rot"),
            t1=k.sb("t1B2" + s, [128, 8, 16], F32), t2=k.sb("t2B2" + s, [128, 8, 16], F32), tb=Buf("t12"),
            qkr=k.sb("qkrB2" + s, [128, 8, 128], BF16), qkrb=Buf("qkr"),
            qmb16=k.sb("qmb16B2" + s, [128, 256], BF16), qmb16b=Buf("qmb16"),
            sgm=k.sb("sgmB2" + s, [128, 256], F32), sgmb=Buf("sgm"),
            matmp=(k.sb("qmTB2" + s, [128, 2, 128], BF16), Buf(), k.sb("EmemB2" + s, [128, 2, 128], BF16), Buf(),
                   k.sb("rsmB2" + s, [128, 1], F32), Buf()),
            mgt=k.sb("mgtB2" + s, [128, 256], BF16), mgtb=Buf("mgt"))

    sets = [mkset("0"), mkset("1")]
    mgTr = Ring(k, "mgTB2", [128, 2, 128], BF16, 2)
    hv = hall.rearrange("(c g ii) p n -> c ii p g n", g=4, ii=4)
    ld = {}

    def load(i):
        if i < n_tiles:
            hn, hnb = hnr.next()
            rp, rpb = rpr.next()
            k.dma("sp", hn, hv[i // 4][i % 4], [hall_cb[i // 4]], [hnb])
            k.dma("sp", rp, dcs[i * 128:(i + 1) * 128, :], (), [rpb])
            ld[i] = (hn, hnb, rp, rpb)

    def s1(i):
        hn, hnb, rp, rpb = ld[i]
        for gi in range(5):
            pp = P.f32(gi)
            for c in range(8):
                k.mm(pp, hn[:, c // 2, (c % 2) * 128:(c % 2 + 1) * 128], wt[:, c, gi * 512:(gi + 1) * 512],
                     c == 0, c == 7, [hnb, wtb], [P.bufs[gi]])

    stt_ = {}

    def s2a(i):
        r = rstd[:, i:i + 1]
        st, stb = stg.next()
        stt_[i] = (st, stb)
        S_ = sets[i % 2]
        rot, rotb, qkr, qkrb = S_["rot"], S_["rotb"], S_["qkr"], S_["qkrb"]
        for gi in range(2):
            p3 = P.f32(gi).rearrange("p (u d) -> p u d", u=4)
            k.act(rot[:, gi * 4:(gi + 1) * 4, :], p3[:, :, 0:32], AF.Copy, [P.bufs[gi], rstdb], [rotb], scale=r)
            k.act(qkr[:, gi * 4:(gi + 1) * 4, :], p3, AF.Copy, [P.bufs[gi], rstdb], [qkrb], scale=r)
        k.act(st[:, 1024:1536], P.f32(2), AF.Copy, [P.bufs[2], rstdb], [stb], scale=r)
        k.act(st[:, 1536:2048], P.f32(3), AF.Silu, [P.bufs[3], rstdb], [stb], scale=r)
        p4 = P.f32(4)
        k.ts("dve", S_["qmb16"], p4[:, 0:256], r, None, ALU.mult, None, [P.bufs[4], rstdb], [S_["qmb16b"]])
        k.act(S_["sgm"], p4[:, 256:512], AF.Silu, [P.bufs[4], rstdb], [S_["sgmb"]], scale=r)

    def s2b(i):
        hn, hnb, rp, rpb = ld.pop(i)
        st, stb = stt_.pop(i)
        S_ = sets[i % 2]
        rot, rotb, t1, t2, tb, qkr, qkrb = S_["rot"], S_["rotb"], S_["t1"], S_["t2"], S_["tb"], S_["qkr"], S_["qkrb"]
        qmb16, qmb16b, sgm, sgmb, matmp, mgt, mgtb = (S_["qmb16"], S_["qmb16b"], S_["sgm"], S_["sgmb"], S_["matmp"],
                                                      S_["mgt"], S_["mgtb"])
        x1 = rot[:, :, 0:16]
        x2 = rot[:, :, 16:32]
        cosb = rp[:, 0:16].unsqueeze(1).to_broadcast([128, 8, 16])
        sinb = rp[:, 16:32].unsqueeze(1).to_broadcast([128, 8, 16])
        k.tt("dve", t1, x1, cosb, ALU.mult, [rotb, rpb], [tb])
        k.tt("dve", t2, x2, sinb, ALU.mult, [rotb, rpb], [tb])
        k.tt("dve", qkr[:, :, 0:16], t1, t2, ALU.subtract, [tb], [qkrb])
        k.tt("dve", t1, x1, sinb, ALU.mult, [rotb, rpb], [tb])
        k.tt("dve", t2, x2, cosb, ALU.mult, [rotb, rpb], [tb])
        k.tt("dve", qkr[:, :, 16:32], t1, t2, ALU.add, [tb], [qkrb])
        pT = P.bf16(5)
        for u in range(8):
            k.tr(pT[:, u * 128:(u + 1) * 128], qkr[:, u, :], ident, [qkrb, identb], [P.bufs[5]], inc=(u == 7))
        k.cp("dve", st[:, 0:1024], pT[:, 0:1024], [P.bufs[5]], [stb])
        k.dma("sp", scr1[i], st, [stb], [scr1_b[i]])
        mem_attn(k, P, qmb16, qmb16b, sgm, sgmb, mkT[:, 0], mkTb, mva[:, 0], mvab, ident, identb, matmp,
                 (6, 7, 6), mgt, mgtb)
        mgT, mgTb = mgTr.next()
        pT = P.bf16(7)
        for ec in range(2):
            k.tr(pT[:, 512 + ec * 128:512 + (ec + 1) * 128], mgt[:, ec * 128:(ec + 1) * 128], ident,
                 [mgtb, identb], [P.bufs[7]], inc=(ec == 1))
        k.cp("act", mgT, pT[:, 512:768].rearrange("p (c t) -> p c t", c=2), [P.bufs[7]], [mgTb])
        k.dma("sp", ogt[i][:, 512:768], mgT.rearrange("p c t -> p (c t)"), [mgTb], [ogt_mb[i]])

    load(0)
    load(1)
    for t in range(n_tiles + 1):
        load(t + 2)
        if 0 <= t - 1 < n_tiles:
            s2a(t - 1)
        if t < n_tiles:
            s1(t)
        if 0 <= t - 1 < n_tiles:
            s2b(t - 1)


def phase_C(k, P, nc, ident, identb, T, n_tiles):
    scr1, scr1_b = T["scr1"], T["scr1_b"]
    ogt, ogt_yb = T["ogt"], T["ogt_yb"]
    gather_og = T["gather_og"]
    lam4, slg = T["lamC"], T["slgC"]
    n_all = n_tiles
    nq = n_tiles // 4
    lv = k.sb("lv", [128, 4, 128], F32)
    lvb = Buf("lv")
    k.dma("sp", lv, lam4.rearrange("a d -> (a d)").partition_broadcast(128).rearrange("p (a d) -> p a d", a=4), (), [lvb])
    lp = k.sb("lp", [128, 2, 128], F32)
    k.tt("dve", lp[:, 0, :], lv[:, 0, :], lv[:, 1, :], ALU.mult, [lvb], [lvb])
    k.tt("dve", lp[:, 1, :], lv[:, 2, :], lv[:, 3, :], ALU.mult, [lvb], [lvb])
    ls = k.sb("ls", [128, 2], F32)
    k.op("dve", lambda e: e.tensor_reduce(out=ls, in_=lp, axis=AX.X, op=ALU.add), [lvb], [lvb])
    k.act(ls, ls, AF.Exp, [lvb], [lvb])
    nlam = k.sb("nlam", [128, 1], F32)
    nlamb = Buf("nlam")
    k.tt("dve", nlam, ls[:, 1:2], ls[:, 0:1], ALU.subtract, [lvb], [nlamb])
    k.ts("dve", nlam, nlam, -LAMBDA_INIT, None, ALU.add, None, [nlamb], [nlamb])
    sgain = k.sb("sgain", [128, 256], F32)
    sgainb = Buf("sgain")
    k.dma("sp", sgain, slg.partition_broadcast(128), (), [sgainb])
    k.ts("dve", sgain, sgain, 1.0 - LAMBDA_INIT, None, ALU.mult, None, [sgainb], [sgainb])

    kc = [k.sb(f"kc{c}", [128, n_all, 128], BF16) for c in range(2)]
    kcb = [Buf("kc0"), Buf("kc1")]
    va = k.sb("va", [128, n_all, 258], BF16)
    vab = Buf("va")
    k.memset("dve", va, 1.0, [vab])
    qtr = Ring(k, "qtile", [128, 2, 4, 128], BF16, 2)
    sgr = Ring(k, "sgt", [128, 4, 256], BF16, 2)
    er = Ring(k, "E", [128, 512], BF16, 3)
    on = k.sb("on", [128, 2, 4, 256], F32)
    onb = Buf("on")
    rs = k.sb("rs", [128, 4], F32)
    rsb = Buf("rs")
    ssq = k.sb("ssq", [128, 4], F32)
    ssqb = Buf("ssq")
    junk = k.sb("junkc", [128, 256], BF16)
    junkb = Buf("junkc")
    ogtile = k.sb("ogtile", [128, 4, 256], BF16)
    ogtileb = Buf("ogtile")
    ogTr = Ring(k, "ogT", [128, 4, 2, 128], BF16, 2)
    scale = 128.0 ** -0.5
    sbank = 0
    allscr = list(scr1_b)
    for hh in range(2):
        for c in range(2):
            u = hh * 2 + c
            for i0 in range(0, n_all, 16):
                i1 = min(n_all, i0 + 16)
                k.dma("sp", kc[c][:, i0:i1, :],
                      scr1[i0:i1, :, 512 + u * 128:512 + (u + 1) * 128].rearrange("i p t -> p i t"),
                      allscr[i0:i1], [kcb[c]])
        for i0 in range(0, n_all, 16):
            i1 = min(n_all, i0 + 16)
            k.dma("sp", va[:, i0:i1, 0:256],
                  scr1[i0:i1, :, 1024 + hh * 256:1024 + (hh + 1) * 256].rearrange("i p e -> p i e"),
                  allscr[i0:i1], [vab])
        ld = {}

        def load(q):
            if q < nq:
                qt, qtb = qtr.next()
                sgt, sgtb = sgr.next()
                for c in range(2):
                    u = hh * 2 + c
                    k.dma("sp", qt[:, c], scr1[q * 4:(q + 1) * 4, :, u * 128:(u + 1) * 128].rearrange("i p t -> p i t"),
                          allscr[q * 4:(q + 1) * 4], [qtb])
                k.dma("sp", sgt, scr1[q * 4:(q + 1) * 4, :, 1536 + hh * 256:1536 + (hh + 1) * 256].rearrange("i p e -> p i e"),
                      allscr[q * 4:(q + 1) * 4], [sgtb])
                ld[q] = (qt, qtb, sgt, sgtb)

        load(0)
        for q in range(nq):
            load(q + 1)
            qt, qtb, sgt, sgtb = ld.pop(q)
            items = [(c, kt_i) for c in range(2) for kt_i in range(n_all)]
            pend = {}

            def emit_S(c, kt_i):
                nonlocal sbank
                sbank = (sbank + 1) % 3
                pS = P.f32(sbank)
                k.mm(pS, kc[c][:, kt_i, :], qt[:, c].rearrange("p a t -> p (a t)"), True, True, [kcb[c], qtb],
                     [P.bufs[sbank]])
                pend[(c, kt_i)] = sbank

            emit_S(*items[0])
            for idx, (c, kt_i) in enumerate(items):
                if idx + 1 < len(items):
                    emit_S(*items[idx + 1])
                sb_ = pend.pop((c, kt_i))
                E, Eb = er.next()
                k.act(E, P.f32(sb_), AF.Exp, [P.bufs[sb_]], [Eb], scale=scale)
                for qs in range(4):
                    k.mm(P.f32(3 + qs)[:, 0:257], E[:, qs * 128:(qs + 1) * 128], va[:, kt_i, 0:257],
                         kt_i == 0, kt_i == n_all - 1, [Eb, vab], [P.bufs[3 + qs]], inc=(qs == 3))
                if kt_i == n_all - 1:
                    for qs in range(4):
                        pO = P.f32(3 + qs)
                        k.recip(rs[:, qs:qs + 1], pO[:, 256:257], [P.bufs[3 + qs]], [rsb])
                        k.ts("dve", on[:, c, qs, :], pO[:, 0:256], rs[:, qs:qs + 1], None, ALU.mult, None,
                             [P.bufs[3 + qs], rsb], [onb])
            on0 = on[:, 0].rearrange("p a e -> p (a e)")
            on1 = on[:, 1].rearrange("p a e -> p (a e)")
            k.stt(on0, on1, nlam, on0, ALU.mult, ALU.add, [onb, nlamb], [onb])
            for qs in range(4):
                k.act(junk, on[:, 0, qs, :], AF.Square, [onb], [junkb, ssqb], accum_out=ssq[:, qs:qs + 1])
            k.act(ssq, ssq, AF.Sqrt, [ssqb], [ssqb], scale=1.0 / 256, bias=EPS)
            k.recip(ssq, ssq, [ssqb], [ssqb])
            for qs in range(4):
                k.stt(on[:, 0, qs, :], on[:, 0, qs, :], ssq[:, qs:qs + 1], sgain, ALU.mult, ALU.mult,
                      [onb, ssqb, sgainb], [onb])
            k.tt("dve", ogtile.rearrange("p a e -> p (a e)"), on0, sgt.rearrange("p a e -> p (a e)"), ALU.mult,
                 [onb, sgtb], [ogtileb])
            ogT, ogTb = ogTr.next()
            pT = P.bf16(7)
            for qs in range(4):
                for ec in range(2):
                    k.tr(pT[:, (qs * 2 + ec) * 128:(qs * 2 + ec + 1) * 128], ogtile[:, qs, ec * 128:(ec + 1) * 128],
                         ident, [ogtileb, identb], [P.bufs[7]], inc=(qs == 3 and ec == 1))
            k.cp("act", ogT.rearrange("p a c t -> p (a c t)"), pT[:, 0:1024], [P.bufs[7]], [ogTb])
            k.dma("sp", ogt[q * 4:(q + 1) * 4, :, hh * 256:(hh + 1) * 256].rearrange("i p n -> p i n"),
                  ogT.rearrange("p a c t -> p a (c t)"), [ogTb], [ogt_yb[q][hh]])
            if hh == 1:
                gather_og(q)


def phase_D(k, P, nc, ident, identb, T, n_tiles, GROUPS):
    ogall, ogall_cb = T["ogtall"], T["ogtall_cb"]
    wo, fgc = T["woD"], T["fgcD"]
    h1s, h1s_b = T["h1s"], T["h1s_b"]
    out = T["out"]
    wot = k.sb("wotD", [128, 24, 256], BF16)
    wotb = Buf("wotD")
    for c in range(2):
        k.dma("pool", wot[:, c * 12:(c + 1) * 12, :],
              wo[c * 1536:(c + 1) * 1536, :].rearrange("(c p) n -> p c n", p=128), (), [wotb])
    gc = k.sb("gcD", [128, 256], F32)
    gcb = Buf("gcD")
    k.dma("sp", gc, fgc.partition_broadcast(128), (), [gcb])
    ssq = k.sb("ssqD", [128, n_tiles], F32)
    ssqb = Buf("ssqD")
    k.memset("dve", ssq, 0.0, [ssqb])
    h2 = k.sb("h2D", [128, n_tiles, 256], F32)
    h2b = [Buf(f"h2_{i}") for i in range(n_tiles)]
    ogr = Ring(k, "ogD", [128, 4, 768], BF16, 3)
    hr = Ring(k, "h1D", [128, 256], F32, 3)
    junk = k.sb("junkD", [128, 256], BF16)
    junkb = Buf("junkD")
    ov = ogall.rearrange("(c g ii) p n -> c ii p g n", g=4, ii=4)
    ld = {}

    def load(i):
        if i < n_tiles:
            og, ogb = ogr.next()
            h, hb = hr.next()
            k.dma("sp", og, ov[i // 4][i % 4], [ogall_cb[i // 4]], [ogb])
            k.dma("sp", h, h1s[i], [h1s_b[i]], [hb])
            ld[i] = (og, ogb, h, hb)

    pps = {}

    def proj(i):
        if i < n_tiles:
            og, ogb, h, hb = ld[i]
            bank = 1 + (i % 2)
            pps[i] = (colproj(k, P, og, ogb, wot, wotb, bank), bank)

    load(0)
    load(1)
    proj(0)
    for i in range(n_tiles):
        load(i + 2)
        proj(i + 1)
        og, ogb, h, hb = ld.pop(i)
        pp, bank = pps.pop(i)
        k.tt("dve", h2[:, i, :], pp, h, ALU.add, [P.bufs[bank], hb], [h2b[i]])
        k.act(junk, h2[:, i, :], AF.Square, [h2b[i]], [junkb, ssqb], accum_out=ssq[:, i:i + 1])
    rstd, rstdb = sumsq_gather(k, P, nc, ssq, ssqb, T["ssq2"], T["ssq2_b"], T["ssq2all"], T["ssq2all_b"], GROUPS,
                               D, "D")
    outr = Ring(k, "outD", [128, 256], F32, 3)
    for i in range(n_tiles):
        o, ob = outr.next()
        k.stt(o, h2[:, i, :], rstd[:, i:i + 1], gc, ALU.mult, ALU.mult, [h2b[i], rstdb, gcb], [ob])
        k.dma("sp", out[i * 128:(i + 1) * 128, :], o, [ob], (), is_output=True)


def build_fused(n_tiles=128, groups=None):
    GROUPS = groups or [[0, 1, 2, 3], [4, 5, 6, 7]]
    nc = bass.Bass("TRN2", target_bir_lowering=False)
    S_ = n_tiles * 128
    T = {}

    def ext(name, shape, dt=F32):
        T[name] = nc.dram_tensor(name, list(shape), dt, kind="ExternalInput").ap()

    def scratch(name, shape, dt):
        T[name] = nc.dram_tensor(name, list(shape), dt).ap()
        T[name + "_b"] = Buf(name)

    ext("xA", [S_, D]); ext("ngA", [D]); ext("wA", [D, 2048]); ext("memA", [256, D]); ext("mngA", [D])
    ext("wkvA", [D, 512]); ext("decA", [4]); ext("rcsA", [S_, 128])
    ext("xcB", [S_, 256]); ext("woB", [3072, 256]); ext("ngcB", [256])
    ext("wB", [D, 2560]); ext("memB", [256, D]); ext("mngB", [D]); ext("wkvB", [D, 512]); ext("dcsB", [S_, 32])
    ext("lamC", [4, 128]); ext("slgC", [256]); ext("woD", [3072, 256]); ext("fgcD", [256])
    T["out"] = nc.dram_tensor("out", [S_, 256], F32, kind="ExternalOutput").ap()
    scratch("yg", [n_tiles, 128, 768], BF16)
    scratch("ygall", [4 * n_tiles, 128, 768], BF16)
    T["scrA"] = nc.dram_tensor("scrA", [n_tiles, 128, 2048], BF16).ap()
    T["sbsA"] = nc.dram_tensor("sbsA", [n_tiles, 128, 512], BF16).ap()
    T["h1s"] = nc.dram_tensor("h1s", [n_tiles, 128, 256], F32).ap()
    T["h1s_b"] = [Buf(f"h1s{i}") for i in range(n_tiles)]
    scratch("h1gt", [n_tiles, 128, 256], BF16)
    scratch("h1gtall", [4 * n_tiles, 128, 256], BF16)
    scratch("ssq1", [128, n_tiles], F32)
    scratch("ssq1all", [4 * 128, n_tiles], F32)
    T["scr1"] = nc.dram_tensor("scr1", [n_tiles, 128, 2048], BF16).ap()
    T["scr1_b"] = [Buf(f"scr1_{i}") for i in range(n_tiles)]
    scratch("ogt", [n_tiles, 128, 768], BF16)
    scratch("ogtall", [4 * n_tiles, 128, 768], BF16)
    scratch("ssq2", [128, n_tiles], F32)
    scratch("ssq2all", [4 * 128, n_tiles], F32)

    k = K(nc)
    P = Psum(k)
    ident, identb, io, iob = make_ident(k)

    def flat(ap):
        return ap.rearrange("i p n -> (i p) n")

    nch = n_tiles // 4

    def mk_gather(src, dst, dst_cb, rd, lag):
        done = set()

        def emit(c):
            if c in done or c >= nch or c < 0:
                return
            done.add(c)
            k.collective("AllGather", flat(T[src][c * 4:(c + 1) * 4]), flat(T[dst][c * 16:(c + 1) * 16]), GROUPS,
                         rd(c), [dst_cb[c]])

        def f(i, final=False):
            if final:
                for c in range(nch):
                    emit(c)
            else:
                c = (i - 3 - lag) // 4
                if (i - 3 - lag) % 4 == 0:
                    emit(c)
        return f

    T["yg_mb"] = [Buf() for _ in range(n_tiles)]
    T["yg_yb"] = [Buf() for _ in range(n_tiles)]
    T["ygall_cb"] = [Buf() for _ in range(nch)]
    T["gather_yg"] = mk_gather("yg", "ygall", T["ygall_cb"],
                               lambda c: T["yg_mb"][c * 4:(c + 1) * 4] + T["yg_yb"][c * 4:(c + 1) * 4], 2)
    T["h1gt_tb"] = [Buf() for _ in range(n_tiles)]
    T["h1gtall_cb"] = [Buf() for _ in range(nch)]
    T["gather_h1"] = mk_gather("h1gt", "h1gtall", T["h1gtall_cb"], lambda c: T["h1gt_tb"][c * 4:(c + 1) * 4], 2)
    T["ogt_mb"] = [Buf() for _ in range(n_tiles)]
    T["ogt_yb"] = [[Buf(), Buf()] for _ in range(nch)]
    T["ogtall_cb"] = [Buf() for _ in range(nch)]
    g_og = mk_gather("ogt", "ogtall", T["ogtall_cb"],
                     lambda c: T["ogt_mb"][c * 4:(c + 1) * 4] + T["ogt_yb"][c], 0)
    T["gather_og"] = lambda q: g_og(q * 4 + 3 - 4) if q > 0 else None

    k.phase_begin()
    phase_A(k, P, nc, ident, identb, io, iob, T, n_tiles)
    T["gather_yg"](0, final=True)
    k.phase_end()

    k.phase_begin()
    ssq, ssqb = phase_B(k, P, nc, ident, identb, T, n_tiles)
    T["gather_h1"](0, final=True)
    rstd, rstdb = sumsq_gather(k, P, nc, ssq, ssqb, T["ssq1"], T["ssq1_b"], T["ssq1all"], T["ssq1all_b"], GROUPS,
                               D, "B")
    phase_B2(k, P, nc, ident, identb, T, n_tiles, rstd, rstdb)
    k.phase_end()

    k.phase_begin()
    phase_C(k, P, nc, ident, identb, T, n_tiles)
    g_og(0, final=True)
    k.phase_end()

    k.phase_begin()
    phase_D(k, P, nc, ident, identb, T, n_tiles, GROUPS)
    k.finish()
    return nc, k


S = 16384


def rope_tab(seq, dim, theta):
    inv = (1.0 / (np.float32(theta) ** (np.arange(0, dim, 2, dtype=np.float32) / np.float32(dim)))).astype(np.float32)
    ang = (np.arange(seq, dtype=np.float32)[:, None] * inv[None, :]).astype(np.float32)
    return np.concatenate([np.cos(ang), np.sin(ang)], -1).astype(np.float32)


def prep_A(inp, b, g, S_=S):
    w = inp["ret_w_in"][0]
    hs = [2 * g, 2 * g + 1]
    cols = []
    for base, wd in ((0, 128), (1024, 128), (2048, 256), (4096, 256)):
        for h in hs:
            cols.append(np.arange(base + h * wd, base + (h + 1) * wd))
    cols.append(np.arange(6144 + g * 256, 6144 + (g + 1) * 256))
    cols.append(np.arange(7168 + g * 256, 7168 + (g + 1) * 256))
    cols = np.concatenate(cols)
    wkv = inp["mem_w_kv"][0]
    wkvc = np.concatenate([np.arange(g * 256, (g + 1) * 256), np.arange(1024 + g * 256, 1024 + (g + 1) * 256)])
    dec = np.array([inp["ret_decay_fwd"][0, hs[0]], inp["ret_decay_fwd"][0, hs[1]],
                    inp["ret_decay_bwd"][0, hs[0]], inp["ret_decay_bwd"][0, hs[1]]], np.float32)
    return {
        "xA": np.ascontiguousarray(inp["x"][b, :S_]),
        "ngA": np.ascontiguousarray(inp["norm_g"][0]),
        "wA": np.ascontiguousarray(w[:, cols]),
        "memA": np.ascontiguousarray(inp["mem"][b]),
        "mngA": np.ascontiguousarray(inp["mem_norm_g"][0]),
        "wkvA": np.ascontiguousarray(wkv[:, wkvc]),
        "decA": dec,
        "rcsA": rope_tab(S_, 128, 10000.0),
    }


def prep_fused(inp, core, S_=16384):
    b, j = core // 4, core % 4
    d = prep_A(inp, b, j, S_)
    rows = []
    for g in range(4):
        rows.append(np.arange(g * 512, (g + 1) * 512))
        rows.append(np.arange(2048 + g * 256, 2048 + (g + 1) * 256))
    rows = np.concatenate(rows)
    cs = slice(j * 256, (j + 1) * 256)
    w = inp["diff_w_in"][0]
    hs = [2 * j, 2 * j + 1]
    cols = []
    for base in (0, 2048, 4096, 6144):
        for h in hs:
            cols.append(np.arange(base + h * 256, base + (h + 1) * 256))
    cols.append(np.arange(8192 + j * 256, 8192 + (j + 1) * 256))
    cols.append(np.arange(9216 + j * 256, 9216 + (j + 1) * 256))
    cols = np.concatenate(cols)
    wkv = inp["mem_w_kv"][1]
    kvc = np.concatenate([np.arange(j * 256, (j + 1) * 256), np.arange(1024 + j * 256, 1024 + (j + 1) * 256)])
    d.update({
        "xcB": np.ascontiguousarray(inp["x"][b, :S_, cs]),
        "woB": np.ascontiguousarray(inp["ret_w_out"][0][rows][:, cs]),
        "ngcB": np.ascontiguousarray(inp["norm_g"][1][cs]),
        "wB": np.ascontiguousarray(w[:, cols]),
        "memB": np.ascontiguousarray(inp["mem"][b]),
        "mngB": np.ascontiguousarray(inp["mem_norm_g"][1]),
        "wkvB": np.ascontiguousarray(wkv[:, kvc]),
        "dcsB": np.ascontiguousarray(rope_tab(S_, 32, 500000.0)),
        "lamC": np.ascontiguousarray(np.stack([inp["diff_lambda_q1"][0], inp["diff_lambda_k1"][0],
                                               inp["diff_lambda_q2"][0], inp["diff_lambda_k2"][0]])),
        "slgC": np.ascontiguousarray(inp["diff_subln_g"][0]),
        "woD": np.ascontiguousarray(inp["diff_w_out"][0][rows][:, cs]),
        "fgcD": np.ascontiguousarray(inp["final_norm_g"][cs]),
    })
    return d


_CACHE = {}


def kernel(**inputs):
    inp = {k_: np.asarray(v) for k_, v in inputs.items()}
    cores = list(range(8))
    if "nc" not in _CACHE:
        _CACHE["nc"] = build_fused(128)[0]
    res = run_bass_kernel_spmd(_CACHE["nc"], [prep_fused(inp, c) for c in cores], core_ids=cores).results
    out = np.empty((2, 16384, 1024), np.float32)
    for c in cores:
        out[c // 4, :, (c % 4) * 256:(c % 4 + 1) * 256] = np.asarray(res[c]["out"])
    return out
```

```python
import numpy as np
import concourse.bass as bass
import concourse.mybir as mybir
from concourse.bass_utils import run_bass_kernel_spmd

F32 = mybir.dt.float32
BF16 = mybir.dt.bfloat16
I32 = mybir.dt.int32
AF = mybir.ActivationFunctionType
ALU = mybir.AluOpType
AX = mybir.AxisListType

SEM_ROLL = 30000


class Buf:
    __slots__ = ("name", "w", "r", "excl")

    def __init__(self, name="", excl=False):
        self.name = name
        self.excl = excl
        self.w = None
        self.r = {}


class Eng:
    def __init__(self, K, name, e):
        self.K = K
        self.name = name
        self.e = e
        self.sem = K.nc.alloc_semaphore(f"s_{name}_0")
        self.nsem = 1
        self.cnt = 0
        self.waited = {}

    def roll(self):
        if self.cnt >= SEM_ROLL:
            self.sem = self.K.nc.alloc_semaphore(f"s_{self.name}_{self.nsem}")
            self.nsem += 1
            self.cnt = 0


class K:
    def __init__(self, nc, n_dma_sems=12):
        self.nc = nc
        self.eng = {
            "pe": Eng(self, "pe", nc.tensor),
            "act": Eng(self, "act", nc.scalar),
            "dve": Eng(self, "dve", nc.vector),
            "pool": Eng(self, "pool", nc.gpsimd),
            "sp": Eng(self, "sp", nc.sync),
        }
        self.dsems = {}
        for q in ("sp", "act", "pool"):
            self.dsems[q] = [[nc.alloc_semaphore(f"d_{q}_{i}"), 0] for i in range(n_dma_sems)]
        self.dnext = {"sp": 0, "act": 0, "pool": 0}
        self.out_tokens = []
        self.n_instr = 0

    def _wait(self, E, tok):
        sem, val = tok
        key = id(sem)
        if E.waited.get(key, 0) < val:
            E.e.wait_ge(sem, val)
            E.waited[key] = val

    def _deps(self, E, reads, writes, skip_own=False):
        toks = []
        for b in reads:
            if b.w is not None:
                toks.append(b.w)
        for b in writes:
            if b.w is not None:
                toks.append(b.w)
            toks.extend(b.r.values())
        for t in toks:
            if skip_own and t[0] is E.sem:
                continue
            self._wait(E, t)

    def _mark(self, tok, reads, writes):
        for b in reads:
            cur = b.r.get(id(tok[0]))
            if cur is None or cur[1] < tok[1]:
                b.r[id(tok[0])] = tok
        for b in writes:
            b.w = tok
            b.r = {}

    def op(self, eng, fn, reads=(), writes=(), inc=True):
        E = self.eng[eng]
        if any(b.excl for b in reads):
            writes = list(writes) + [b for b in reads if b.excl]
            reads = [b for b in reads if not b.excl]
        self._deps(E, reads, writes, skip_own=(eng == "pe"))
        ins = fn(E.e)
        self.n_instr += 1
        if inc:
            E.cnt += 1
            ins.then_inc(E.sem, 1)
            tok = (E.sem, E.cnt)
        else:
            tok = (E.sem, E.cnt + 1)
        self._mark(tok, reads, writes)
        if inc:
            E.roll()
        return tok

    def dma(self, q, out, in_, reads=(), writes=(), is_output=False, **kw):
        E = self.eng[q]
        self._deps(E, reads, writes)
        pool = self.dsems[q]
        i = self.dnext[q]
        self.dnext[q] = (i + 1) % len(pool)
        slot = pool[i]
        if slot[1] > 0:
            self._wait(E, (slot[0], slot[1]))
        ins = E.e.dma_start(out=out, in_=in_, **kw)
        slot[1] += 16
        ins.then_inc(slot[0], 16)
        tok = (slot[0], slot[1])
        self.n_instr += 1
        self._mark(tok, reads, writes)
        if is_output:
            self.out_tokens.append(tok)
        return tok

    def finish(self):
        E = self.eng["sp"]
        for q in self.dsems:
            for slot in self.dsems[q]:
                if slot[1] > 0:
                    self._wait(E, (slot[0], slot[1]))
        for n in ("pe", "act", "dve", "pool"):
            e2 = self.eng[n]
            if e2.cnt > 0:
                self._wait(E, (e2.sem, e2.cnt))


def _sb(self, name, shape, dt=F32):
    stack = getattr(self, "_stack", None)
    if stack is None:
        return self.nc.alloc_sbuf_tensor(name, list(shape), dt).ap()
    self._nsb = getattr(self, "_nsb", 0) + 1
    h = stack.enter_context(self.nc.sbuf_tensor(f"{name}_{self._nsb}", list(shape), dt))
    return h.ap()


def _phase_begin(self):
    import contextlib
    self._stack = contextlib.ExitStack()


def _barrier(self):
    toks = []
    for n in ("pe", "act", "dve", "pool", "sp"):
        e2 = self.eng[n]
        if e2.cnt > 0:
            toks.append((e2.sem, e2.cnt))
    for q in self.dsems:
        for slot in self.dsems[q]:
            if slot[1] > 0:
                toks.append((slot[0], slot[1]))
    toks.extend(getattr(self, "cc_tokens", []))
    for n in ("pe", "act", "dve", "pool", "sp"):
        E = self.eng[n]
        for t in toks:
            if t[0] is E.sem:
                continue
            self._wait(E, t)


def _phase_end(self):
    self.barrier()
    self._stack.close()
    self._stack = None


K.phase_begin = _phase_begin
K.phase_end = _phase_end
K.barrier = _barrier


def _ps(self, name, shape, dt=F32):
    return self.nc.alloc_psum_tensor(name, list(shape), dt).ap()


K.sb = _sb
K.ps = _ps


def _mm(self, out, lhsT, rhs, start, stop, reads, writes, inc=None, **kw):
    if inc is None:
        inc = stop
    return self.op("pe", lambda e: e.matmul(out, lhsT=lhsT, rhs=rhs, start=start, stop=stop, **kw),
                   reads, writes, inc=inc)


def _tr(self, out, in_, ident, reads, writes, inc=True):
    return self.op("pe", lambda e: e.transpose(out=out, in_=in_, identity=ident), reads, writes, inc=inc)


def _act(self, out, in_, func, reads, writes, **kw):
    return self.op("act", lambda e: e.activation(out=out, in_=in_, func=func, **kw), reads, writes)


def _tt(self, eng, out, in0, in1, op, reads, writes):
    return self.op(eng, lambda e: e.tensor_tensor(out=out, in0=in0, in1=in1, op=op), reads, writes)


def _ts(self, eng, out, in0, s1, s2, op0, op1, reads, writes):
    if op1 is None and eng == "pool" and op0 == ALU.mult:
        op1, s2 = ALU.mult, 1.0
    if op1 is None:
        return self.op(eng, lambda e: e.tensor_scalar(out=out, in0=in0, scalar1=s1, scalar2=None, op0=op0),
                       reads, writes)
    return self.op(eng, lambda e: e.tensor_scalar(out=out, in0=in0, scalar1=s1, scalar2=s2, op0=op0, op1=op1),
                   reads, writes)


def _stt(self, out, in0, scalar, in1, op0, op1, reads, writes):
    return self.op("dve", lambda e: e.scalar_tensor_tensor(out=out, in0=in0, scalar=scalar, in1=in1,
                                                           op0=op0, op1=op1), reads, writes)


def _cp(self, eng, out, in_, reads, writes):
    if eng == "act":
        return self.op("act", lambda e: e.copy(out=out, in_=in_), reads, writes)
    return self.op(eng, lambda e: e.tensor_copy(out=out, in_=in_), reads, writes)


def _recip(self, out, in_, reads, writes):
    return self.op("dve", lambda e: e.reciprocal(out=out, in_=in_), reads, writes)


def _memset(self, eng, ap, val, writes):
    return self.op(eng, lambda e: e.memset(ap, val), (), writes)


K.mm = _mm
K.tr = _tr
K.act = _act
K.tt = _tt
K.ts = _ts
K.stt = _stt
K.cp = _cp
K.recip = _recip
K.memset = _memset


class Ring:
    def __init__(self, k, name, shape, dt, n):
        self.aps = [k.sb(f"{name}{i}", shape, dt) for i in range(n)]
        self.bufs = [Buf(f"{name}{i}") for i in range(n)]
        self.n = n
        self.i = -1

    def next(self):
        self.i = (self.i + 1) % self.n
        return self.aps[self.i], self.bufs[self.i]

    def cur(self):
        return self.aps[self.i], self.bufs[self.i]


class Psum:
    def __init__(self, k):
        self.banks = [k.ps(f"bank{i}", [128, 512], F32) for i in range(8)]
        self.bufs = [Buf(f"bank{i}", excl=True) for i in range(8)]

    def f32(self, i):
        return self.banks[i]

    def bf16(self, i):
        return self.banks[i].bitcast(BF16)


def _collective(self, kind, in_ap, out_ap, groups, reads=(), writes=()):
    E = self.eng["pool"]
    self._deps(E, reads, writes)
    if not hasattr(self, "cc_sem"):
        self.cc_sem = self.nc.alloc_semaphore("cc_sem")
        self.cc_cnt = 0
        self.cc_tokens = []
    ins = E.e.collective_compute(kind, ALU.bypass, replica_groups=groups, ins=[in_ap.opt()], outs=[out_ap.opt()])
    ins.then_inc(self.cc_sem)
    self.cc_cnt += 1
    tok = (self.cc_sem, self.cc_cnt)
    self.n_instr += 1
    self._mark(tok, reads, writes)
    self.cc_tokens = [tok]
    return tok


K.collective = _collective


S = 16384
D = 1024
NT = S // 128
EPS = 1e-6


def norm_to_T(k, P, xt, xb, gt, gb, ident, identb, n_feat, scr, hnT, hnTb, pbank):
    junk, junkb, ss, ssb, rstd, rstdb, hn, hnb = scr
    k.act(junk, xt, AF.Square, [xb], [junkb, ssb], accum_out=ss)
    k.act(rstd, ss, AF.Sqrt, [ssb], [rstdb], scale=1.0 / n_feat, bias=EPS)
    k.recip(rstd, rstd, [rstdb], [rstdb])
    k.stt(hn, xt, rstd, gt, ALU.mult, ALU.mult, [xb, rstdb, gb], [hnb])
    nch = n_feat // 128
    pT = P.bf16(pbank)
    for c in range(nch):
        k.tr(pT[:, c * 128:(c + 1) * 128], hn[:, c * 128:(c + 1) * 128], ident, [hnb, identb], [P.bufs[pbank]],
             inc=(c == nch - 1))
    k.cp("act", hnT, pT[:, 0:nch * 128].rearrange("p (c t) -> p c t", c=nch), [P.bufs[pbank]], [hnTb])


def make_ident(k):
    io = k.sb("io_id", [128, 128], F32)
    iob = Buf("io")
    idf = k.sb("identf", [128, 128], F32)
    ident = k.sb("ident", [128, 128], BF16)
    identb = Buf("ident")
    k.op("pool", lambda e: e.iota(io, pattern=[[1, 128]], base=0, channel_multiplier=-1,
                                  allow_small_or_imprecise_dtypes=True), (), [iob])
    k.ts("dve", idf, io, 0.0, None, ALU.is_equal, None, [iob], [iob])
    k.cp("dve", ident, idf, [iob], [identb])
    return ident, identb, io, iob


def mem_prep(k, P, nc, mem, mng, wkv, nheads, ident, identb, scr, gt, gb):
    mkT = k.sb("mkT", [128, nheads, 2, 256], BF16)
    mkTb = Buf("mkT")
    mva = k.sb("mva", [128, nheads, 2, 258], BF16)
    mvab = Buf("mva")
    mt_x = k.sb("mem_x", [128, D], F32)
    mt_xb = Buf("mem_x")
    mg = k.sb("mem_g", [128, D], F32)
    mgb = Buf("mem_g")
    mnT = k.sb("mnT", [128, 8, 128], BF16)
    mnTb = Buf("mnT")
    wk = k.sb("wkv_sb", [128, 8, 512], BF16)
    wkb = Buf("wkv_sb")
    mkt = k.sb("mk_tok", [128, 256], BF16)
    mktb = Buf("mk_tok")
    k.dma("sp", mg, mng.partition_broadcast(128), (), [mgb])
    k.memset("dve", mva, 1.0, [mvab])
    for hd in range(nheads):
        k.dma("pool", wk, wkv[:, hd * 512:(hd + 1) * 512].rearrange("(c p) n -> p c n", p=128), (), [wkb])
        for mt in range(2):
            k.dma("sp", mt_x, mem[mt * 128:(mt + 1) * 128, :], (), [mt_xb])
            norm_to_T(k, P, mt_x, mt_xb, mg, mgb, ident, identb, D, scr, mnT, mnTb, 0)
            pp = P.f32(1)
            for c in range(8):
                k.mm(pp, mnT[:, c, :], wk[:, c, :], c == 0, c == 7, [mnTb, wkb], [P.bufs[1]])
            k.cp("act", mkt, pp[:, 0:256], [P.bufs[1]], [mktb])
            k.cp("dve", mva[:, hd, mt, 0:256], pp[:, 256:512], [P.bufs[1]], [mvab])
            pT = P.bf16(0)
            for dc in range(2):
                k.tr(pT[:, dc * 128:(dc + 1) * 128], mkt[:, dc * 128:(dc + 1) * 128], ident, [mktb, identb],
                     [P.bufs[0]], inc=(dc == 1))
            k.cp("act", mkT[:, hd, :, mt * 128:(mt + 1) * 128],
                 pT[:, 0:256].rearrange("p (c t) -> p c t", c=2), [P.bufs[0]], [mkTb])
    return mkT, mkTb, mva, mvab


def mem_attn(k, P, qm_src, qm_srcb, sgm, sgmb, mkT_h, mkTb, mva_h, mvab, ident, identb, tmp, banks, out, outb):
    qmT, qmTb, E, Eb, rs, rsb = tmp
    bT, bS, bO = banks
    pT = P.bf16(bT)
    for dc in range(2):
        k.tr(pT[:, dc * 128:(dc + 1) * 128], qm_src[:, dc * 128:(dc + 1) * 128], ident, [qm_srcb, identb],
             [P.bufs[bT]], inc=(dc == 1))
    k.cp("act", qmT, pT[:, 0:256].rearrange("p (c t) -> p c t", c=2), [P.bufs[bT]], [qmTb])
    pS = P.f32(bS)
    for mt in range(2):
        for dc in range(2):
            k.mm(pS[:, mt * 128:(mt + 1) * 128], mkT_h[:, dc, mt * 128:(mt + 1) * 128], qmT[:, dc, :],
                 dc == 0, dc == 1, [mkTb, qmTb], [P.bufs[bS]], inc=(mt == 1 and dc == 1))
    k.act(E, pS[:, 0:256].rearrange("p (c t) -> p c t", c=2), AF.Exp, [P.bufs[bS]], [Eb], scale=1.0 / 16.0)
    pO = P.f32(bO)
    for mt in range(2):
        k.mm(pO[:, 0:257], E[:, mt, :], mva_h[:, mt, 0:257], mt == 0, mt == 1, [Eb, mvab], [P.bufs[bO]])
    k.recip(rs, pO[:, 256:257], [P.bufs[bO]], [rsb])
    k.stt(out, pO[:, 0:256], rs, sgm, ALU.mult, ALU.mult, [P.bufs[bO], rsb, sgmb], [outb])


def phase_A(k, P, nc, ident, identb, io, iob, T, n_tiles):
    x, ng, w, mem, mng, wkv, dec, rcs = (T[n] for n in ("xA", "ngA", "wA", "memA", "mngA", "wkvA", "decA", "rcsA"))
    yg, scrd, sbs = T["yg"], T["scrA"], T["sbsA"]
    yg_mb, yg_yb = T["yg_mb"], T["yg_yb"]
    gather_yg = T["gather_yg"]
    scr_b = [Buf(f"scr{i}") for i in range(n_tiles)]
    sbs_b = [Buf(f"sbs{i}") for i in range(n_tiles)]
    stop = 99
    wt = k.sb("wt", [128, 8, 2048], BF16)
    wtb = Buf("wt")
    for c4 in range(4):
        k.dma("pool", wt[:, :, c4 * 512:(c4 + 1) * 512],
              w[:, c4 * 512:(c4 + 1) * 512].rearrange("(c p) n -> p c n", p=128), (), [wtb])
    gt = k.sb("gt", [128, D], F32)
    gb = Buf("gt")
    k.dma("sp", gt, ng.partition_broadcast(128), (), [gb])

    def mkscr(tag, nf):
        return (k.sb("junk" + tag, [128, nf], F32), Buf(), k.sb("ss" + tag, [128, 1], F32), Buf(),
                k.sb("rstd" + tag, [128, 1], F32), Buf(), k.sb("hn" + tag, [128, nf], BF16), Buf())

    scr = mkscr("a", D)
    mkT, mkTb, mva, mvab = mem_prep(k, P, nc, mem, mng, wkv, 1, ident, identb, scr, gt, gb)

    lg = k.sb("lg", [128, 4], F32)
    lgb = Buf("lg")
    k.dma("sp", lg, dec.partition_broadcast(128), (), [lgb])
    k.act(lg, lg, AF.Exp, [lgb], [lgb], scale=-1.0)
    k.act(lg, lg, AF.Ln, [lgb], [lgb], bias=1.0)
    k.ts("dve", lg, lg, -1.0, None, ALU.mult, None, [lgb], [lgb])
    cb = Buf("consts")
    tmpc = k.sb("tmpc", [128, 128], F32)
    tmpm = k.sb("tmpm", [128, 128], F32)
    tmpe = k.sb("tmpe", [128, 128], F32)
    DT = k.sb("DT", [128, 2, 128], F32)
    QDF = k.sb("QDF", [128, 2, 128], F32)
    QDB = k.sb("QDB", [128, 2, 128], F32)
    kd = k.sb("kd", [128, 4], F32)
    cd = k.sb("cd", [128, 4], F32)
    ci = k.sb("ci", [128, 128], F32)
    cbk = k.sb("cbk", [128, 128], F32)
    pi = k.sb("pi", [128, 2], F32)
    for h in range(2):
        k.ts("dve", tmpc, io, 0.0, None, ALU.max, None, [iob], [cb])
        k.act(tmpe, tmpc, AF.Exp, [cb, lgb], [cb], scale=lg[:, h:h + 1])
        k.ts("dve", tmpm, io, 0.0, None, ALU.is_ge, None, [iob], [cb])
        k.tt("dve", DT[:, h, :], tmpe, tmpm, ALU.mult, [cb], [cb])
        k.ts("dve", tmpc, io, -1.0, 0.0, ALU.mult, ALU.max, [iob], [cb])
        k.act(tmpe, tmpc, AF.Exp, [cb, lgb], [cb], scale=lg[:, 2 + h:3 + h])
        k.ts("dve", tmpm, io, 0.0, None, ALU.is_lt, None, [iob], [cb])
        k.tt("dve", tmpe, tmpe, tmpm, ALU.mult, [cb], [cb])
        k.tt("dve", DT[:, h, :], DT[:, h, :], tmpe, ALU.add, [cb], [cb])
    k.op("pool", lambda e: e.iota(ci, pattern=[[1, 128]], base=1, channel_multiplier=0,
                                  allow_small_or_imprecise_dtypes=True), (), [cb])
    k.op("pool", lambda e: e.iota(cbk, pattern=[[-1, 128]], base=128, channel_multiplier=0,
                                  allow_small_or_imprecise_dtypes=True), (), [cb])
    k.op("pool", lambda e: e.iota(pi[:, 0:1], pattern=[[0, 1]], base=127, channel_multiplier=-1,
                                  allow_small_or_imprecise_dtypes=True), (), [cb])
    k.op("pool", lambda e: e.iota(pi[:, 1:2], pattern=[[0, 1]], base=0, channel_multiplier=1,
                                  allow_small_or_imprecise_dtypes=True), (), [cb])
    for h in range(2):
        k.act(QDF[:, h, :], ci, AF.Exp, [cb, lgb], [cb], scale=lg[:, h:h + 1])
        k.act(QDB[:, h, :], cbk, AF.Exp, [cb, lgb], [cb], scale=lg[:, 2 + h:3 + h])
        k.act(kd[:, h:h + 1], pi[:, 0:1], AF.Exp, [cb, lgb], [cb], scale=lg[:, h:h + 1])
        k.act(kd[:, 2 + h:3 + h], pi[:, 1:2], AF.Exp, [cb, lgb], [cb], scale=lg[:, 2 + h:3 + h])
    k.act(cd, lg, AF.Exp, [lgb], [cb], scale=128.0)

    NRING = 8
    rg = dict(
        x=Ring(k, "pxt", [128, D], F32, 3), rc=Ring(k, "prc", [128, 128], F32, NRING),
        hn=Ring(k, "phn", [128, D], BF16, 3), hnT=Ring(k, "phnT", [128, 8, 128], BF16, 2),
        qk=Ring(k, "pqk", [128, 4, 128], F32, 3), qkr=Ring(k, "pqkr", [128, 4, 128], BF16, 4),
        st=Ring(k, "pst", [128, 2048], BF16, 5), qm=Ring(k, "pqm", [128, 256], BF16, 3),
        sgm=Ring(k, "psgm", [128, 256], F32, NRING), qmT=Ring(k, "pqmT", [128, 2, 128], BF16, 3),
        E=Ring(k, "pE", [128, 2, 128], BF16, 3), mgt=Ring(k, "pmgt", [128, 256], BF16, 3),
        mgT=Ring(k, "pmgT", [128, 2, 128], BF16, 3), ss=Ring(k, "pss", [128, 2], F32, 3),
        rs=Ring(k, "prs", [128, 1], F32, 3), t1=Ring(k, "pt1", [128, 4, 64], F32, 2),
        t2=Ring(k, "pt2", [128, 4, 64], F32, 2), junk=Ring(k, "pjunk", [128, D], BF16, 2),
    )
    tl = {}

    def g(i, name):
        d = tl.setdefault(i, {})
        if name not in d:
            d[name] = rg[name].next()
        return d[name]

    def f_load(i):
        xt, xb = g(i, "x")
        rc, rcb = g(i, "rc")
        k.dma("sp", xt, x[i * 128:(i + 1) * 128, :], (), [xb])
        k.dma("sp", rc, rcs[i * 128:(i + 1) * 128, :], (), [rcb])

    def f_norm(i):
        xt, xb = g(i, "x")
        ss, ssb = g(i, "ss")
        hn, hnb = g(i, "hn")
        junk, junkb = g(i, "junk")
        k.act(junk, xt, AF.Square, [xb], [junkb, ssb], accum_out=ss[:, 0:1])
        k.act(ss[:, 1:2], ss[:, 0:1], AF.Sqrt, [ssb], [ssb], scale=1.0 / D, bias=EPS)
        k.recip(ss[:, 1:2], ss[:, 1:2], [ssb], [ssb])
        k.stt(hn, xt, ss[:, 1:2], gt, ALU.mult, ALU.mult, [xb, ssb, gb], [hnb])

    def f_tr(i):
        hn, hnb = g(i, "hn")
        pT = P.bf16(0)
        for c in range(8):
            k.tr(pT[:, c * 128:(c + 1) * 128], hn[:, c * 128:(c + 1) * 128], ident, [hnb, identb], [P.bufs[0]],
                 inc=(c == 7))

    def f_proj(i):
        hnT, hnTb = g(i, "hnT")
        k.cp("act", hnT, P.bf16(0)[:, 0:1024].rearrange("p (c t) -> p c t", c=8), [P.bufs[0]], [hnTb])
        for gi in range(4):
            pp = P.f32(1 + gi)
            for c in range(8):
                k.mm(pp, hnT[:, c, :], wt[:, c, gi * 512:(gi + 1) * 512], c == 0, c == 7, [hnTb, wtb],
                     [P.bufs[1 + gi]])

    def f_evac(i):
        st, stb = g(i, "st")
        qk, qkb = g(i, "qk")
        qm, qmb = g(i, "qm")
        sgm, sgmb = g(i, "sgm")
        p0 = P.f32(1)
        k.cp("act", qk[:, 0:2, :], p0[:, 0:256].rearrange("p (u d) -> p u d", u=2), [P.bufs[1]], [qkb])
        k.act(qk[:, 2:4, :], p0[:, 256:512].rearrange("p (u d) -> p u d", u=2), AF.Copy, [P.bufs[1]], [qkb],
              scale=128.0 ** -0.5)
        k.cp("act", st[:, 1024:1536], P.f32(2), [P.bufs[2]], [stb])
        k.act(st[:, 1536:2048], P.f32(3), AF.Silu, [P.bufs[3]], [stb])
        p3 = P.f32(4)
        k.cp("dve", qm, p3[:, 0:256], [P.bufs[4]], [qmb])
        k.act(sgm, p3[:, 256:512], AF.Silu, [P.bufs[4]], [sgmb])

    def f_rot(i):
        qk, qkb = g(i, "qk")
        qkr, qkrb = g(i, "qkr")
        rc, rcb = g(i, "rc")
        t1, t1b = rg["t1"].next()
        t2, t2b = rg["t2"].next()
        x1 = qk[:, :, 0:64]
        x2 = qk[:, :, 64:128]
        cosb = rc[:, 0:64].unsqueeze(1).to_broadcast([128, 4, 64])
        sinb = rc[:, 64:128].unsqueeze(1).to_broadcast([128, 4, 64])
        k.tt("dve", t1, x1, cosb, ALU.mult, [qkb, rcb], [t1b])
        k.tt("dve", t2, x2, sinb, ALU.mult, [qkb, rcb], [t2b])
        k.tt("dve", qkr[:, :, 0:64], t1, t2, ALU.subtract, [t1b, t2b], [qkrb])
        k.tt("dve", t1, x1, sinb, ALU.mult, [qkb, rcb], [t1b])
        k.tt("dve", t2, x2, cosb, ALU.mult, [qkb, rcb], [t2b])
        k.tt("dve", qkr[:, :, 64:128], t1, t2, ALU.add, [t1b, t2b], [qkrb])
        qm, qmb = g(i, "qm")
        pT = P.bf16(6)
        for dc in range(2):
            k.tr(pT[:, 512 + dc * 128:512 + (dc + 1) * 128], qm[:, dc * 128:(dc + 1) * 128], ident, [qmb, identb],
                 [P.bufs[6]], inc=(dc == 1))

    def f_kfb(i):
        qkr, qkrb = g(i, "qkr")
        st, stb = g(i, "st")
        for h in range(2):
            k.ts("pool", st[:, 512 + h * 128:512 + (h + 1) * 128], qkr[:, 2 + h, :], kd[:, h:h + 1], None,
                 ALU.mult, None, [qkrb, cb], [stb])
            k.ts("pool", st[:, 768 + h * 128:768 + (h + 1) * 128], qkr[:, 2 + h, :], kd[:, 2 + h:3 + h], None,
                 ALU.mult, None, [qkrb, cb], [stb])
        qmT, qmTb = g(i, "qmT")
        k.cp("act", qmT, P.bf16(6)[:, 512:768].rearrange("p (c t) -> p c t", c=2), [P.bufs[6]], [qmTb])
        pT = P.bf16(6)
        for u in range(4):
            k.tr(pT[:, u * 128:(u + 1) * 128], qkr[:, u, :], ident, [qkrb, identb], [P.bufs[6]], inc=(u == 3))

    def f_store(i):
        st, stb = g(i, "st")
        k.cp("act", st[:, 0:512], P.bf16(6)[:, 0:512], [P.bufs[6]], [stb])
        k.dma("sp", scrd[i], st, [stb], [scr_b[i]])
        qmT, qmTb = g(i, "qmT")
        pS = P.f32(7)
        for mt in range(2):
            for dc in range(2):
                k.mm(pS[:, mt * 128:(mt + 1) * 128], mkT[:, 0][:, dc, mt * 128:(mt + 1) * 128], qmT[:, dc, :],
                     dc == 0, dc == 1, [mkTb, qmTb], [P.bufs[7]], inc=(mt == 1 and dc == 1))

    def f_exp(i):
        E, Eb = g(i, "E")
        k.act(E, P.f32(7)[:, 0:256].rearrange("p (c t) -> p c t", c=2), AF.Exp, [P.bufs[7]], [Eb], scale=1.0 / 16.0)

    def f_av(i):
        E, Eb = g(i, "E")
        pO = P.f32(5)
        for mt in range(2):
            k.mm(pO[:, 0:257], E[:, mt, :], mva[:, 0][:, mt, 0:257], mt == 0, mt == 1, [Eb, mvab], [P.bufs[5]])

    def f_mnorm(i):
        rs, rsb = g(i, "rs")
        sgm, sgmb = g(i, "sgm")
        mgt, mgtb = g(i, "mgt")
        pO = P.f32(5)
        k.recip(rs, pO[:, 256:257], [P.bufs[5]], [rsb])
        k.stt(mgt, pO[:, 0:256], rs, sgm, ALU.mult, ALU.mult, [P.bufs[5], rsb, sgmb], [mgtb])

    def f_mtr(i):
        mgt, mgtb = g(i, "mgt")
        pT = P.bf16(7)
        for ec in range(2):
            k.tr(pT[:, 512 + ec * 128:512 + (ec + 1) * 128], mgt[:, ec * 128:(ec + 1) * 128], ident,
                 [mgtb, identb], [P.bufs[7]], inc=(ec == 1))

    def f_mout(i):
        mgT, mgTb = g(i, "mgT")
        k.cp("act", mgT, P.bf16(7)[:, 512:768].rearrange("p (c t) -> p c t", c=2), [P.bufs[7]], [mgTb])
        k.dma("sp", yg[i][:, 512:768], mgT.rearrange("p c t -> p (c t)"), [mgTb], [yg_mb[i]])
        tl.pop(i, None)

    stages = [(0, f_load), (1, f_norm), (2, f_tr), (3, f_proj), (4, f_evac), (5, f_rot), (6, f_kfb), (7, f_store),
              (8, f_exp), (9, f_av), (10, f_mnorm), (11, f_mtr), (12, f_mout)]
    stages = stages[::-1]
    maxoff = max(o for o, _ in stages)
    for t in range(n_tiles + maxoff):
        for off, fn in stages:
            i = t - off
            if 0 <= i < n_tiles:
                fn(i)

    Sb = k.sb("Sb", [128, 2, 256], F32)
    Sbb = Buf("Sb")
    Sf = k.sb("Sf", [128, 2, 256], F32)
    Sfb = Buf("Sf")
    Sf16 = k.sb("Sf16", [128, 2, 256], BF16)
    Sf16b = Buf("Sf16")
    k.memset("dve", Sb, 0.0, [Sbb])
    k.memset("dve", Sf, 0.0, [Sfb])
    k.memset("dve", Sf16, 0.0, [Sf16b])
    kvr = Ring(k, "kvr", [128, 1024], BF16, 3)
    sbst = Ring(k, "sbst", [128, 512], BF16, 3)
    ldB = {}

    def loadBk(n):
        if n >= 0:
            kv, kvb = kvr.next()
            k.dma("sp", kv, scrd[n][:, 512:1536], [scr_b[n]], [kvb])
            ldB[n] = (kv, kvb)

    loadBk(n_tiles - 1)
    loadBk(n_tiles - 2)
    for n in range(n_tiles - 1, -1, -1):
        loadBk(n - 2)
        kv, kvb = ldB.pop(n)
        so, sob = sbst.next()
        k.cp("act", so, Sb.rearrange("p h e -> p (h e)"), [Sbb], [sob])
        k.dma("sp", sbs[n], so, [sob], [sbs_b[n]])
        if n == 0:
            break
        for h in range(2):
            bk = 1 + h
            k.mm(P.f32(bk)[:, 0:256], kv[:, 256 + h * 128:256 + (h + 1) * 128], kv[:, 512 + h * 256:512 + (h + 1) * 256],
                 True, True, [kvb], [P.bufs[bk]])
            k.stt(Sb[:, h, :], Sb[:, h, :], cd[:, 2 + h:3 + h], P.f32(bk)[:, 0:256], ALU.mult, ALU.add,
                  [Sbb, cb, P.bufs[bk]], [Sbb])

    chr_ = Ring(k, "chk", [128, 2048], BF16, 3)
    sbr = Ring(k, "sbin", [128, 512], BF16, 3)
    AT = Ring(k, "AT", [128, 128], BF16, 2)
    qsf = Ring(k, "qsf", [128, 128], BF16, 2)
    qsb = Ring(k, "qsb", [128, 128], BF16, 2)
    ygr = Ring(k, "ygt", [128, 256], BF16, 2)
    ygT = Ring(k, "ygT", [128, 4, 128], BF16, 2)
    ss2 = k.sb("ss2", [128, 1], F32)
    ss2b = Buf("ss2")
    junk2 = k.sb("junk2", [128, 256], F32)
    junk2b = Buf("junk2")
    ldF = {}

    def loadF(n):
        if n < n_tiles:
            ch, chb = chr_.next()
            sbn, sbnb = sbr.next()
            k.dma("sp", ch, scrd[n], [scr_b[n]], [chb])
            k.dma("sp", sbn, sbs[n], [sbs_b[n]], [sbnb])
            ldF[n] = (ch, chb, sbn, sbnb)

    loadF(0)
    loadF(1)
    for n in range(n_tiles):
        loadF(n + 2)
        ch, chb, sbn, sbnb = ldF.pop(n)
        yT, yTb = ygT.next()
        for h in range(2):
            qT = ch[:, h * 128:(h + 1) * 128]
            kT = ch[:, 256 + h * 128:256 + (h + 1) * 128]
            kf = ch[:, 512 + h * 128:512 + (h + 1) * 128]
            v = ch[:, 1024 + h * 256:1024 + (h + 1) * 256]
            sg = ch[:, 1536 + h * 256:1536 + (h + 1) * 256]
            bS, bY, bKV = 1 + h, 3 + h, 5 + h
            a, ab = AT.next()
            f, fb = qsf.next()
            bq, bqb = qsb.next()
            k.mm(P.f32(bS)[:, 0:128], kT, qT, True, True, [chb], [P.bufs[bS]])
            k.tt("dve", a, P.f32(bS)[:, 0:128], DT[:, h, :], ALU.mult, [P.bufs[bS], cb], [ab])
            k.tt("pool", f, qT, QDF[:, h, :], ALU.mult, [chb, cb], [fb])
            k.tt("pool", bq, qT, QDB[:, h, :], ALU.mult, [chb, cb], [bqb])
            pY = P.f32(bY)[:, 0:256]
            k.mm(pY, a, v, True, False, [ab, chb], [P.bufs[bY]])
            k.mm(pY, f, Sf16[:, h, :], False, False, [fb, Sf16b], [P.bufs[bY]])
            k.mm(pY, bq, sbn[:, h * 256:(h + 1) * 256], False, True, [bqb, sbnb], [P.bufs[bY]])
            if n < n_tiles - 1:
                pKV = P.f32(bKV)[:, 0:256]
                k.mm(pKV, kf, v, True, True, [chb], [P.bufs[bKV]])
                k.stt(Sf[:, h, :], Sf[:, h, :], cd[:, h:h + 1], pKV, ALU.mult, ALU.add,
                      [Sfb, cb, P.bufs[bKV]], [Sfb])
                k.cp("act", Sf16[:, h, :], Sf[:, h, :], [Sfb], [Sf16b])
            k.act(junk2, pY, AF.Square, [P.bufs[bY]], [junk2b, ss2b], accum_out=ss2)
            k.act(ss2, ss2, AF.Sqrt, [ss2b], [ss2b], scale=1.0 / 256, bias=EPS)
            k.recip(ss2, ss2, [ss2b], [ss2b])
            yg_t, ygb = ygr.next()
            k.stt(yg_t, pY, ss2, sg, ALU.mult, ALU.mult, [P.bufs[bY], ss2b, chb], [ygb])
            pT = P.bf16(7)
            for ec in range(2):
                k.tr(pT[:, (h * 2 + ec) * 128:(h * 2 + ec + 1) * 128], yg_t[:, ec * 128:(ec + 1) * 128], ident,
                     [ygb, identb], [P.bufs[7]], inc=(ec == 1))
            k.cp("act", yT[:, h * 2:(h + 1) * 2, :],
                 pT[:, h * 256:(h + 1) * 256].rearrange("p (c t) -> p c t", c=2), [P.bufs[7]], [yTb])
        k.dma("sp", yg[n][:, 0:512], yT.rearrange("p c t -> p (c t)"), [yTb], [yg_yb[n]])
        gather_yg(n)


import math

LAMBDA_INIT = 0.8 - 0.6 * math.exp(-0.3 * 1)
def sumsq_gather(k, P, nc, ssq, ssqb, dr_in, dr_in_b, dr_all, dr_all_b, groups, n_feat, tag):
    nt = ssq.shape[1]
    k.dma("sp", dr_in, ssq, [ssqb], [dr_in_b])
    k.collective("AllGather", dr_in, dr_all, groups, [dr_in_b], [dr_all_b])
    g4 = k.sb("ssq4" + tag, [128, 4, nt], F32)
    g4b = Buf("ssq4" + tag)
    k.dma("sp", g4, dr_all.rearrange("(r p) n -> p r n", p=128), [dr_all_b], [g4b])
    rstd = k.sb("rstd" + tag, [128, nt], F32)
    rstdb = Buf("rstd" + tag)
    k.tt("dve", rstd, g4[:, 0, :], g4[:, 1, :], ALU.add, [g4b], [rstdb])
    k.tt("dve", rstd, rstd, g4[:, 2, :], ALU.add, [g4b, rstdb], [rstdb])
    k.tt("dve", rstd, rstd, g4[:, 3, :], ALU.add, [g4b, rstdb], [rstdb])
    k.act(rstd, rstd, AF.Sqrt, [rstdb], [rstdb], scale=1.0 / n_feat, bias=EPS)
    k.recip(rstd, rstd, [rstdb], [rstdb])
    return rstd, rstdb


def colproj(k, P, src, srcb, wot, wotb, bank):
    pp = P.f32(bank)[:, 0:256]
    for g in range(4):
        for c in range(6):
            ch = g * 6 + c
            k.mm(pp, src[:, g, c * 128:(c + 1) * 128], wot[:, ch, :], ch == 0, ch == 23, [srcb, wotb],
                 [P.bufs[bank]])
    return pp


def phase_B(k, P, nc, ident, identb, T, n_tiles):
    ygall, ygall_cb = T["ygall"], T["ygall_cb"]
    xc, wo, ngc = T["xcB"], T["woB"], T["ngcB"]
    h1s, h1s_b = T["h1s"], T["h1s_b"]
    h1gt, h1gt_tb = T["h1gt"], T["h1gt_tb"]
    gather_h1 = T["gather_h1"]
    wot = k.sb("wotB", [128, 24, 256], BF16)
    wotb = Buf("wotB")
    for c in range(2):
        k.dma("pool", wot[:, c * 12:(c + 1) * 12, :],
              wo[c * 1536:(c + 1) * 1536, :].rearrange("(c p) n -> p c n", p=128), (), [wotb])
    gc = k.sb("gcB", [128, 256], F32)
    gcb = Buf("gcB")
    k.dma("sp", gc, ngc.partition_broadcast(128), (), [gcb])
    ssq = k.sb("ssqB", [128, n_tiles], F32)
    ssqb = Buf("ssqB")
    k.memset("dve", ssq, 0.0, [ssqb])
    ygr = Ring(k, "ygtB", [128, 4, 768], BF16, 3)
    xr = Ring(k, "xcB", [128, 256], F32, 3)
    hr = Ring(k, "h1sB", [128, 256], F32, 2)
    hgr = Ring(k, "h1gB", [128, 256], BF16, 2)
    hTr = Ring(k, "h1gTB", [128, 2, 128], BF16, 2)
    junk = k.sb("junkB", [128, 256], BF16)
    junkb = Buf("junkB")
    yv = ygall.rearrange("(c g ii) p n -> c ii p g n", g=4, ii=4)
    ld = {}

    def load(i):
        if i < n_tiles:
            yt, yb = ygr.next()
            xt, xb = xr.next()
            k.dma("sp", yt, yv[i // 4][i % 4], [ygall_cb[i // 4]], [yb])
            k.dma("sp", xt, xc[i * 128:(i + 1) * 128, :], (), [xb])
            ld[i] = (yt, yb, xt, xb)

    pps = {}

    def proj(i):
        if i < n_tiles:
            yt, yb, xt, xb = ld[i]
            bank = 1 + (i % 2)
            pps[i] = (colproj(k, P, yt, yb, wot, wotb, bank), bank)

    load(0)
    load(1)
    proj(0)
    for i in range(n_tiles):
        load(i + 2)
        proj(i + 1)
        yt, yb, xt, xb = ld.pop(i)
        pp, bank = pps.pop(i)
        h, hb = hr.next()
        k.tt("dve", h, pp, xt, ALU.add, [P.bufs[bank], xb], [hb])
        k.dma("sp", h1s[i], h, [hb], [h1s_b[i]])
        k.act(junk, h, AF.Square, [hb], [junkb, ssqb], accum_out=ssq[:, i:i + 1])
        hg, hgb = hgr.next()
        k.tt("dve", hg, h, gc, ALU.mult, [hb, gcb], [hgb])
        hT, hTb = hTr.next()
        pT = P.bf16(3 + (i % 2))
        for c in range(2):
            k.tr(pT[:, c * 128:(c + 1) * 128], hg[:, c * 128:(c + 1) * 128], ident, [hgb, identb],
                 [P.bufs[3 + (i % 2)]], inc=(c == 1))
        k.cp("act", hT, pT[:, 0:256].rearrange("p (c t) -> p c t", c=2), [P.bufs[3 + (i % 2)]], [hTb])
        k.dma("sp", h1gt[i], hT.rearrange("p c t -> p (c t)"), [hTb], [h1gt_tb[i]])
        gather_h1(i)
    return ssq, ssqb


def phase_B2(k, P, nc, ident, identb, T, n_tiles, rstd, rstdb):
    hall, hall_cb = T["h1gtall"], T["h1gtall_cb"]
    w, mem, mng, wkv, dcs = T["wB"], T["memB"], T["mngB"], T["wkvB"], T["dcsB"]
    scr1, scr1_b = T["scr1"], T["scr1_b"]
    ogt, ogt_mb = T["ogt"], T["ogt_mb"]
    wt = k.sb("wtB2", [128, 8, 2560], BF16)
    wtb = Buf("wtB2")
    for c in range(5):
        k.dma("pool", wt[:, :, c * 512:(c + 1) * 512],
              w[:, c * 512:(c + 1) * 512].rearrange("(c p) n -> p c n", p=128), (), [wtb])
    gt = k.sb("gtB2", [128, D], F32)
    gb = Buf("gtB2")
    scr = (k.sb("junkB2", [128, D], BF16), Buf(), k.sb("ssB2", [128, 1], F32), Buf(),
           k.sb("rstdB2", [128, 1], F32), Buf(), k.sb("hnB2", [128, D], BF16), Buf())
    mkT, mkTb, mva, mvab = mem_prep(k, P, nc, mem, mng, wkv, 1, ident, identb, scr, gt, gb)
    rg = dict(
        hn=Ring(k, "qhn", [128, 4, 256], BF16, 3), rp=Ring(k, "qrp", [128, 32], F32, 5),
        rot=Ring(k, "qrot", [128, 8, 32], F32, 3), qkr=Ring(k, "qqkr", [128, 8, 128], BF16, 4),
        st=Ring(k, "qst", [128, 2048], BF16, 5), qm=Ring(k, "qqm", [128, 256], BF16, 3),
        sgm=Ring(k, "qsgm", [128, 256], F32, 8), qmT=Ring(k, "qqmT", [128, 2, 128], BF16, 3),
        E=Ring(k, "qE", [128, 2, 128], BF16, 3), mgt=Ring(k, "qmgt", [128, 256], BF16, 3),
        mgT=Ring(k, "qmgT", [128, 2, 128], BF16, 3), rs=Ring(k, "qrs", [128, 1], F32, 3),
        t1=Ring(k, "qt1", [128, 8, 16], F32, 2), t2=Ring(k, "qt2", [128, 8, 16], F32, 2),
    )
    hv = hall.rearrange("(c g ii) p n -> c ii p g n", g=4, ii=4)
    tl = {}

    def g(i, name):
        d = tl.setdefault(i, {})
        if name not in d:
            d[name] = rg[name].next()
        return d[name]

    def f_load(i):
        hn, hnb = g(i, "hn")
        rp, rpb = g(i, "rp")
        k.dma("sp", hn, hv[i // 4][i % 4], [hall_cb[i // 4]], [hnb])
        k.dma("sp", rp, dcs[i * 128:(i + 1) * 128, :], (), [rpb])

    def f_proj(i):
        hn, hnb = g(i, "hn")
        for gi in range(5):
            pp = P.f32(gi)
            for c in range(8):
                k.mm(pp, hn[:, c // 2, (c % 2) * 128:(c % 2 + 1) * 128], wt[:, c, gi * 512:(gi + 1) * 512],
                     c == 0, c == 7, [hnb, wtb], [P.bufs[gi]])

    def f_evac(i):
        r = rstd[:, i:i + 1]
        st, stb = g(i, "st")
        rot, rotb = g(i, "rot")
        qkr, qkrb = g(i, "qkr")
        qm, qmb = g(i, "qm")
        sgm, sgmb = g(i, "sgm")
        for gi in range(2):
            p3 = P.f32(gi).rearrange("p (u d) -> p u d", u=4)
            k.act(rot[:, gi * 4:(gi + 1) * 4, :], p3[:, :, 0:32], AF.Copy, [P.bufs[gi], rstdb], [rotb], scale=r)
            k.act(qkr[:, gi * 4:(gi + 1) * 4, :], p3, AF.Copy, [P.bufs[gi], rstdb], [qkrb], scale=r)
        k.act(st[:, 1024:1536], P.f32(2), AF.Copy, [P.bufs[2], rstdb], [stb], scale=r)
        k.act(st[:, 1536:2048], P.f32(3), AF.Silu, [P.bufs[3], rstdb], [stb], scale=r)
        p4 = P.f32(4)
        k.ts("dve", qm, p4[:, 0:256], r, None, ALU.mult, None, [P.bufs[4], rstdb], [qmb])
        k.act(sgm, p4[:, 256:512], AF.Silu, [P.bufs[4], rstdb], [sgmb], scale=r)

    def f_rot(i):
        rot, rotb = g(i, "rot")
        qkr, qkrb = g(i, "qkr")
        rp, rpb = g(i, "rp")
        t1, t1b = rg["t1"].next()
        t2, t2b = rg["t2"].next()
        x1 = rot[:, :, 0:16]
        x2 = rot[:, :, 16:32]
        cosb = rp[:, 0:16].unsqueeze(1).to_broadcast([128, 8, 16])
        sinb = rp[:, 16:32].unsqueeze(1).to_broadcast([128, 8, 16])
        k.tt("dve", t1, x1, cosb, ALU.mult, [rotb, rpb], [t1b])
        k.tt("dve", t2, x2, sinb, ALU.mult, [rotb, rpb], [t2b])
        k.tt("dve", qkr[:, :, 0:16], t1, t2, ALU.subtract, [t1b, t2b], [qkrb])
        k.tt("dve", t1, x1, sinb, ALU.mult, [rotb, rpb], [t1b])
        k.tt("dve", t2, x2, cosb, ALU.mult, [rotb, rpb], [t2b])
        k.tt("dve", qkr[:, :, 16:32], t1, t2, ALU.add, [t1b, t2b], [qkrb])
        qm, qmb = g(i, "qm")
        pT = P.bf16(6)
        for dc in range(2):
            k.tr(pT[:, 768 + dc * 128:768 + (dc + 1) * 128], qm[:, dc * 128:(dc + 1) * 128], ident, [qmb, identb],
                 [P.bufs[6]], inc=(dc == 1))

    def f_qktr(i):
        qkr, qkrb = g(i, "qkr")
        qmT, qmTb = g(i, "qmT")
        k.cp("act", qmT, P.bf16(6)[:, 768:1024].rearrange("p (c t) -> p c t", c=2), [P.bufs[6]], [qmTb])
        pT = P.bf16(5)
        for u in range(8):
            k.tr(pT[:, u * 128:(u + 1) * 128], qkr[:, u, :], ident, [qkrb, identb], [P.bufs[5]], inc=(u == 7))

    def f_store(i):
        st, stb = g(i, "st")
        k.cp("dve", st[:, 0:1024], P.bf16(5)[:, 0:1024], [P.bufs[5]], [stb])
        k.dma("sp", scr1[i], st, [stb], [scr1_b[i]])
        qmT, qmTb = g(i, "qmT")
        pS = P.f32(7)
        for mt in range(2):
            for dc in range(2):
                k.mm(pS[:, mt * 128:(mt + 1) * 128], mkT[:, 0][:, dc, mt * 128:(mt + 1) * 128], qmT[:, dc, :],
                     dc == 0, dc == 1, [mkTb, qmTb], [P.bufs[7]], inc=(mt == 1 and dc == 1))

    def f_exp(i):
        E, Eb = g(i, "E")
        k.act(E, P.f32(7)[:, 0:256].rearrange("p (c t) -> p c t", c=2), AF.Exp, [P.bufs[7]], [Eb], scale=1.0 / 16.0)

    def f_av(i):
        E, Eb = g(i, "E")
        pO = P.f32(6)
        for mt in range(2):
            k.mm(pO[:, 0:257], E[:, mt, :], mva[:, 0][:, mt, 0:257], mt == 0, mt == 1, [Eb, mvab], [P.bufs[6]])

    def f_mnorm(i):
        rs, rsb = g(i, "rs")
        sgm, sgmb = g(i, "sgm")
        mgt, mgtb = g(i, "mgt")
        pO = P.f32(6)
        k.recip(rs, pO[:, 256:257], [P.bufs[6]], [rsb])
        k.stt(mgt, pO[:, 0:256], rs, sgm, ALU.mult, ALU.mult, [P.bufs[6], rsb, sgmb], [mgtb])

    def f_mtr(i):
        mgt, mgtb = g(i, "mgt")
        pT = P.bf16(7)
        for ec in range(2):
            k.tr(pT[:, 512 + ec * 128:512 + (ec + 1) * 128], mgt[:, ec * 128:(ec + 1) * 128], ident,
                 [mgtb, identb], [P.bufs[7]], inc=(ec == 1))

    def f_mout(i):
        mgT, mgTb = g(i, "mgT")
        k.cp("act", mgT, P.bf16(7)[:, 512:768].rearrange("p (c t) -> p c t", c=2), [P.bufs[7]], [mgTb])
        k.dma("sp", ogt[i][:, 512:768], mgT.rearrange("p c t -> p (c t)"), [mgTb], [ogt_mb[i]])
        tl.pop(i, None)

    stages = [(0, f_load), (1, f_proj), (2, f_evac), (3, f_rot), (4, f_qktr), (5, f_store), (6, f_exp), (7, f_av),
              (8, f_mnorm), (9, f_mtr), (10, f_mout)][::-1]
    maxoff = 10
    for t in range(n_tiles + maxoff):
        for off, fn in stages:
            i = t - off
            if 0 <= i < n_tiles:
                fn(i)


def phase_C(k, P, nc, ident, identb, T, n_tiles):
    scr1, scr1_b = T["scr1"], T["scr1_b"]
    ogt, ogt_yb = T["ogt"], T["ogt_yb"]
    gather_og = T["gather_og"]
    lam4, slg = T["lamC"], T["slgC"]
    n_all = n_tiles
    nq = n_tiles // 4
    lv = k.sb("lv", [128, 4, 128], F32)
    lvb = Buf("lv")
    k.dma("sp", lv, lam4.rearrange("a d -> (a d)").partition_broadcast(128).rearrange("p (a d) -> p a d", a=4), (), [lvb])
    lp = k.sb("lp", [128, 2, 128], F32)
    k.tt("dve", lp[:, 0, :], lv[:, 0, :], lv[:, 1, :], ALU.mult, [lvb], [lvb])
    k.tt("dve", lp[:, 1, :], lv[:, 2, :], lv[:, 3, :], ALU.mult, [lvb], [lvb])
    ls = k.sb("ls", [128, 2], F32)
    k.op("dve", lambda e: e.tensor_reduce(out=ls, in_=lp, axis=AX.X, op=ALU.add), [lvb], [lvb])
    k.act(ls, ls, AF.Exp, [lvb], [lvb])
    nlam = k.sb("nlam", [128, 1], F32)
    nlamb = Buf("nlam")
    k.tt("dve", nlam, ls[:, 1:2], ls[:, 0:1], ALU.subtract, [lvb], [nlamb])
    k.ts("dve", nlam, nlam, -LAMBDA_INIT, None, ALU.add, None, [nlamb], [nlamb])
    sgain = k.sb("sgain", [128, 256], F32)
    sgainb = Buf("sgain")
    k.dma("sp", sgain, slg.partition_broadcast(128), (), [sgainb])
    k.ts("dve", sgain, sgain, 1.0 - LAMBDA_INIT, None, ALU.mult, None, [sgainb], [sgainb])

    kc = [k.sb(f"kc{c}", [128, n_all, 128], BF16) for c in range(2)]
    kcb = [Buf("kc0"), Buf("kc1")]
    va = k.sb("va", [128, n_all, 258], BF16)
    vab = Buf("va")
    k.memset("dve", va, 1.0, [vab])
    qtr = Ring(k, "qtile", [128, 2, 4, 128], BF16, 2)
    sgr = Ring(k, "sgt", [128, 4, 256], BF16, 2)
    er = Ring(k, "E", [128, 512], BF16, 3)
    on = k.sb("on", [128, 2, 4, 256], F32)
    onb = Buf("on")
    rs = k.sb("rs", [128, 4], F32)
    rsb = Buf("rs")
    ssq = k.sb("ssq", [128, 4], F32)
    ssqb = Buf("ssq")
    junk = k.sb("junkc", [128, 256], BF16)
    junkb = Buf("junkc")
    ogtile = k.sb("ogtile", [128, 4, 256], BF16)
    ogtileb = Buf("ogtile")
    ogTr = Ring(k, "ogT", [128, 4, 2, 128], BF16, 2)
    scale = 128.0 ** -0.5
    sbank = 0
    allscr = list(scr1_b)
    for hh in range(2):
        for c in range(2):
            u = hh * 2 + c
            for i0 in range(0, n_all, 16):
                i1 = min(n_all, i0 + 16)
                k.dma("sp", kc[c][:, i0:i1, :],
                      scr1[i0:i1, :, 512 + u * 128:512 + (u + 1) * 128].rearrange("i p t -> p i t"),
                      allscr[i0:i1], [kcb[c]])
        for i0 in range(0, n_all, 16):
            i1 = min(n_all, i0 + 16)
            k.dma("sp", va[:, i0:i1, 0:256],
                  scr1[i0:i1, :, 1024 + hh * 256:1024 + (hh + 1) * 256].rearrange("i p e -> p i e"),
                  allscr[i0:i1], [vab])
        ld = {}

        def load(q):
            if q < nq:
                qt, qtb = qtr.next()
                sgt, sgtb = sgr.next()
                for c in range(2):
                    u = hh * 2 + c
                    k.dma("sp", qt[:, c], scr1[q * 4:(q + 1) * 4, :, u * 128:(u + 1) * 128].rearrange("i p t -> p i t"),
                          allscr[q * 4:(q + 1) * 4], [qtb])
                k.dma("sp", sgt, scr1[q * 4:(q + 1) * 4, :, 1536 + hh * 256:1536 + (hh + 1) * 256].rearrange("i p e -> p i e"),
                      allscr[q * 4:(q + 1) * 4], [sgtb])
                ld[q] = (qt, qtb, sgt, sgtb)

        load(0)
        for q in range(nq):
            load(q + 1)
            qt, qtb, sgt, sgtb = ld.pop(q)
            items = [(c, kt_i) for c in range(2) for kt_i in range(n_all)]
            pend = {}

            def emit_S(c, kt_i):
                nonlocal sbank
                sbank = (sbank + 1) % 3
                pS = P.f32(sbank)
                k.mm(pS, kc[c][:, kt_i, :], qt[:, c].rearrange("p a t -> p (a t)"), True, True, [kcb[c], qtb],
                     [P.bufs[sbank]])
                pend[(c, kt_i)] = sbank

            emit_S(*items[0])
            emit_S(*items[1])
            for idx, (c, kt_i) in enumerate(items):
                if idx + 2 < len(items):
                    emit_S(*items[idx + 2])
                sb_ = pend.pop((c, kt_i))
                E, Eb = er.next()
                k.act(E, P.f32(sb_), AF.Exp, [P.bufs[sb_]], [Eb], scale=scale)
                for qs in range(4):
                    k.mm(P.f32(3 + qs)[:, 0:257], E[:, qs * 128:(qs + 1) * 128], va[:, kt_i, 0:257],
                         kt_i == 0, kt_i == n_all - 1, [Eb, vab], [P.bufs[3 + qs]], inc=(qs == 3))
                if kt_i == n_all - 1:
                    for qs in range(4):
                        pO = P.f32(3 + qs)
                        k.recip(rs[:, qs:qs + 1], pO[:, 256:257], [P.bufs[3 + qs]], [rsb])
                        k.ts("dve", on[:, c, qs, :], pO[:, 0:256], rs[:, qs:qs + 1], None, ALU.mult, None,
                             [P.bufs[3 + qs], rsb], [onb])
            on0 = on[:, 0].rearrange("p a e -> p (a e)")
            on1 = on[:, 1].rearrange("p a e -> p (a e)")
            k.stt(on0, on1, nlam, on0, ALU.mult, ALU.add, [onb, nlamb], [onb])
            for qs in range(4):
                k.act(junk, on[:, 0, qs, :], AF.Square, [onb], [junkb, ssqb], accum_out=ssq[:, qs:qs + 1])
            k.act(ssq, ssq, AF.Sqrt, [ssqb], [ssqb], scale=1.0 / 256, bias=EPS)
            k.recip(ssq, ssq, [ssqb], [ssqb])
            for qs in range(4):
                k.stt(on[:, 0, qs, :], on[:, 0, qs, :], ssq[:, qs:qs + 1], sgain, ALU.mult, ALU.mult,
                      [onb, ssqb, sgainb], [onb])
            k.tt("dve", ogtile.rearrange("p a e -> p (a e)"), on0, sgt.rearrange("p a e -> p (a e)"), ALU.mult,
                 [onb, sgtb], [ogtileb])
            ogT, ogTb = ogTr.next()
            pT = P.bf16(7)
            for qs in range(4):
                for ec in range(2):
                    k.tr(pT[:, (qs * 2 + ec) * 128:(qs * 2 + ec + 1) * 128], ogtile[:, qs, ec * 128:(ec + 1) * 128],
                         ident, [ogtileb, identb], [P.bufs[7]], inc=(qs == 3 and ec == 1))
            k.cp("act", ogT.rearrange("p a c t -> p (a c t)"), pT[:, 0:1024], [P.bufs[7]], [ogTb])
            k.dma("sp", ogt[q * 4:(q + 1) * 4, :, hh * 256:(hh + 1) * 256].rearrange("i p n -> p i n"),
                  ogT.rearrange("p a c t -> p a (c t)"), [ogTb], [ogt_yb[q][hh]])
            if hh == 1:
                gather_og(q)


def phase_D(k, P, nc, ident, identb, T, n_tiles, GROUPS):
    ogall, ogall_cb = T["ogtall"], T["ogtall_cb"]
    wo, fgc = T["woD"], T["fgcD"]
    h1s, h1s_b = T["h1s"], T["h1s_b"]
    out = T["out"]
    wot = k.sb("wotD", [128, 24, 256], BF16)
    wotb = Buf("wotD")
    for c in range(2):
        k.dma("pool", wot[:, c * 12:(c + 1) * 12, :],
              wo[c * 1536:(c + 1) * 1536, :].rearrange("(c p) n -> p c n", p=128), (), [wotb])
    gc = k.sb("gcD", [128, 256], F32)
    gcb = Buf("gcD")
    k.dma("sp", gc, fgc.partition_broadcast(128), (), [gcb])
    ssq = k.sb("ssqD", [128, n_tiles], F32)
    ssqb = Buf("ssqD")
    k.memset("dve", ssq, 0.0, [ssqb])
    h2 = k.sb("h2D", [128, n_tiles, 256], F32)
    h2b = [Buf(f"h2_{i}") for i in range(n_tiles)]
    ogr = Ring(k, "ogD", [128, 4, 768], BF16, 3)
    hr = Ring(k, "h1D", [128, 256], F32, 3)
    junk = k.sb("junkD", [128, 256], BF16)
    junkb = Buf("junkD")
    ov = ogall.rearrange("(c g ii) p n -> c ii p g n", g=4, ii=4)
    ld = {}

    def load(i):
        if i < n_tiles:
            og, ogb = ogr.next()
            h, hb = hr.next()
            k.dma("sp", og, ov[i // 4][i % 4], [ogall_cb[i // 4]], [ogb])
            k.dma("sp", h, h1s[i], [h1s_b[i]], [hb])
            ld[i] = (og, ogb, h, hb)

    pps = {}

    def proj(i):
        if i < n_tiles:
            og, ogb, h, hb = ld[i]
            bank = 1 + (i % 2)
            pps[i] = (colproj(k, P, og, ogb, wot, wotb, bank), bank)

    load(0)
    load(1)
    proj(0)
    for i in range(n_tiles):
        load(i + 2)
        proj(i + 1)
        og, ogb, h, hb = ld.pop(i)
        pp, bank = pps.pop(i)
        k.tt("dve", h2[:, i, :], pp, h, ALU.add, [P.bufs[bank], hb], [h2b[i]])
        k.act(junk, h2[:, i, :], AF.Square, [h2b[i]], [junkb, ssqb], accum_out=ssq[:, i:i + 1])
    rstd, rstdb = sumsq_gather(k, P, nc, ssq, ssqb, T["ssq2"], T["ssq2_b"], T["ssq2all"], T["ssq2all_b"], GROUPS,
                               D, "D")
    outr = Ring(k, "outD", [128, 256], F32, 3)
    for i in range(n_tiles):
        o, ob = outr.next()
        k.stt(o, h2[:, i, :], rstd[:, i:i + 1], gc, ALU.mult, ALU.mult, [h2b[i], rstdb, gcb], [ob])
        k.dma("sp", out[i * 128:(i + 1) * 128, :], o, [ob], (), is_output=True)


def build_fused(n_tiles=128, groups=None):
    GROUPS = groups or [[0, 1, 2, 3], [4, 5, 6, 7]]
    nc = bass.Bass("TRN2", target_bir_lowering=False)
    S_ = n_tiles * 128
    T = {}

    def ext(name, shape, dt=F32):
        T[name] = nc.dram_tensor(name, list(shape), dt, kind="ExternalInput").ap()

    def scratch(name, shape, dt):
        T[name] = nc.dram_tensor(name, list(shape), dt).ap()
        T[name + "_b"] = Buf(name)

    ext("xA", [S_, D]); ext("ngA", [D]); ext("wA", [D, 2048]); ext("memA", [256, D]); ext("mngA", [D])
    ext("wkvA", [D, 512]); ext("decA", [4]); ext("rcsA", [S_, 128])
    ext("xcB", [S_, 256]); ext("woB", [3072, 256]); ext("ngcB", [256])
    ext("wB", [D, 2560]); ext("memB", [256, D]); ext("mngB", [D]); ext("wkvB", [D, 512]); ext("dcsB", [S_, 32])
    ext("lamC", [4, 128]); ext("slgC", [256]); ext("woD", [3072, 256]); ext("fgcD", [256])
    T["out"] = nc.dram_tensor("out", [S_, 256], F32, kind="ExternalOutput").ap()
    scratch("yg", [n_tiles, 128, 768], BF16)
    scratch("ygall", [4 * n_tiles, 128, 768], BF16)
    T["scrA"] = nc.dram_tensor("scrA", [n_tiles, 128, 2048], BF16).ap()
    T["sbsA"] = nc.dram_tensor("sbsA", [n_tiles, 128, 512], BF16).ap()
    T["h1s"] = nc.dram_tensor("h1s", [n_tiles, 128, 256], F32).ap()
    T["h1s_b"] = [Buf(f"h1s{i}") for i in range(n_tiles)]
    scratch("h1gt", [n_tiles, 128, 256], BF16)
    scratch("h1gtall", [4 * n_tiles, 128, 256], BF16)
    scratch("ssq1", [128, n_tiles], F32)
    scratch("ssq1all", [4 * 128, n_tiles], F32)
    T["scr1"] = nc.dram_tensor("scr1", [n_tiles, 128, 2048], BF16).ap()
    T["scr1_b"] = [Buf(f"scr1_{i}") for i in range(n_tiles)]
    scratch("ogt", [n_tiles, 128, 768], BF16)
    scratch("ogtall", [4 * n_tiles, 128, 768], BF16)
    scratch("ssq2", [128, n_tiles], F32)
    scratch("ssq2all", [4 * 128, n_tiles], F32)

    k = K(nc)
    P = Psum(k)
    ident, identb, io, iob = make_ident(k)

    def flat(ap):
        return ap.rearrange("i p n -> (i p) n")

    nch = n_tiles // 4

    def mk_gather(src, dst, dst_cb, rd, lag):
        done = set()

        def emit(c):
            if c in done or c >= nch or c < 0:
                return
            done.add(c)
            k.collective("AllGather", flat(T[src][c * 4:(c + 1) * 4]), flat(T[dst][c * 16:(c + 1) * 16]), GROUPS,
                         rd(c), [dst_cb[c]])

        def f(i, final=False):
            if final:
                for c in range(nch):
                    emit(c)
            else:
                c = (i - 3 - lag) // 4
                if (i - 3 - lag) % 4 == 0:
                    emit(c)
        return f

    T["yg_mb"] = [Buf() for _ in range(n_tiles)]
    T["yg_yb"] = [Buf() for _ in range(n_tiles)]
    T["ygall_cb"] = [Buf() for _ in range(nch)]
    T["gather_yg"] = mk_gather("yg", "ygall", T["ygall_cb"],
                               lambda c: T["yg_mb"][c * 4:(c + 1) * 4] + T["yg_yb"][c * 4:(c + 1) * 4], 2)
    T["h1gt_tb"] = [Buf() for _ in range(n_tiles)]
    T["h1gtall_cb"] = [Buf() for _ in range(nch)]
    T["gather_h1"] = mk_gather("h1gt", "h1gtall", T["h1gtall_cb"], lambda c: T["h1gt_tb"][c * 4:(c + 1) * 4], 2)
    T["ogt_mb"] = [Buf() for _ in range(n_tiles)]
    T["ogt_yb"] = [[Buf(), Buf()] for _ in range(nch)]
    T["ogtall_cb"] = [Buf() for _ in range(nch)]
    g_og = mk_gather("ogt", "ogtall", T["ogtall_cb"],
                     lambda c: T["ogt_mb"][c * 4:(c + 1) * 4] + T["ogt_yb"][c], 0)
    T["gather_og"] = lambda q: g_og(q * 4 + 3 - 4) if q > 0 else None

    k.phase_begin()
    phase_A(k, P, nc, ident, identb, io, iob, T, n_tiles)
    T["gather_yg"](0, final=True)
    k.phase_end()

    k.phase_begin()
    ssq, ssqb = phase_B(k, P, nc, ident, identb, T, n_tiles)
    T["gather_h1"](0, final=True)
    rstd, rstdb = sumsq_gather(k, P, nc, ssq, ssqb, T["ssq1"], T["ssq1_b"], T["ssq1all"], T["ssq1all_b"], GROUPS,
                               D, "B")
    phase_B2(k, P, nc, ident, identb, T, n_tiles, rstd, rstdb)
    k.phase_end()

    k.phase_begin()
    phase_C(k, P, nc, ident, identb, T, n_tiles)
    g_og(0, final=True)
    k.phase_end()

    k.phase_begin()
    phase_D(k, P, nc, ident, identb, T, n_tiles, GROUPS)
    k.finish()
    return nc, k


S = 16384


def rope_tab(seq, dim, theta):
    inv = (1.0 / (np.float32(theta) ** (np.arange(0, dim, 2, dtype=np.float32) / np.float32(dim)))).astype(np.float32)
    ang = (np.arange(seq, dtype=np.float32)[:, None] * inv[None, :]).astype(np.float32)
    return np.concatenate([np.cos(ang), np.sin(ang)], -1).astype(np.float32)


def prep_A(inp, b, g, S_=S):
    w = inp["ret_w_in"][0]
    hs = [2 * g, 2 * g + 1]
    cols = []
    for base, wd in ((0, 128), (1024, 128), (2048, 256), (4096, 256)):
        for h in hs:
            cols.append(np.arange(base + h * wd, base + (h + 1) * wd))
    cols.append(np.arange(6144 + g * 256, 6144 + (g + 1) * 256))
    cols.append(np.arange(7168 + g * 256, 7168 + (g + 1) * 256))
    cols = np.concatenate(cols)
    wkv = inp["mem_w_kv"][0]
    wkvc = np.concatenate([np.arange(g * 256, (g + 1) * 256), np.arange(1024 + g * 256, 1024 + (g + 1) * 256)])
    dec = np.array([inp["ret_decay_fwd"][0, hs[0]], inp["ret_decay_fwd"][0, hs[1]],
                    inp["ret_decay_bwd"][0, hs[0]], inp["ret_decay_bwd"][0, hs[1]]], np.float32)
    return {
        "xA": np.ascontiguousarray(inp["x"][b, :S_]),
        "ngA": np.ascontiguousarray(inp["norm_g"][0]),
        "wA": np.ascontiguousarray(w[:, cols]),
        "memA": np.ascontiguousarray(inp["mem"][b]),
        "mngA": np.ascontiguousarray(inp["mem_norm_g"][0]),
        "wkvA": np.ascontiguousarray(wkv[:, wkvc]),
        "decA": dec,
        "rcsA": rope_tab(S_, 128, 10000.0),
    }


def prep_fused(inp, core, S_=16384):
    b, j = core // 4, core % 4
    d = prep_A(inp, b, j, S_)
    rows = []
    for g in range(4):
        rows.append(np.arange(g * 512, (g + 1) * 512))
        rows.append(np.arange(2048 + g * 256, 2048 + (g + 1) * 256))
    rows = np.concatenate(rows)
    cs = slice(j * 256, (j + 1) * 256)
    w = inp["diff_w_in"][0]
    hs = [2 * j, 2 * j + 1]
    cols = []
    for base in (0, 2048, 4096, 6144):
        for h in hs:
            cols.append(np.arange(base + h * 256, base + (h + 1) * 256))
    cols.append(np.arange(8192 + j * 256, 8192 + (j + 1) * 256))
    cols.append(np.arange(9216 + j * 256, 9216 + (j + 1) * 256))
    cols = np.concatenate(cols)
    wkv = inp["mem_w_kv"][1]
    kvc = np.concatenate([np.arange(j * 256, (j + 1) * 256), np.arange(1024 + j * 256, 1024 + (j + 1) * 256)])
    d.update({
        "xcB": np.ascontiguousarray(inp["x"][b, :S_, cs]),
        "woB": np.ascontiguousarray(inp["ret_w_out"][0][rows][:, cs]),
        "ngcB": np.ascontiguousarray(inp["norm_g"][1][cs]),
        "wB": np.ascontiguousarray(w[:, cols]),
        "memB": np.ascontiguousarray(inp["mem"][b]),
        "mngB": np.ascontiguousarray(inp["mem_norm_g"][1]),
        "wkvB": np.ascontiguousarray(wkv[:, kvc]),
        "dcsB": np.ascontiguousarray(rope_tab(S_, 32, 500000.0)),
        "lamC": np.ascontiguousarray(np.stack([inp["diff_lambda_q1"][0], inp["diff_lambda_k1"][0],
                                               inp["diff_lambda_q2"][0], inp["diff_lambda_k2"][0]])),
        "slgC": np.ascontiguousarray(inp["diff_subln_g"][0]),
        "woD": np.ascontiguousarray(inp["diff_w_out"][0][rows][:, cs]),
        "fgcD": np.ascontiguousarray(inp["final_norm_g"][cs]),
    })
    return d


_CACHE = {}


def kernel(**inputs):
    inp = {k_: np.asarray(v) for k_, v in inputs.items()}
    cores = list(range(8))
    if "nc" not in _CACHE:
        _CACHE["nc"] = build_fused(128)[0]
    res = run_bass_kernel_spmd(_CACHE["nc"], [prep_fused(inp, c) for c in cores], core_ids=cores).results
    out = np.empty((2, 16384, 1024), np.float32)
    for c in cores:
        out[c // 4, :, (c % 4) * 256:(c % 4 + 1) * 256] = np.asarray(res[c]["out"])
    return out
```

```python
import numpy as np
import concourse.bass as bass
import concourse.mybir as mybir
from concourse.bass_utils import run_bass_kernel_spmd

F32 = mybir.dt.float32
BF16 = mybir.dt.bfloat16
I32 = mybir.dt.int32
AF = mybir.ActivationFunctionType
ALU = mybir.AluOpType
AX = mybir.AxisListType

SEM_ROLL = 30000


class Buf:
    __slots__ = ("name", "w", "r", "excl")

    def __init__(self, name="", excl=False):
        self.name = name
        self.excl = excl
        self.w = None
        self.r = {}


class Eng:
    def __init__(self, K, name, e):
        self.K = K
        self.name = name
        self.e = e
        self.sem = K.nc.alloc_semaphore(f"s_{name}_0")
        self.nsem = 1
        self.cnt = 0
        self.waited = {}

    def roll(self):
        if self.cnt >= SEM_ROLL:
            self.sem = self.K.nc.alloc_semaphore(f"s_{self.name}_{self.nsem}")
            self.nsem += 1
            self.cnt = 0


class K:
    def __init__(self, nc, n_dma_sems=12):
        self.nc = nc
        self.eng = {
            "pe": Eng(self, "pe", nc.tensor),
            "act": Eng(self, "act", nc.scalar),
            "dve": Eng(self, "dve", nc.vector),
            "pool": Eng(self, "pool", nc.gpsimd),
            "sp": Eng(self, "sp", nc.sync),
        }
        self.dsems = {}
        for q in ("sp", "act", "pool"):
            self.dsems[q] = [[nc.alloc_semaphore(f"d_{q}_{i}"), 0] for i in range(n_dma_sems)]
        self.dnext = {"sp": 0, "act": 0, "pool": 0}
        self.out_tokens = []
        self.n_instr = 0

    def _wait(self, E, tok):
        sem, val = tok
        key = id(sem)
        if E.waited.get(key, 0) < val:
            E.e.wait_ge(sem, val)
            E.waited[key] = val

    def _deps(self, E, reads, writes, skip_own=False):
        toks = []
        for b in reads:
            if b.w is not None:
                toks.append(b.w)
        for b in writes:
            if b.w is not None:
                toks.append(b.w)
            toks.extend(b.r.values())
        for t in toks:
            if skip_own and t[0] is E.sem:
                continue
            self._wait(E, t)

    def _mark(self, tok, reads, writes):
        for b in reads:
            cur = b.r.get(id(tok[0]))
            if cur is None or cur[1] < tok[1]:
                b.r[id(tok[0])] = tok
        for b in writes:
            b.w = tok
            b.r = {}

    def op(self, eng, fn, reads=(), writes=(), inc=True):
        E = self.eng[eng]
        if any(b.excl for b in reads):
            writes = list(writes) + [b for b in reads if b.excl]
            reads = [b for b in reads if not b.excl]
        self._deps(E, reads, writes, skip_own=(eng == "pe"))
        ins = fn(E.e)
        self.n_instr += 1
        if inc:
            E.cnt += 1
            ins.then_inc(E.sem, 1)
            tok = (E.sem, E.cnt)
        else:
            tok = (E.sem, E.cnt + 1)
        self._mark(tok, reads, writes)
        if inc:
            E.roll()
        return tok

    def dma(self, q, out, in_, reads=(), writes=(), is_output=False, **kw):
        E = self.eng[q]
        self._deps(E, reads, writes)
        pool = self.dsems[q]
        i = self.dnext[q]
        self.dnext[q] = (i + 1) % len(pool)
        slot = pool[i]
        if slot[1] > 0:
            self._wait(E, (slot[0], slot[1]))
        ins = E.e.dma_start(out=out, in_=in_, **kw)
        slot[1] += 16
        ins.then_inc(slot[0], 16)
        tok = (slot[0], slot[1])
        self.n_instr += 1
        self._mark(tok, reads, writes)
        if is_output:
            self.out_tokens.append(tok)
        return tok

    def finish(self):
        E = self.eng["sp"]
        for q in self.dsems:
            for slot in self.dsems[q]:
                if slot[1] > 0:
                    self._wait(E, (slot[0], slot[1]))
        for n in ("pe", "act", "dve", "pool"):
            e2 = self.eng[n]
            if e2.cnt > 0:
                self._wait(E, (e2.sem, e2.cnt))


def _sb(self, name, shape, dt=F32):
    stack = getattr(self, "_stack", None)
    if stack is None:
        return self.nc.alloc_sbuf_tensor(name, list(shape), dt).ap()
    self._nsb = getattr(self, "_nsb", 0) + 1
    h = stack.enter_context(self.nc.sbuf_tensor(f"{name}_{self._nsb}", list(shape), dt))
    return h.ap()


def _phase_begin(self):
    import contextlib
    self._stack = contextlib.ExitStack()


def _barrier(self):
    toks = []
    for n in ("pe", "act", "dve", "pool", "sp"):
        e2 = self.eng[n]
        if e2.cnt > 0:
            toks.append((e2.sem, e2.cnt))
    for q in self.dsems:
        for slot in self.dsems[q]:
            if slot[1] > 0:
                toks.append((slot[0], slot[1]))
    toks.extend(getattr(self, "cc_tokens", []))
    for n in ("pe", "act", "dve", "pool", "sp"):
        E = self.eng[n]
        for t in toks:
            if t[0] is E.sem:
                continue
            self._wait(E, t)


def _phase_end(self):
    self.barrier()
    self._stack.close()
    self._stack = None


K.phase_begin = _phase_begin
K.phase_end = _phase_end
K.barrier = _barrier


def _ps(self, name, shape, dt=F32):
    return self.nc.alloc_psum_tensor(name, list(shape), dt).ap()


K.sb = _sb
K.ps = _ps


def _mm(self, out, lhsT, rhs, start, stop, reads, writes, inc=None, **kw):
    if inc is None:
        inc = stop
    return self.op("pe", lambda e: e.matmul(out, lhsT=lhsT, rhs=rhs, start=start, stop=stop, **kw),
                   reads, writes, inc=inc)


def _tr(self, out, in_, ident, reads, writes, inc=True):
    return self.op("pe", lambda e: e.transpose(out=out, in_=in_, identity=ident), reads, writes, inc=inc)


def _act(self, out, in_, func, reads, writes, **kw):
    return self.op("act", lambda e: e.activation(out=out, in_=in_, func=func, **kw), reads, writes)


def _tt(self, eng, out, in0, in1, op, reads, writes):
    return self.op(eng, lambda e: e.tensor_tensor(out=out, in0=in0, in1=in1, op=op), reads, writes)


def _ts(self, eng, out, in0, s1, s2, op0, op1, reads, writes):
    if op1 is None and eng == "pool" and op0 == ALU.mult:
        op1, s2 = ALU.mult, 1.0
    if op1 is None:
        return self.op(eng, lambda e: e.tensor_scalar(out=out, in0=in0, scalar1=s1, scalar2=None, op0=op0),
                       reads, writes)
    return self.op(eng, lambda e: e.tensor_scalar(out=out, in0=in0, scalar1=s1, scalar2=s2, op0=op0, op1=op1),
                   reads, writes)


def _stt(self, out, in0, scalar, in1, op0, op1, reads, writes):
    return self.op("dve", lambda e: e.scalar_tensor_tensor(out=out, in0=in0, scalar=scalar, in1=in1,
                                                           op0=op0, op1=op1), reads, writes)


def _cp(self, eng, out, in_, reads, writes):
    if eng == "act":
        return self.op("act", lambda e: e.copy(out=out, in_=in_), reads, writes)
    return self.op(eng, lambda e: e.tensor_copy(out=out, in_=in_), reads, writes)


def _recip(self, out, in_, reads, writes):
    return self.op("dve", lambda e: e.reciprocal(out=out, in_=in_), reads, writes)


def _memset(self, eng, ap, val, writes):
    return self.op(eng, lambda e: e.memset(ap, val), (), writes)


K.mm = _mm
K.tr = _tr
K.act = _act
K.tt = _tt
K.ts = _ts
K.stt = _stt
K.cp = _cp
K.recip = _recip
K.memset = _memset


class Ring:
    def __init__(self, k, name, shape, dt, n):
        self.aps = [k.sb(f"{name}{i}", shape, dt) for i in range(n)]
        self.bufs = [Buf(f"{name}{i}") for i in range(n)]
        self.n = n
        self.i = -1

    def next(self):
        self.i = (self.i + 1) % self.n
        return self.aps[self.i], self.bufs[self.i]

    def cur(self):
        return self.aps[self.i], self.bufs[self.i]


class Psum:
    def __init__(self, k):
        self.banks = [k.ps(f"bank{i}", [128, 512], F32) for i in range(8)]
        self.bufs = [Buf(f"bank{i}", excl=True) for i in range(8)]

    def f32(self, i):
        return self.banks[i]

    def bf16(self, i):
        return self.banks[i].bitcast(BF16)


def _collective(self, kind, in_ap, out_ap, groups, reads=(), writes=()):
    E = self.eng["pool"]
    self._deps(E, reads, writes)
    if not hasattr(self, "cc_sem"):
        self.cc_sem = self.nc.alloc_semaphore("cc_sem")
        self.cc_cnt = 0
        self.cc_tokens = []
    ins = E.e.collective_compute(kind, ALU.bypass, replica_groups=groups, ins=[in_ap.opt()], outs=[out_ap.opt()])
    ins.then_inc(self.cc_sem)
    self.cc_cnt += 1
    tok = (self.cc_sem, self.cc_cnt)
    self.n_instr += 1
    self._mark(tok, reads, writes)
    self.cc_tokens = [tok]
    return tok


K.collective = _collective


S = 16384
D = 1024
NT = S // 128
EPS = 1e-6


def norm_to_T(k, P, xt, xb, gt, gb, ident, identb, n_feat, scr, hnT, hnTb, pbank):
    junk, junkb, ss, ssb, rstd, rstdb, hn, hnb = scr
    k.act(junk, xt, AF.Square, [xb], [junkb, ssb], accum_out=ss)
    k.act(rstd, ss, AF.Sqrt, [ssb], [rstdb], scale=1.0 / n_feat, bias=EPS)
    k.recip(rstd, rstd, [rstdb], [rstdb])
    k.stt(hn, xt, rstd, gt, ALU.mult, ALU.mult, [xb, rstdb, gb], [hnb])
    nch = n_feat // 128
    pT = P.bf16(pbank)
    for c in range(nch):
        k.tr(pT[:, c * 128:(c + 1) * 128], hn[:, c * 128:(c + 1) * 128], ident, [hnb, identb], [P.bufs[pbank]],
             inc=(c == nch - 1))
    k.cp("act", hnT, pT[:, 0:nch * 128].rearrange("p (c t) -> p c t", c=nch), [P.bufs[pbank]], [hnTb])


def make_ident(k):
    io = k.sb("io_id", [128, 128], F32)
    iob = Buf("io")
    idf = k.sb("identf", [128, 128], F32)
    ident = k.sb("ident", [128, 128], BF16)
    identb = Buf("ident")
    k.op("pool", lambda e: e.iota(io, pattern=[[1, 128]], base=0, channel_multiplier=-1,
                                  allow_small_or_imprecise_dtypes=True), (), [iob])
    k.ts("dve", idf, io, 0.0, None, ALU.is_equal, None, [iob], [iob])
    k.cp("dve", ident, idf, [iob], [identb])
    return ident, identb, io, iob


def mem_prep(k, P, nc, mem, mng, wkv, nheads, ident, identb, scr, gt, gb):
    mkT = k.sb("mkT", [128, nheads, 2, 256], BF16)
    mkTb = Buf("mkT")
    mva = k.sb("mva", [128, nheads, 2, 258], BF16)
    mvab = Buf("mva")
    mt_x = k.sb("mem_x", [128, D], F32)
    mt_xb = Buf("mem_x")
    mg = k.sb("mem_g", [128, D], F32)
    mgb = Buf("mem_g")
    mnT = k.sb("mnT", [128, 8, 128], BF16)
    mnTb = Buf("mnT")
    wk = k.sb("wkv_sb", [128, 8, 512], BF16)
    wkb = Buf("wkv_sb")
    mkt = k.sb("mk_tok", [128, 256], BF16)
    mktb = Buf("mk_tok")
    k.dma("sp", mg, mng.partition_broadcast(128), (), [mgb])
    k.memset("dve", mva, 1.0, [mvab])
    for hd in range(nheads):
        k.dma("pool", wk, wkv[:, hd * 512:(hd + 1) * 512].rearrange("(c p) n -> p c n", p=128), (), [wkb])
        for mt in range(2):
            k.dma("sp", mt_x, mem[mt * 128:(mt + 1) * 128, :], (), [mt_xb])
            norm_to_T(k, P, mt_x, mt_xb, mg, mgb, ident, identb, D, scr, mnT, mnTb, 0)
            pp = P.f32(1)
            for c in range(8):
                k.mm(pp, mnT[:, c, :], wk[:, c, :], c == 0, c == 7, [mnTb, wkb], [P.bufs[1]])
            k.cp("act", mkt, pp[:, 0:256], [P.bufs[1]], [mktb])
            k.cp("dve", mva[:, hd, mt, 0:256], pp[:, 256:512], [P.bufs[1]], [mvab])
            pT = P.bf16(0)
            for dc in range(2):
                k.tr(pT[:, dc * 128:(dc + 1) * 128], mkt[:, dc * 128:(dc + 1) * 128], ident, [mktb, identb],
                     [P.bufs[0]], inc=(dc == 1))
            k.cp("act", mkT[:, hd, :, mt * 128:(mt + 1) * 128],
                 pT[:, 0:256].rearrange("p (c t) -> p c t", c=2), [P.bufs[0]], [mkTb])
    return mkT, mkTb, mva, mvab


def mem_attn(k, P, qm_src, qm_srcb, sgm, sgmb, mkT_h, mkTb, mva_h, mvab, ident, identb, tmp, banks, out, outb):
    qmT, qmTb, E, Eb, rs, rsb = tmp
    bT, bS, bO = banks
    pT = P.bf16(bT)
    for dc in range(2):
        k.tr(pT[:, dc * 128:(dc + 1) * 128], qm_src[:, dc * 128:(dc + 1) * 128], ident, [qm_srcb, identb],
             [P.bufs[bT]], inc=(dc == 1))
    k.cp("act", qmT, pT[:, 0:256].rearrange("p (c t) -> p c t", c=2), [P.bufs[bT]], [qmTb])
    pS = P.f32(bS)
    for mt in range(2):
        for dc in range(2):
            k.mm(pS[:, mt * 128:(mt + 1) * 128], mkT_h[:, dc, mt * 128:(mt + 1) * 128], qmT[:, dc, :],
                 dc == 0, dc == 1, [mkTb, qmTb], [P.bufs[bS]], inc=(mt == 1 and dc == 1))
    k.act(E, pS[:, 0:256].rearrange("p (c t) -> p c t", c=2), AF.Exp, [P.bufs[bS]], [Eb], scale=1.0 / 16.0)
    pO = P.f32(bO)
    for mt in range(2):
        k.mm(pO[:, 0:257], E[:, mt, :], mva_h[:, mt, 0:257], mt == 0, mt == 1, [Eb, mvab], [P.bufs[bO]])
    k.recip(rs, pO[:, 256:257], [P.bufs[bO]], [rsb])
    k.stt(out, pO[:, 0:256], rs, sgm, ALU.mult, ALU.mult, [P.bufs[bO], rsb, sgmb], [outb])


def phase_A(k, P, nc, ident, identb, io, iob, T, n_tiles):
    x, ng, w, mem, mng, wkv, dec, rcs = (T[n] for n in ("xA", "ngA", "wA", "memA", "mngA", "wkvA", "decA", "rcsA"))
    yg, ygm, scrd, sbs = T["yg"], T["ygm"], T["scrA"], T["sbsA"]
    gather_ygm = T["gather_ygm"]
    yg_mb, yg_yb = T["yg_mb"], T["yg_yb"]
    gather_yg = T["gather_yg"]
    scr_b = [Buf(f"scr{i}") for i in range(n_tiles)]
    sbs_b = [Buf(f"sbs{i}") for i in range(n_tiles)]
    stop = 99
    wt = k.sb("wt", [128, 8, 2048], BF16)
    wtb = Buf("wt")
    for c4 in range(4):
        k.dma("pool", wt[:, :, c4 * 512:(c4 + 1) * 512],
              w[:, c4 * 512:(c4 + 1) * 512].rearrange("(c p) n -> p c n", p=128), (), [wtb])
    gt = k.sb("gt", [128, D], F32)
    gb = Buf("gt")
    k.dma("sp", gt, ng.partition_broadcast(128), (), [gb])

    def mkscr(tag, nf):
        return (k.sb("junk" + tag, [128, nf], F32), Buf(), k.sb("ss" + tag, [128, 1], F32), Buf(),
                k.sb("rstd" + tag, [128, 1], F32), Buf(), k.sb("hn" + tag, [128, nf], BF16), Buf())

    scr = mkscr("a", D)
    mkT, mkTb, mva, mvab = mem_prep(k, P, nc, mem, mng, wkv, 1, ident, identb, scr, gt, gb)

    lg = k.sb("lg", [128, 4], F32)
    lgb = Buf("lg")
    k.dma("sp", lg, dec.partition_broadcast(128), (), [lgb])
    k.act(lg, lg, AF.Exp, [lgb], [lgb], scale=-1.0)
    k.act(lg, lg, AF.Ln, [lgb], [lgb], bias=1.0)
    k.ts("dve", lg, lg, -1.0, None, ALU.mult, None, [lgb], [lgb])
    cb = Buf("consts")
    tmpc = k.sb("tmpc", [128, 128], F32)
    tmpm = k.sb("tmpm", [128, 128], F32)
    tmpe = k.sb("tmpe", [128, 128], F32)
    DT = k.sb("DT", [128, 2, 128], F32)
    QDF = k.sb("QDF", [128, 2, 128], F32)
    QDB = k.sb("QDB", [128, 2, 128], F32)
    kd = k.sb("kd", [128, 4], F32)
    cd = k.sb("cd", [128, 4], F32)
    ci = k.sb("ci", [128, 128], F32)
    cbk = k.sb("cbk", [128, 128], F32)
    pi = k.sb("pi", [128, 2], F32)
    for h in range(2):
        k.ts("dve", tmpc, io, 0.0, None, ALU.max, None, [iob], [cb])
        k.act(tmpe, tmpc, AF.Exp, [cb, lgb], [cb], scale=lg[:, h:h + 1])
        k.ts("dve", tmpm, io, 0.0, None, ALU.is_ge, None, [iob], [cb])
        k.tt("dve", DT[:, h, :], tmpe, tmpm, ALU.mult, [cb], [cb])
        k.ts("dve", tmpc, io, -1.0, 0.0, ALU.mult, ALU.max, [iob], [cb])
        k.act(tmpe, tmpc, AF.Exp, [cb, lgb], [cb], scale=lg[:, 2 + h:3 + h])
        k.ts("dve", tmpm, io, 0.0, None, ALU.is_lt, None, [iob], [cb])
        k.tt("dve", tmpe, tmpe, tmpm, ALU.mult, [cb], [cb])
        k.tt("dve", DT[:, h, :], DT[:, h, :], tmpe, ALU.add, [cb], [cb])
    k.op("pool", lambda e: e.iota(ci, pattern=[[1, 128]], base=1, channel_multiplier=0,
                                  allow_small_or_imprecise_dtypes=True), (), [cb])
    k.op("pool", lambda e: e.iota(cbk, pattern=[[-1, 128]], base=128, channel_multiplier=0,
                                  allow_small_or_imprecise_dtypes=True), (), [cb])
    k.op("pool", lambda e: e.iota(pi[:, 0:1], pattern=[[0, 1]], base=127, channel_multiplier=-1,
                                  allow_small_or_imprecise_dtypes=True), (), [cb])
    k.op("pool", lambda e: e.iota(pi[:, 1:2], pattern=[[0, 1]], base=0, channel_multiplier=1,
                                  allow_small_or_imprecise_dtypes=True), (), [cb])
    for h in range(2):
        k.act(QDF[:, h, :], ci, AF.Exp, [cb, lgb], [cb], scale=lg[:, h:h + 1])
        k.act(QDB[:, h, :], cbk, AF.Exp, [cb, lgb], [cb], scale=lg[:, 2 + h:3 + h])
        k.act(kd[:, h:h + 1], pi[:, 0:1], AF.Exp, [cb, lgb], [cb], scale=lg[:, h:h + 1])
        k.act(kd[:, 2 + h:3 + h], pi[:, 1:2], AF.Exp, [cb, lgb], [cb], scale=lg[:, 2 + h:3 + h])
    k.act(cd, lg, AF.Exp, [lgb], [cb], scale=128.0)

    NRING = 8
    rg = dict(
        x=Ring(k, "pxt", [128, D], F32, 3), rc=Ring(k, "prc", [128, 128], F32, NRING),
        hn=Ring(k, "phn", [128, D], BF16, 3), hnT=Ring(k, "phnT", [128, 8, 128], BF16, 2),
        qk=Ring(k, "pqk", [128, 4, 128], F32, 3), qkr=Ring(k, "pqkr", [128, 4, 128], BF16, 4),
        st=Ring(k, "pst", [128, 2048], BF16, 5), qm=Ring(k, "pqm", [128, 256], BF16, 3),
        sgm=Ring(k, "psgm", [128, 256], F32, NRING), qmT=Ring(k, "pqmT", [128, 2, 128], BF16, 3),
        E=Ring(k, "pE", [128, 2, 128], BF16, 3), mgt=Ring(k, "pmgt", [128, 256], BF16, 3),
        mgT=Ring(k, "pmgT", [128, 2, 128], BF16, 3), ss=Ring(k, "pss", [128, 2], F32, 3),
        rs=Ring(k, "prs", [128, 1], F32, 3), t1=Ring(k, "pt1", [128, 4, 64], F32, 2),
        t2=Ring(k, "pt2", [128, 4, 64], F32, 2), junk=Ring(k, "pjunk", [128, D], BF16, 2),
    )
    tl = {}

    def g(i, name):
        d = tl.setdefault(i, {})
        if name not in d:
            d[name] = rg[name].next()
        return d[name]

    def f_load(i):
        xt, xb = g(i, "x")
        rc, rcb = g(i, "rc")
        k.dma("sp", xt, x[i * 128:(i + 1) * 128, :], (), [xb])
        k.dma("sp", rc, rcs[i * 128:(i + 1) * 128, :], (), [rcb])

    def f_norm(i):
        xt, xb = g(i, "x")
        ss, ssb = g(i, "ss")
        hn, hnb = g(i, "hn")
        junk, junkb = g(i, "junk")
        k.act(junk, xt, AF.Square, [xb], [junkb, ssb], accum_out=ss[:, 0:1])
        k.act(ss[:, 1:2], ss[:, 0:1], AF.Sqrt, [ssb], [ssb], scale=1.0 / D, bias=EPS)
        k.recip(ss[:, 1:2], ss[:, 1:2], [ssb], [ssb])
        k.stt(hn, xt, ss[:, 1:2], gt, ALU.mult, ALU.mult, [xb, ssb, gb], [hnb])

    def f_tr(i):
        hn, hnb = g(i, "hn")
        pT = P.bf16(0)
        for c in range(8):
            k.tr(pT[:, c * 128:(c + 1) * 128], hn[:, c * 128:(c + 1) * 128], ident, [hnb, identb], [P.bufs[0]],
                 inc=(c == 7))

    def f_proj(i):
        hnT, hnTb = g(i, "hnT")
        k.cp("act", hnT, P.bf16(0)[:, 0:1024].rearrange("p (c t) -> p c t", c=8), [P.bufs[0]], [hnTb])
        for gi in range(4):
            pp = P.f32(1 + gi)
            for c in range(8):
                k.mm(pp, hnT[:, c, :], wt[:, c, gi * 512:(gi + 1) * 512], c == 0, c == 7, [hnTb, wtb],
                     [P.bufs[1 + gi]])

    def f_evac(i):
        st, stb = g(i, "st")
        qk, qkb = g(i, "qk")
        qm, qmb = g(i, "qm")
        sgm, sgmb = g(i, "sgm")
        p0 = P.f32(1)
        k.cp("act", qk[:, 0:2, :], p0[:, 0:256].rearrange("p (u d) -> p u d", u=2), [P.bufs[1]], [qkb])
        k.act(qk[:, 2:4, :], p0[:, 256:512].rearrange("p (u d) -> p u d", u=2), AF.Copy, [P.bufs[1]], [qkb],
              scale=128.0 ** -0.5)
        k.cp("act", st[:, 1024:1536], P.f32(2), [P.bufs[2]], [stb])
        k.act(st[:, 1536:2048], P.f32(3), AF.Silu, [P.bufs[3]], [stb])
        p3 = P.f32(4)
        k.cp("dve", qm, p3[:, 0:256], [P.bufs[4]], [qmb])
        k.act(sgm, p3[:, 256:512], AF.Silu, [P.bufs[4]], [sgmb])

    def f_rot(i):
        qk, qkb = g(i, "qk")
        qkr, qkrb = g(i, "qkr")
        rc, rcb = g(i, "rc")
        t1, t1b = rg["t1"].next()
        t2, t2b = rg["t2"].next()
        x1 = qk[:, :, 0:64]
        x2 = qk[:, :, 64:128]
        cosb = rc[:, 0:64].unsqueeze(1).to_broadcast([128, 4, 64])
        sinb = rc[:, 64:128].unsqueeze(1).to_broadcast([128, 4, 64])
        k.tt("dve", t1, x1, cosb, ALU.mult, [qkb, rcb], [t1b])
        k.tt("dve", t2, x2, sinb, ALU.mult, [qkb, rcb], [t2b])
        k.tt("dve", qkr[:, :, 0:64], t1, t2, ALU.subtract, [t1b, t2b], [qkrb])
        k.tt("dve", t1, x1, sinb, ALU.mult, [qkb, rcb], [t1b])
        k.tt("dve", t2, x2, cosb, ALU.mult, [qkb, rcb], [t2b])
        k.tt("dve", qkr[:, :, 64:128], t1, t2, ALU.add, [t1b, t2b], [qkrb])
        qm, qmb = g(i, "qm")
        pT = P.bf16(6)
        for dc in range(2):
            k.tr(pT[:, 512 + dc * 128:512 + (dc + 1) * 128], qm[:, dc * 128:(dc + 1) * 128], ident, [qmb, identb],
                 [P.bufs[6]], inc=(dc == 1))

    def f_kfb(i):
        qkr, qkrb = g(i, "qkr")
        st, stb = g(i, "st")
        for h in range(2):
            k.ts("pool", st[:, 512 + h * 128:512 + (h + 1) * 128], qkr[:, 2 + h, :], kd[:, h:h + 1], None,
                 ALU.mult, None, [qkrb, cb], [stb])
            k.ts("pool", st[:, 768 + h * 128:768 + (h + 1) * 128], qkr[:, 2 + h, :], kd[:, 2 + h:3 + h], None,
                 ALU.mult, None, [qkrb, cb], [stb])
        qmT, qmTb = g(i, "qmT")
        k.cp("act", qmT, P.bf16(6)[:, 512:768].rearrange("p (c t) -> p c t", c=2), [P.bufs[6]], [qmTb])
        pT = P.bf16(6)
        for u in range(4):
            k.tr(pT[:, u * 128:(u + 1) * 128], qkr[:, u, :], ident, [qkrb, identb], [P.bufs[6]], inc=(u == 3))

    def f_store(i):
        st, stb = g(i, "st")
        k.cp("act", st[:, 0:512], P.bf16(6)[:, 0:512], [P.bufs[6]], [stb])
        k.dma("sp", scrd[i], st, [stb], [scr_b[i]])
        qmT, qmTb = g(i, "qmT")
        pS = P.f32(7)
        for mt in range(2):
            for dc in range(2):
                k.mm(pS[:, mt * 128:(mt + 1) * 128], mkT[:, 0][:, dc, mt * 128:(mt + 1) * 128], qmT[:, dc, :],
                     dc == 0, dc == 1, [mkTb, qmTb], [P.bufs[7]], inc=(mt == 1 and dc == 1))

    def f_exp(i):
        E, Eb = g(i, "E")
        k.act(E, P.f32(7)[:, 0:256].rearrange("p (c t) -> p c t", c=2), AF.Exp, [P.bufs[7]], [Eb], scale=1.0 / 16.0)

    def f_av(i):
        E, Eb = g(i, "E")
        pO = P.f32(5)
        for mt in range(2):
            k.mm(pO[:, 0:257], E[:, mt, :], mva[:, 0][:, mt, 0:257], mt == 0, mt == 1, [Eb, mvab], [P.bufs[5]])

    def f_mnorm(i):
        rs, rsb = g(i, "rs")
        sgm, sgmb = g(i, "sgm")
        mgt, mgtb = g(i, "mgt")
        pO = P.f32(5)
        k.recip(rs, pO[:, 256:257], [P.bufs[5]], [rsb])
        k.stt(mgt, pO[:, 0:256], rs, sgm, ALU.mult, ALU.mult, [P.bufs[5], rsb, sgmb], [mgtb])

    def f_mtr(i):
        mgt, mgtb = g(i, "mgt")
        pT = P.bf16(7)
        for ec in range(2):
            k.tr(pT[:, 512 + ec * 128:512 + (ec + 1) * 128], mgt[:, ec * 128:(ec + 1) * 128], ident,
                 [mgtb, identb], [P.bufs[7]], inc=(ec == 1))

    def f_mout(i):
        mgT, mgTb = g(i, "mgT")
        k.cp("act", mgT, P.bf16(7)[:, 512:768].rearrange("p (c t) -> p c t", c=2), [P.bufs[7]], [mgTb])
        k.dma("sp", ygm[i], mgT.rearrange("p c t -> p (c t)"), [mgTb], [yg_mb[i]])
        gather_ygm(i)
        tl.pop(i, None)

    stages = [(0, f_load), (1, f_norm), (2, f_tr), (3, f_proj), (4, f_evac), (5, f_rot), (6, f_kfb), (7, f_store),
              (8, f_exp), (9, f_av), (10, f_mnorm), (11, f_mtr), (12, f_mout)]
    stages = stages[::-1]
    maxoff = max(o for o, _ in stages)
    for t in range(n_tiles + maxoff):
        for off, fn in stages:
            i = t - off
            if 0 <= i < n_tiles:
                fn(i)

    gather_ygm(0, final=True)

    Sb = k.sb("Sb", [128, 2, 256], F32)
    Sbb = Buf("Sb")
    Sf = k.sb("Sf", [128, 2, 256], F32)
    Sfb = Buf("Sf")
    Sf16 = k.sb("Sf16", [128, 2, 256], BF16)
    Sf16b = Buf("Sf16")
    k.memset("dve", Sb, 0.0, [Sbb])
    k.memset("dve", Sf, 0.0, [Sfb])
    k.memset("dve", Sf16, 0.0, [Sf16b])
    kvr = Ring(k, "kvr", [128, 1024], BF16, 3)
    sbst = Ring(k, "sbst", [128, 512], BF16, 3)
    ldB = {}

    def loadBk(n):
        if n >= 0:
            kv, kvb = kvr.next()
            k.dma("sp", kv, scrd[n][:, 512:1536], [scr_b[n]], [kvb])
            ldB[n] = (kv, kvb)

    loadBk(n_tiles - 1)
    loadBk(n_tiles - 2)
    for n in range(n_tiles - 1, -1, -1):
        loadBk(n - 2)
        kv, kvb = ldB.pop(n)
        so, sob = sbst.next()
        k.cp("act", so, Sb.rearrange("p h e -> p (h e)"), [Sbb], [sob])
        k.dma("sp", sbs[n], so, [sob], [sbs_b[n]])
        if n == 0:
            break
        for h in range(2):
            bk = 1 + h
            k.mm(P.f32(bk)[:, 0:256], kv[:, 256 + h * 128:256 + (h + 1) * 128], kv[:, 512 + h * 256:512 + (h + 1) * 256],
                 True, True, [kvb], [P.bufs[bk]])
            k.stt(Sb[:, h, :], Sb[:, h, :], cd[:, 2 + h:3 + h], P.f32(bk)[:, 0:256], ALU.mult, ALU.add,
                  [Sbb, cb, P.bufs[bk]], [Sbb])

    chr_ = Ring(k, "chk", [128, 2048], BF16, 3)
    sbr = Ring(k, "sbin", [128, 512], BF16, 3)
    AT = Ring(k, "AT", [128, 128], BF16, 2)
    qsf = Ring(k, "qsf", [128, 128], BF16, 2)
    qsb = Ring(k, "qsb", [128, 128], BF16, 2)
    ygr = Ring(k, "ygt", [128, 256], BF16, 2)
    ygT = Ring(k, "ygT", [128, 4, 128], BF16, 2)
    ss2 = k.sb("ss2", [128, 1], F32)
    ss2b = Buf("ss2")
    junk2 = k.sb("junk2", [128, 256], F32)
    junk2b = Buf("junk2")
    ldF = {}

    def loadF(n):
        if n < n_tiles:
            ch, chb = chr_.next()
            sbn, sbnb = sbr.next()
            k.dma("sp", ch, scrd[n], [scr_b[n]], [chb])
            k.dma("sp", sbn, sbs[n], [sbs_b[n]], [sbnb])
            ldF[n] = (ch, chb, sbn, sbnb)

    loadF(0)
    loadF(1)
    for n in range(n_tiles):
        loadF(n + 2)
        ch, chb, sbn, sbnb = ldF.pop(n)
        yT, yTb = ygT.next()
        for h in range(2):
            qT = ch[:, h * 128:(h + 1) * 128]
            kT = ch[:, 256 + h * 128:256 + (h + 1) * 128]
            kf = ch[:, 512 + h * 128:512 + (h + 1) * 128]
            v = ch[:, 1024 + h * 256:1024 + (h + 1) * 256]
            sg = ch[:, 1536 + h * 256:1536 + (h + 1) * 256]
            bS, bY, bKV = 1 + h, 3 + h, 5 + h
            a, ab = AT.next()
            f, fb = qsf.next()
            bq, bqb = qsb.next()
            k.mm(P.f32(bS)[:, 0:128], kT, qT, True, True, [chb], [P.bufs[bS]])
            k.tt("dve", a, P.f32(bS)[:, 0:128], DT[:, h, :], ALU.mult, [P.bufs[bS], cb], [ab])
            k.tt("pool", f, qT, QDF[:, h, :], ALU.mult, [chb, cb], [fb])
            k.tt("pool", bq, qT, QDB[:, h, :], ALU.mult, [chb, cb], [bqb])
            pY = P.f32(bY)[:, 0:256]
            k.mm(pY, a, v, True, False, [ab, chb], [P.bufs[bY]])
            k.mm(pY, f, Sf16[:, h, :], False, False, [fb, Sf16b], [P.bufs[bY]])
            k.mm(pY, bq, sbn[:, h * 256:(h + 1) * 256], False, True, [bqb, sbnb], [P.bufs[bY]])
            if n < n_tiles - 1:
                pKV = P.f32(bKV)[:, 0:256]
                k.mm(pKV, kf, v, True, True, [chb], [P.bufs[bKV]])
                k.stt(Sf[:, h, :], Sf[:, h, :], cd[:, h:h + 1], pKV, ALU.mult, ALU.add,
                      [Sfb, cb, P.bufs[bKV]], [Sfb])
                k.cp("act", Sf16[:, h, :], Sf[:, h, :], [Sfb], [Sf16b])
            k.act(junk2, pY, AF.Square, [P.bufs[bY]], [junk2b, ss2b], accum_out=ss2)
            k.act(ss2, ss2, AF.Sqrt, [ss2b], [ss2b], scale=1.0 / 256, bias=EPS)
            k.recip(ss2, ss2, [ss2b], [ss2b])
            yg_t, ygb = ygr.next()
            k.stt(yg_t, pY, ss2, sg, ALU.mult, ALU.mult, [P.bufs[bY], ss2b, chb], [ygb])
            pT = P.bf16(7)
            for ec in range(2):
                k.tr(pT[:, (h * 2 + ec) * 128:(h * 2 + ec + 1) * 128], yg_t[:, ec * 128:(ec + 1) * 128], ident,
                     [ygb, identb], [P.bufs[7]], inc=(ec == 1))
            k.cp("act", yT[:, h * 2:(h + 1) * 2, :],
                 pT[:, h * 256:(h + 1) * 256].rearrange("p (c t) -> p c t", c=2), [P.bufs[7]], [yTb])
        k.dma("sp", yg[n], yT.rearrange("p c t -> p (c t)"), [yTb], [yg_yb[n]])
        gather_yg(n)


import math

LAMBDA_INIT = 0.8 - 0.6 * math.exp(-0.3 * 1)
def sumsq_gather(k, P, nc, ssq, ssqb, dr_in, dr_in_b, dr_all, dr_all_b, groups, n_feat, tag):
    nt = ssq.shape[1]
    k.dma("sp", dr_in, ssq, [ssqb], [dr_in_b])
    k.collective("AllGather", dr_in, dr_all, groups, [dr_in_b], [dr_all_b])
    g4 = k.sb("ssq4" + tag, [128, 4, nt], F32)
    g4b = Buf("ssq4" + tag)
    k.dma("sp", g4, dr_all.rearrange("(r p) n -> p r n", p=128), [dr_all_b], [g4b])
    rstd = k.sb("rstd" + tag, [128, nt], F32)
    rstdb = Buf("rstd" + tag)
    k.tt("dve", rstd, g4[:, 0, :], g4[:, 1, :], ALU.add, [g4b], [rstdb])
    k.tt("dve", rstd, rstd, g4[:, 2, :], ALU.add, [g4b, rstdb], [rstdb])
    k.tt("dve", rstd, rstd, g4[:, 3, :], ALU.add, [g4b, rstdb], [rstdb])
    k.act(rstd, rstd, AF.Sqrt, [rstdb], [rstdb], scale=1.0 / n_feat, bias=EPS)
    k.recip(rstd, rstd, [rstdb], [rstdb])
    return rstd, rstdb


def colproj(k, P, src, srcb, srcm, srcmb, wot, wotb, bank):
    pp = P.f32(bank)[:, 0:256]
    n = 0
    for g in range(4):
        for c in (4, 5):
            k.mm(pp, srcm[:, g, (c - 4) * 128:(c - 3) * 128], wot[:, g * 6 + c, :], n == 0, n == 23,
                 [srcmb, wotb], [P.bufs[bank]])
            n += 1
    for g in range(4):
        for c in range(4):
            k.mm(pp, src[:, g, c * 128:(c + 1) * 128], wot[:, g * 6 + c, :], n == 0, n == 23, [srcb, wotb],
                 [P.bufs[bank]])
            n += 1
    return pp


def phase_B(k, P, nc, ident, identb, T, n_tiles):
    ygall, ygall_cb = T["ygall"], T["ygall_cb"]
    ygmall, ygmall_cb = T["ygmall"], T["ygmall_cb"]
    xc, wo, ngc = T["xcB"], T["woB"], T["ngcB"]
    h1s, h1s_b = T["h1s"], T["h1s_b"]
    h1gt, h1gt_tb = T["h1gt"], T["h1gt_tb"]
    gather_h1 = T["gather_h1"]
    wot = k.sb("wotB", [128, 24, 256], BF16)
    wotb = Buf("wotB")
    for c in range(2):
        k.dma("pool", wot[:, c * 12:(c + 1) * 12, :],
              wo[c * 1536:(c + 1) * 1536, :].rearrange("(c p) n -> p c n", p=128), (), [wotb])
    gc = k.sb("gcB", [128, 256], F32)
    gcb = Buf("gcB")
    k.dma("sp", gc, ngc.partition_broadcast(128), (), [gcb])
    ssq = k.sb("ssqB", [128, n_tiles], F32)
    ssqb = Buf("ssqB")
    k.memset("dve", ssq, 0.0, [ssqb])
    ygr = Ring(k, "ygtB", [128, 4, 512], BF16, 3)
    ygmr = Ring(k, "ygmB", [128, 4, 256], BF16, 3)
    xr = Ring(k, "xcB", [128, 256], F32, 3)
    hr = Ring(k, "h1sB", [128, 256], F32, 2)
    hgr = Ring(k, "h1gB", [128, 256], BF16, 2)
    hTr = Ring(k, "h1gTB", [128, 2, 128], BF16, 2)
    junk = k.sb("junkB", [128, 256], BF16)
    junkb = Buf("junkB")
    yv = ygall.rearrange("(c g ii) p n -> c ii p g n", g=4, ii=4)
    ymv = ygmall.rearrange("(c g ii) p n -> c ii p g n", g=4, ii=4)
    ld = {}

    def load(i):
        if i < n_tiles:
            yt, yb = ygr.next()
            xt, xb = xr.next()
            ym, ymb = ygmr.next()
            k.dma("sp", ym, ymv[i // 4][i % 4], [ygmall_cb[i // 4]], [ymb])
            k.dma("sp", yt, yv[i // 4][i % 4], [ygall_cb[i // 4]], [yb])
            k.dma("sp", xt, xc[i * 128:(i + 1) * 128, :], (), [xb])
            ld[i] = (yt, yb, xt, xb, ym, ymb)

    pps = {}

    def proj(i):
        if i < n_tiles:
            yt, yb, xt, xb, ym, ymb = ld[i]
            bank = 1 + (i % 2)
            pps[i] = (colproj(k, P, yt, yb, ym, ymb, wot, wotb, bank), bank)

    load(0)
    load(1)
    proj(0)
    for i in range(n_tiles):
        load(i + 2)
        proj(i + 1)
        yt, yb, xt, xb, ym, ymb = ld.pop(i)
        pp, bank = pps.pop(i)
        h, hb = hr.next()
        k.tt("dve", h, pp, xt, ALU.add, [P.bufs[bank], xb], [hb])
        k.dma("sp", h1s[i], h, [hb], [h1s_b[i]])
        k.act(junk, h, AF.Square, [hb], [junkb, ssqb], accum_out=ssq[:, i:i + 1])
        hg, hgb = hgr.next()
        k.tt("dve", hg, h, gc, ALU.mult, [hb, gcb], [hgb])
        hT, hTb = hTr.next()
        pT = P.bf16(3 + (i % 2))
        for c in range(2):
            k.tr(pT[:, c * 128:(c + 1) * 128], hg[:, c * 128:(c + 1) * 128], ident, [hgb, identb],
                 [P.bufs[3 + (i % 2)]], inc=(c == 1))
        k.cp("act", hT, pT[:, 0:256].rearrange("p (c t) -> p c t", c=2), [P.bufs[3 + (i % 2)]], [hTb])
        k.dma("sp", h1gt[i], hT.rearrange("p c t -> p (c t)"), [hTb], [h1gt_tb[i]])
        gather_h1(i)
    return ssq, ssqb


def phase_B2(k, P, nc, ident, identb, T, n_tiles, rstd, rstdb):
    hall, hall_cb = T["h1gtall"], T["h1gtall_cb"]
    w, mem, mng, wkv, dcs = T["wB"], T["memB"], T["mngB"], T["wkvB"], T["dcsB"]
    scr1, scr1_b = T["scr1"], T["scr1_b"]
    ogm, ogt_mb = T["ogm"], T["ogt_mb"]
    gather_ogm = T["gather_ogm"]
    wt = k.sb("wtB2", [128, 8, 2560], BF16)
    wtb = Buf("wtB2")
    for c in range(5):
        k.dma("pool", wt[:, :, c * 512:(c + 1) * 512],
              w[:, c * 512:(c + 1) * 512].rearrange("(c p) n -> p c n", p=128), (), [wtb])
    gt = k.sb("gtB2", [128, D], F32)
    gb = Buf("gtB2")
    scr = (k.sb("junkB2", [128, D], BF16), Buf(), k.sb("ssB2", [128, 1], F32), Buf(),
           k.sb("rstdB2", [128, 1], F32), Buf(), k.sb("hnB2", [128, D], BF16), Buf())
    mkT, mkTb, mva, mvab = mem_prep(k, P, nc, mem, mng, wkv, 1, ident, identb, scr, gt, gb)
    rg = dict(
        hn=Ring(k, "qhn", [128, 4, 256], BF16, 3), rp=Ring(k, "qrp", [128, 32], F32, 5),
        rot=Ring(k, "qrot", [128, 8, 32], F32, 3), qkr=Ring(k, "qqkr", [128, 8, 128], BF16, 4),
        st=Ring(k, "qst", [128, 2048], BF16, 5), qm=Ring(k, "qqm", [128, 256], BF16, 3),
        sgm=Ring(k, "qsgm", [128, 256], F32, 8), qmT=Ring(k, "qqmT", [128, 2, 128], BF16, 3),
        E=Ring(k, "qE", [128, 2, 128], BF16, 3), mgt=Ring(k, "qmgt", [128, 256], BF16, 3),
        mgT=Ring(k, "qmgT", [128, 2, 128], BF16, 3), rs=Ring(k, "qrs", [128, 1], F32, 3),
        t1=Ring(k, "qt1", [128, 8, 16], F32, 2), t2=Ring(k, "qt2", [128, 8, 16], F32, 2),
    )
    hv = hall.rearrange("(c g ii) p n -> c ii p g n", g=4, ii=4)
    tl = {}

    def g(i, name):
        d = tl.setdefault(i, {})
        if name not in d:
            d[name] = rg[name].next()
        return d[name]

    def f_load(i):
        hn, hnb = g(i, "hn")
        rp, rpb = g(i, "rp")
        k.dma("sp", hn, hv[i // 4][i % 4], [hall_cb[i // 4]], [hnb])
        k.dma("sp", rp, dcs[i * 128:(i + 1) * 128, :], (), [rpb])

    def f_proj(i):
        hn, hnb = g(i, "hn")
        for gi in range(5):
            pp = P.f32(gi)
            for c in range(8):
                k.mm(pp, hn[:, c // 2, (c % 2) * 128:(c % 2 + 1) * 128], wt[:, c, gi * 512:(gi + 1) * 512],
                     c == 0, c == 7, [hnb, wtb], [P.bufs[gi]])

    def f_evac(i):
        r = rstd[:, i:i + 1]
        st, stb = g(i, "st")
        rot, rotb = g(i, "rot")
        qkr, qkrb = g(i, "qkr")
        qm, qmb = g(i, "qm")
        sgm, sgmb = g(i, "sgm")
        for gi in range(2):
            p3 = P.f32(gi).rearrange("p (u d) -> p u d", u=4)
            k.act(rot[:, gi * 4:(gi + 1) * 4, :], p3[:, :, 0:32], AF.Copy, [P.bufs[gi], rstdb], [rotb], scale=r)
            k.act(qkr[:, gi * 4:(gi + 1) * 4, :], p3, AF.Copy, [P.bufs[gi], rstdb], [qkrb], scale=r)
        k.act(st[:, 1024:1536], P.f32(2), AF.Copy, [P.bufs[2], rstdb], [stb], scale=r)
        k.act(st[:, 1536:2048], P.f32(3), AF.Silu, [P.bufs[3], rstdb], [stb], scale=r)
        p4 = P.f32(4)
        k.ts("dve", qm, p4[:, 0:256], r, None, ALU.mult, None, [P.bufs[4], rstdb], [qmb])
        k.act(sgm, p4[:, 256:512], AF.Silu, [P.bufs[4], rstdb], [sgmb], scale=r)

    def f_rot(i):
        rot, rotb = g(i, "rot")
        qkr, qkrb = g(i, "qkr")
        rp, rpb = g(i, "rp")
        t1, t1b = rg["t1"].next()
        t2, t2b = rg["t2"].next()
        x1 = rot[:, :, 0:16]
        x2 = rot[:, :, 16:32]
        cosb = rp[:, 0:16].unsqueeze(1).to_broadcast([128, 8, 16])
        sinb = rp[:, 16:32].unsqueeze(1).to_broadcast([128, 8, 16])
        k.tt("dve", t1, x1, cosb, ALU.mult, [rotb, rpb], [t1b])
        k.tt("dve", t2, x2, sinb, ALU.mult, [rotb, rpb], [t2b])
        k.tt("dve", qkr[:, :, 0:16], t1, t2, ALU.subtract, [t1b, t2b], [qkrb])
        k.tt("dve", t1, x1, sinb, ALU.mult, [rotb, rpb], [t1b])
        k.tt("dve", t2, x2, cosb, ALU.mult, [rotb, rpb], [t2b])
        k.tt("dve", qkr[:, :, 16:32], t1, t2, ALU.add, [t1b, t2b], [qkrb])
        qm, qmb = g(i, "qm")
        pT = P.bf16(6)
        for dc in range(2):
            k.tr(pT[:, 768 + dc * 128:768 + (dc + 1) * 128], qm[:, dc * 128:(dc + 1) * 128], ident, [qmb, identb],
                 [P.bufs[6]], inc=(dc == 1))

    def f_qktr(i):
        qkr, qkrb = g(i, "qkr")
        qmT, qmTb = g(i, "qmT")
        k.cp("act", qmT, P.bf16(6)[:, 768:1024].rearrange("p (c t) -> p c t", c=2), [P.bufs[6]], [qmTb])
        pT = P.bf16(5)
        for u in range(8):
            k.tr(pT[:, u * 128:(u + 1) * 128], qkr[:, u, :], ident, [qkrb, identb], [P.bufs[5]], inc=(u == 7))

    def f_store(i):
        st, stb = g(i, "st")
        k.cp("dve", st[:, 0:1024], P.bf16(5)[:, 0:1024], [P.bufs[5]], [stb])
        k.dma("sp", scr1[i], st, [stb], [scr1_b[i]])
        qmT, qmTb = g(i, "qmT")
        pS = P.f32(7)
        for mt in range(2):
            for dc in range(2):
                k.mm(pS[:, mt * 128:(mt + 1) * 128], mkT[:, 0][:, dc, mt * 128:(mt + 1) * 128], qmT[:, dc, :],
                     dc == 0, dc == 1, [mkTb, qmTb], [P.bufs[7]], inc=(mt == 1 and dc == 1))

    def f_exp(i):
        E, Eb = g(i, "E")
        k.act(E, P.f32(7)[:, 0:256].rearrange("p (c t) -> p c t", c=2), AF.Exp, [P.bufs[7]], [Eb], scale=1.0 / 16.0)

    def f_av(i):
        E, Eb = g(i, "E")
        pO = P.f32(6)
        for mt in range(2):
            k.mm(pO[:, 0:257], E[:, mt, :], mva[:, 0][:, mt, 0:257], mt == 0, mt == 1, [Eb, mvab], [P.bufs[6]])

    def f_mnorm(i):
        rs, rsb = g(i, "rs")
        sgm, sgmb = g(i, "sgm")
        mgt, mgtb = g(i, "mgt")
        pO = P.f32(6)
        k.recip(rs, pO[:, 256:257], [P.bufs[6]], [rsb])
        k.stt(mgt, pO[:, 0:256], rs, sgm, ALU.mult, ALU.mult, [P.bufs[6], rsb, sgmb], [mgtb])

    def f_mtr(i):
        mgt, mgtb = g(i, "mgt")
        pT = P.bf16(7)
        for ec in range(2):
            k.tr(pT[:, 512 + ec * 128:512 + (ec + 1) * 128], mgt[:, ec * 128:(ec + 1) * 128], ident,
                 [mgtb, identb], [P.bufs[7]], inc=(ec == 1))

    def f_mout(i):
        mgT, mgTb = g(i, "mgT")
        k.cp("act", mgT, P.bf16(7)[:, 512:768].rearrange("p (c t) -> p c t", c=2), [P.bufs[7]], [mgTb])
        k.dma("sp", ogm[i], mgT.rearrange("p c t -> p (c t)"), [mgTb], [ogt_mb[i]])
        gather_ogm(i)
        tl.pop(i, None)

    stages = [(0, f_load), (1, f_proj), (2, f_evac), (3, f_rot), (4, f_qktr), (5, f_store), (6, f_exp), (7, f_av),
              (8, f_mnorm), (9, f_mtr), (10, f_mout)][::-1]
    maxoff = 10
    for t in range(n_tiles + maxoff):
        for off, fn in stages:
            i = t - off
            if 0 <= i < n_tiles:
                fn(i)


def phase_C(k, P, nc, ident, identb, T, n_tiles):
    scr1, scr1_b = T["scr1"], T["scr1_b"]
    ogt, ogt_yb = T["ogt"], T["ogt_yb"]
    gather_og = T["gather_og"]
    lam4, slg = T["lamC"], T["slgC"]
    n_all = n_tiles
    nq = n_tiles // 4
    lv = k.sb("lv", [128, 4, 128], F32)
    lvb = Buf("lv")
    k.dma("sp", lv, lam4.rearrange("a d -> (a d)").partition_broadcast(128).rearrange("p (a d) -> p a d", a=4), (), [lvb])
    lp = k.sb("lp", [128, 2, 128], F32)
    k.tt("dve", lp[:, 0, :], lv[:, 0, :], lv[:, 1, :], ALU.mult, [lvb], [lvb])
    k.tt("dve", lp[:, 1, :], lv[:, 2, :], lv[:, 3, :], ALU.mult, [lvb], [lvb])
    ls = k.sb("ls", [128, 2], F32)
    k.op("dve", lambda e: e.tensor_reduce(out=ls, in_=lp, axis=AX.X, op=ALU.add), [lvb], [lvb])
    k.act(ls, ls, AF.Exp, [lvb], [lvb])
    nlam = k.sb("nlam", [128, 1], F32)
    nlamb = Buf("nlam")
    k.tt("dve", nlam, ls[:, 1:2], ls[:, 0:1], ALU.subtract, [lvb], [nlamb])
    k.ts("dve", nlam, nlam, -LAMBDA_INIT, None, ALU.add, None, [nlamb], [nlamb])
    sgain = k.sb("sgain", [128, 256], F32)
    sgainb = Buf("sgain")
    k.dma("sp", sgain, slg.partition_broadcast(128), (), [sgainb])
    k.ts("dve", sgain, sgain, 1.0 - LAMBDA_INIT, None, ALU.mult, None, [sgainb], [sgainb])

    kc = [k.sb(f"kc{c}", [128, n_all, 128], BF16) for c in range(2)]
    kcb = [Buf("kc0"), Buf("kc1")]
    va = k.sb("va", [128, n_all, 258], BF16)
    vab = Buf("va")
    k.memset("dve", va, 1.0, [vab])
    qtr = Ring(k, "qtile", [128, 2, 4, 128], BF16, 2)
    sgr = Ring(k, "sgt", [128, 4, 256], BF16, 2)
    er = Ring(k, "E", [128, 512], BF16, 3)
    on = k.sb("on", [128, 2, 4, 256], F32)
    onb = Buf("on")
    rs = k.sb("rs", [128, 4], F32)
    rsb = Buf("rs")
    ssq = k.sb("ssq", [128, 4], F32)
    ssqb = Buf("ssq")
    junk = k.sb("junkc", [128, 256], BF16)
    junkb = Buf("junkc")
    ogtile = k.sb("ogtile", [128, 4, 256], BF16)
    ogtileb = Buf("ogtile")
    ogTr = Ring(k, "ogT", [128, 4, 2, 128], BF16, 2)
    scale = 128.0 ** -0.5
    sbank = 0
    allscr = list(scr1_b)
    for hh in range(2):
        for c in range(2):
            u = hh * 2 + c
            for i0 in range(0, n_all, 16):
                i1 = min(n_all, i0 + 16)
                k.dma("sp", kc[c][:, i0:i1, :],
                      scr1[i0:i1, :, 512 + u * 128:512 + (u + 1) * 128].rearrange("i p t -> p i t"),
                      allscr[i0:i1], [kcb[c]])
        for i0 in range(0, n_all, 16):
            i1 = min(n_all, i0 + 16)
            k.dma("sp", va[:, i0:i1, 0:256],
                  scr1[i0:i1, :, 1024 + hh * 256:1024 + (hh + 1) * 256].rearrange("i p e -> p i e"),
                  allscr[i0:i1], [vab])
        ld = {}

        def load(q):
            if q < nq:
                qt, qtb = qtr.next()
                sgt, sgtb = sgr.next()
                for c in range(2):
                    u = hh * 2 + c
                    k.dma("sp", qt[:, c], scr1[q * 4:(q + 1) * 4, :, u * 128:(u + 1) * 128].rearrange("i p t -> p i t"),
                          allscr[q * 4:(q + 1) * 4], [qtb])
                k.dma("sp", sgt, scr1[q * 4:(q + 1) * 4, :, 1536 + hh * 256:1536 + (hh + 1) * 256].rearrange("i p e -> p i e"),
                      allscr[q * 4:(q + 1) * 4], [sgtb])
                ld[q] = (qt, qtb, sgt, sgtb)

        load(0)
        for q in range(nq):
            load(q + 1)
            qt, qtb, sgt, sgtb = ld.pop(q)
            items = [(c, kt_i) for c in range(2) for kt_i in range(n_all)]
            pend = {}

            def emit_S(c, kt_i):
                nonlocal sbank
                sbank = (sbank + 1) % 3
                pS = P.f32(sbank)
                k.mm(pS, kc[c][:, kt_i, :], qt[:, c].rearrange("p a t -> p (a t)"), True, True, [kcb[c], qtb],
                     [P.bufs[sbank]])
                pend[(c, kt_i)] = sbank

            emit_S(*items[0])
            emit_S(*items[1])
            for idx, (c, kt_i) in enumerate(items):
                if idx + 2 < len(items):
                    emit_S(*items[idx + 2])
                sb_ = pend.pop((c, kt_i))
                E, Eb = er.next()
                k.act(E, P.f32(sb_), AF.Exp, [P.bufs[sb_]], [Eb], scale=scale)
                for qs in range(4):
                    k.mm(P.f32(3 + qs)[:, 0:257], E[:, qs * 128:(qs + 1) * 128], va[:, kt_i, 0:257],
                         kt_i == 0, kt_i == n_all - 1, [Eb, vab], [P.bufs[3 + qs]], inc=(qs == 3))
                if kt_i == n_all - 1:
                    for qs in range(4):
                        pO = P.f32(3 + qs)
                        k.recip(rs[:, qs:qs + 1], pO[:, 256:257], [P.bufs[3 + qs]], [rsb])
                        k.ts("dve", on[:, c, qs, :], pO[:, 0:256], rs[:, qs:qs + 1], None, ALU.mult, None,
                             [P.bufs[3 + qs], rsb], [onb])
            on0 = on[:, 0].rearrange("p a e -> p (a e)")
            on1 = on[:, 1].rearrange("p a e -> p (a e)")
            k.stt(on0, on1, nlam, on0, ALU.mult, ALU.add, [onb, nlamb], [onb])
            for qs in range(4):
                k.act(junk, on[:, 0, qs, :], AF.Square, [onb], [junkb, ssqb], accum_out=ssq[:, qs:qs + 1])
            k.act(ssq, ssq, AF.Sqrt, [ssqb], [ssqb], scale=1.0 / 256, bias=EPS)
            k.recip(ssq, ssq, [ssqb], [ssqb])
            for qs in range(4):
                k.stt(on[:, 0, qs, :], on[:, 0, qs, :], ssq[:, qs:qs + 1], sgain, ALU.mult, ALU.mult,
                      [onb, ssqb, sgainb], [onb])
            k.tt("dve", ogtile.rearrange("p a e -> p (a e)"), on0, sgt.rearrange("p a e -> p (a e)"), ALU.mult,
                 [onb, sgtb], [ogtileb])
            ogT, ogTb = ogTr.next()
            pT = P.bf16(7)
            for qs in range(4):
                for ec in range(2):
                    k.tr(pT[:, (qs * 2 + ec) * 128:(qs * 2 + ec + 1) * 128], ogtile[:, qs, ec * 128:(ec + 1) * 128],
                         ident, [ogtileb, identb], [P.bufs[7]], inc=(qs == 3 and ec == 1))
            k.cp("act", ogT.rearrange("p a c t -> p (a c t)"), pT[:, 0:1024], [P.bufs[7]], [ogTb])
            k.dma("sp", ogt[q * 4:(q + 1) * 4, :, hh * 256:(hh + 1) * 256].rearrange("i p n -> p i n"),
                  ogT.rearrange("p a c t -> p a (c t)"), [ogTb], [ogt_yb[q][hh]])
            if hh == 1:
                gather_og(q)


def phase_D(k, P, nc, ident, identb, T, n_tiles, GROUPS):
    ogall, ogall_cb = T["ogtall"], T["ogtall_cb"]
    ogmall, ogmall_cb = T["ogmall"], T["ogmall_cb"]
    wo, fgc = T["woD"], T["fgcD"]
    h1s, h1s_b = T["h1s"], T["h1s_b"]
    out = T["out"]
    wot = k.sb("wotD", [128, 24, 256], BF16)
    wotb = Buf("wotD")
    for c in range(2):
        k.dma("pool", wot[:, c * 12:(c + 1) * 12, :],
              wo[c * 1536:(c + 1) * 1536, :].rearrange("(c p) n -> p c n", p=128), (), [wotb])
    gc = k.sb("gcD", [128, 256], F32)
    gcb = Buf("gcD")
    k.dma("sp", gc, fgc.partition_broadcast(128), (), [gcb])
    ssq = k.sb("ssqD", [128, n_tiles], F32)
    ssqb = Buf("ssqD")
    k.memset("dve", ssq, 0.0, [ssqb])
    h2 = k.sb("h2D", [128, n_tiles, 256], F32)
    h2b = [Buf(f"h2_{i}") for i in range(n_tiles)]
    ogr = Ring(k, "ogD", [128, 4, 512], BF16, 3)
    ogmr = Ring(k, "ogmD", [128, 4, 256], BF16, 3)
    hr = Ring(k, "h1D", [128, 256], F32, 3)
    junk = k.sb("junkD", [128, 256], BF16)
    junkb = Buf("junkD")
    ov = ogall.rearrange("(c g ii) p n -> c ii p g n", g=4, ii=4)
    omv = ogmall.rearrange("(c g ii) p n -> c ii p g n", g=4, ii=4)
    ld = {}

    def load(i):
        if i < n_tiles:
            og, ogb = ogr.next()
            h, hb = hr.next()
            om, omb = ogmr.next()
            k.dma("sp", om, omv[i // 4][i % 4], [ogmall_cb[i // 4]], [omb])
            k.dma("sp", og, ov[i // 4][i % 4], [ogall_cb[i // 4]], [ogb])
            k.dma("sp", h, h1s[i], [h1s_b[i]], [hb])
            ld[i] = (og, ogb, h, hb, om, omb)

    pps = {}

    def proj(i):
        if i < n_tiles:
            og, ogb, h, hb, om, omb = ld[i]
            bank = 1 + (i % 2)
            pps[i] = (colproj(k, P, og, ogb, om, omb, wot, wotb, bank), bank)

    load(0)
    load(1)
    proj(0)
    for i in range(n_tiles):
        load(i + 2)
        proj(i + 1)
        og, ogb, h, hb, om, omb = ld.pop(i)
        pp, bank = pps.pop(i)
        k.tt("dve", h2[:, i, :], pp, h, ALU.add, [P.bufs[bank], hb], [h2b[i]])
        k.act(junk, h2[:, i, :], AF.Square, [h2b[i]], [junkb, ssqb], accum_out=ssq[:, i:i + 1])
    rstd, rstdb = sumsq_gather(k, P, nc, ssq, ssqb, T["ssq2"], T["ssq2_b"], T["ssq2all"], T["ssq2all_b"], GROUPS,
                               D, "D")
    outr = Ring(k, "outD", [128, 256], F32, 3)
    for i in range(n_tiles):
        o, ob = outr.next()
        k.stt(o, h2[:, i, :], rstd[:, i:i + 1], gc, ALU.mult, ALU.mult, [h2b[i], rstdb, gcb], [ob])
        k.dma("sp", out[i * 128:(i + 1) * 128, :], o, [ob], (), is_output=True)


def build_fused(n_tiles=128, groups=None):
    GROUPS = groups or [[0, 1, 2, 3], [4, 5, 6, 7]]
    nc = bass.Bass("TRN2", target_bir_lowering=False)
    S_ = n_tiles * 128
    T = {}

    def ext(name, shape, dt=F32):
        T[name] = nc.dram_tensor(name, list(shape), dt, kind="ExternalInput").ap()

    def scratch(name, shape, dt):
        T[name] = nc.dram_tensor(name, list(shape), dt).ap()
        T[name + "_b"] = Buf(name)

    ext("xA", [S_, D]); ext("ngA", [D]); ext("wA", [D, 2048]); ext("memA", [256, D]); ext("mngA", [D])
    ext("wkvA", [D, 512]); ext("decA", [4]); ext("rcsA", [S_, 128])
    ext("xcB", [S_, 256]); ext("woB", [3072, 256]); ext("ngcB", [256])
    ext("wB", [D, 2560]); ext("memB", [256, D]); ext("mngB", [D]); ext("wkvB", [D, 512]); ext("dcsB", [S_, 32])
    ext("lamC", [4, 128]); ext("slgC", [256]); ext("woD", [3072, 256]); ext("fgcD", [256])
    T["out"] = nc.dram_tensor("out", [S_, 256], F32, kind="ExternalOutput").ap()
    scratch("yg", [n_tiles, 128, 512], BF16)
    scratch("ygall", [4 * n_tiles, 128, 512], BF16)
    scratch("ygm", [n_tiles, 128, 256], BF16)
    scratch("ygmall", [4 * n_tiles, 128, 256], BF16)
    T["scrA"] = nc.dram_tensor("scrA", [n_tiles, 128, 2048], BF16).ap()
    T["sbsA"] = nc.dram_tensor("sbsA", [n_tiles, 128, 512], BF16).ap()
    T["h1s"] = nc.dram_tensor("h1s", [n_tiles, 128, 256], F32).ap()
    T["h1s_b"] = [Buf(f"h1s{i}") for i in range(n_tiles)]
    scratch("h1gt", [n_tiles, 128, 256], BF16)
    scratch("h1gtall", [4 * n_tiles, 128, 256], BF16)
    scratch("ssq1", [128, n_tiles], F32)
    scratch("ssq1all", [4 * 128, n_tiles], F32)
    T["scr1"] = nc.dram_tensor("scr1", [n_tiles, 128, 2048], BF16).ap()
    T["scr1_b"] = [Buf(f"scr1_{i}") for i in range(n_tiles)]
    scratch("ogt", [n_tiles, 128, 512], BF16)
    scratch("ogtall", [4 * n_tiles, 128, 512], BF16)
    scratch("ogm", [n_tiles, 128, 256], BF16)
    scratch("ogmall", [4 * n_tiles, 128, 256], BF16)
    scratch("ssq2", [128, n_tiles], F32)
    scratch("ssq2all", [4 * 128, n_tiles], F32)

    k = K(nc)
    P = Psum(k)
    ident, identb, io, iob = make_ident(k)

    def flat(ap):
        return ap.rearrange("i p n -> (i p) n")

    nch = n_tiles // 4

    def mk_gather(src, dst, dst_cb, rd, lag):
        done = set()

        def emit(c):
            if c in done or c >= nch or c < 0:
                return
            done.add(c)
            k.collective("AllGather", flat(T[src][c * 4:(c + 1) * 4]), flat(T[dst][c * 16:(c + 1) * 16]), GROUPS,
                         rd(c), [dst_cb[c]])

        def f(i, final=False):
            if final:
                for c in range(nch):
                    emit(c)
            else:
                c = (i - 3 - lag) // 4
                if (i - 3 - lag) % 4 == 0:
                    emit(c)
        return f

    T["yg_mb"] = [Buf() for _ in range(n_tiles)]
    T["yg_yb"] = [Buf() for _ in range(n_tiles)]
    T["ygall_cb"] = [Buf() for _ in range(nch)]
    T["gather_yg"] = mk_gather("yg", "ygall", T["ygall_cb"], lambda c: T["yg_yb"][c * 4:(c + 1) * 4], 1)
    T["ygmall_cb"] = [Buf() for _ in range(nch)]
    T["gather_ygm"] = mk_gather("ygm", "ygmall", T["ygmall_cb"], lambda c: T["yg_mb"][c * 4:(c + 1) * 4], 2)
    T["h1gt_tb"] = [Buf() for _ in range(n_tiles)]
    T["h1gtall_cb"] = [Buf() for _ in range(nch)]
    T["gather_h1"] = mk_gather("h1gt", "h1gtall", T["h1gtall_cb"], lambda c: T["h1gt_tb"][c * 4:(c + 1) * 4], 2)
    T["ogt_mb"] = [Buf() for _ in range(n_tiles)]
    T["ogt_yb"] = [[Buf(), Buf()] for _ in range(nch)]
    T["ogtall_cb"] = [Buf() for _ in range(nch)]
    g_og = mk_gather("ogt", "ogtall", T["ogtall_cb"], lambda c: T["ogt_yb"][c], 0)
    T["ogmall_cb"] = [Buf() for _ in range(nch)]
    T["gather_ogm"] = mk_gather("ogm", "ogmall", T["ogmall_cb"], lambda c: T["ogt_mb"][c * 4:(c + 1) * 4], 2)
    T["gather_og"] = lambda q: g_og(q * 4 + 3 - 4) if q > 0 else None

    k.phase_begin()
    phase_A(k, P, nc, ident, identb, io, iob, T, n_tiles)
    T["gather_ygm"](0, final=True)
    T["gather_yg"](0, final=True)
    k.phase_end()

    k.phase_begin()
    ssq, ssqb = phase_B(k, P, nc, ident, identb, T, n_tiles)
    T["gather_h1"](0, final=True)
    rstd, rstdb = sumsq_gather(k, P, nc, ssq, ssqb, T["ssq1"], T["ssq1_b"], T["ssq1all"], T["ssq1all_b"], GROUPS,
                               D, "B")
    phase_B2(k, P, nc, ident, identb, T, n_tiles, rstd, rstdb)
    T["gather_ogm"](0, final=True)
    k.phase_end()

    k.phase_begin()
    phase_C(k, P, nc, ident, identb, T, n_tiles)
    g_og(0, final=True)
    k.phase_end()

    k.phase_begin()
    phase_D(k, P, nc, ident, identb, T, n_tiles, GROUPS)
    k.finish()
    return nc, k


S = 16384


def rope_tab(seq, dim, theta):
    inv = (1.0 / (np.float32(theta) ** (np.arange(0, dim, 2, dtype=np.float32) / np.float32(dim)))).astype(np.float32)
    ang = (np.arange(seq, dtype=np.float32)[:, None] * inv[None, :]).astype(np.float32)
    return np.concatenate([np.cos(ang), np.sin(ang)], -1).astype(np.float32)


def prep_A(inp, b, g, S_=S):
    w = inp["ret_w_in"][0]
    hs = [2 * g, 2 * g + 1]
    cols = []
    for base, wd in ((0, 128), (1024, 128), (2048, 256), (4096, 256)):
        for h in hs:
            cols.append(np.arange(base + h * wd, base + (h + 1) * wd))
    cols.append(np.arange(6144 + g * 256, 6144 + (g + 1) * 256))
    cols.append(np.arange(7168 + g * 256, 7168 + (g + 1) * 256))
    cols = np.concatenate(cols)
    wkv = inp["mem_w_kv"][0]
    wkvc = np.concatenate([np.arange(g * 256, (g + 1) * 256), np.arange(1024 + g * 256, 1024 + (g + 1) * 256)])
    dec = np.array([inp["ret_decay_fwd"][0, hs[0]], inp["ret_decay_fwd"][0, hs[1]],
                    inp["ret_decay_bwd"][0, hs[0]], inp["ret_decay_bwd"][0, hs[1]]], np.float32)
    return {
        "xA": np.ascontiguousarray(inp["x"][b, :S_]),
        "ngA": np.ascontiguousarray(inp["norm_g"][0]),
        "wA": np.ascontiguousarray(w[:, cols]),
        "memA": np.ascontiguousarray(inp["mem"][b]),
        "mngA": np.ascontiguousarray(inp["mem_norm_g"][0]),
        "wkvA": np.ascontiguousarray(wkv[:, wkvc]),
        "decA": dec,
        "rcsA": rope_tab(S_, 128, 10000.0),
    }


def prep_fused(inp, core, S_=16384):
    b, j = core // 4, core % 4
    d = prep_A(inp, b, j, S_)
    rows = []
    for g in range(4):
        rows.append(np.arange(g * 512, (g + 1) * 512))
        rows.append(np.arange(2048 + g * 256, 2048 + (g + 1) * 256))
    rows = np.concatenate(rows)
    cs = slice(j * 256, (j + 1) * 256)
    w = inp["diff_w_in"][0]
    hs = [2 * j, 2 * j + 1]
    cols = []
    for base in (0, 2048, 4096, 6144):
        for h in hs:
            cols.append(np.arange(base + h * 256, base + (h + 1) * 256))
    cols.append(np.arange(8192 + j * 256, 8192 + (j + 1) * 256))
    cols.append(np.arange(9216 + j * 256, 9216 + (j + 1) * 256))
    cols = np.concatenate(cols)
    wkv = inp["mem_w_kv"][1]
    kvc = np.concatenate([np.arange(j * 256, (j + 1) * 256), np.arange(1024 + j * 256, 1024 + (j + 1) * 256)])
    d.update({
        "xcB": np.ascontiguousarray(inp["x"][b, :S_, cs]),
        "woB": np.ascontiguousarray(inp["ret_w_out"][0][rows][:, cs]),
        "ngcB": np.ascontiguousarray(inp["norm_g"][1][cs]),
        "wB": np.ascontiguousarray(w[:, cols]),
        "memB": np.ascontiguousarray(inp["mem"][b]),
        "mngB": np.ascontiguousarray(inp["mem_norm_g"][1]),
        "wkvB": np.ascontiguousarray(wkv[:, kvc]),
        "dcsB": np.ascontiguousarray(rope_tab(S_, 32, 500000.0)),
        "lamC": np.ascontiguousarray(np.stack([inp["diff_lambda_q1"][0], inp["diff_lambda_k1"][0],
                                               inp["diff_lambda_q2"][0], inp["diff_lambda_k2"][0]])),
        "slgC": np.ascontiguousarray(inp["diff_subln_g"][0]),
        "woD": np.ascontiguousarray(inp["diff_w_out"][0][rows][:, cs]),
        "fgcD": np.ascontiguousarray(inp["final_norm_g"][cs]),
    })
    return d


_CACHE = {}


def kernel(**inputs):
    inp = {k_: np.asarray(v) for k_, v in inputs.items()}
    cores = list(range(8))
    if "nc" not in _CACHE:
        _CACHE["nc"] = build_fused(128)[0]
    res = run_bass_kernel_spmd(_CACHE["nc"], [prep_fused(inp, c) for c in cores], core_ids=cores).results
    out = np.empty((2, 16384, 1024), np.float32)
    for c in cores:
        out[c // 4, :, (c % 4) * 256:(c % 4 + 1) * 256] = np.asarray(res[c]["out"])
    return out
```

```python
import numpy as np
import concourse.bass as bass
import concourse.mybir as mybir
from concourse.bass_utils import run_bass_kernel_spmd

F32 = mybir.dt.float32
BF16 = mybir.dt.bfloat16
I32 = mybir.dt.int32
AF = mybir.ActivationFunctionType
ALU = mybir.AluOpType
AX = mybir.AxisListType

SEM_ROLL = 30000


class Buf:
    __slots__ = ("name", "w", "r", "excl")

    def __init__(self, name="", excl=False):
        self.name = name
        self.excl = excl
        self.w = None
        self.r = {}


class Eng:
    def __init__(self, K, name, e):
        self.K = K
        self.name = name
        self.e = e
        self.sem = K.nc.alloc_semaphore(f"s_{name}_0")
        self.nsem = 1
        self.cnt = 0
        self.waited = {}

    def roll(self):
        if self.cnt >= SEM_ROLL:
            self.sem = self.K.nc.alloc_semaphore(f"s_{self.name}_{self.nsem}")
            self.nsem += 1
            self.cnt = 0


class K:
    def __init__(self, nc, n_dma_sems=12):
        self.nc = nc
        self.eng = {
            "pe": Eng(self, "pe", nc.tensor),
            "act": Eng(self, "act", nc.scalar),
            "dve": Eng(self, "dve", nc.vector),
            "pool": Eng(self, "pool", nc.gpsimd),
            "sp": Eng(self, "sp", nc.sync),
        }
        self.dsems = {}
        for q in ("sp", "act", "pool"):
            self.dsems[q] = [[nc.alloc_semaphore(f"d_{q}_{i}"), 0] for i in range(n_dma_sems)]
        self.dnext = {"sp": 0, "act": 0, "pool": 0}
        self.out_tokens = []
        self.n_instr = 0

    def _wait(self, E, tok):
        sem, val = tok
        key = id(sem)
        if E.waited.get(key, 0) < val:
            E.e.wait_ge(sem, val)
            E.waited[key] = val

    def _deps(self, E, reads, writes, skip_own=False):
        toks = []
        for b in reads:
            if b.w is not None:
                toks.append(b.w)
        for b in writes:
            if b.w is not None:
                toks.append(b.w)
            toks.extend(b.r.values())
        for t in toks:
            if skip_own and t[0] is E.sem:
                continue
            self._wait(E, t)

    def _mark(self, tok, reads, writes):
        for b in reads:
            cur = b.r.get(id(tok[0]))
            if cur is None or cur[1] < tok[1]:
                b.r[id(tok[0])] = tok
        for b in writes:
            b.w = tok
            b.r = {}

    def op(self, eng, fn, reads=(), writes=(), inc=True):
        E = self.eng[eng]
        if any(b.excl for b in reads):
            writes = list(writes) + [b for b in reads if b.excl]
            reads = [b for b in reads if not b.excl]
        self._deps(E, reads, writes, skip_own=(eng == "pe"))
        ins = fn(E.e)
        self.n_instr += 1
        if inc:
            E.cnt += 1
            ins.then_inc(E.sem, 1)
            tok = (E.sem, E.cnt)
        else:
            tok = (E.sem, E.cnt + 1)
        self._mark(tok, reads, writes)
        if inc:
            E.roll()
        return tok

    def dma(self, q, out, in_, reads=(), writes=(), is_output=False, **kw):
        E = self.eng[q]
        self._deps(E, reads, writes)
        pool = self.dsems[q]
        i = self.dnext[q]
        self.dnext[q] = (i + 1) % len(pool)
        slot = pool[i]
        if slot[1] > 0:
            self._wait(E, (slot[0], slot[1]))
        ins = E.e.dma_start(out=out, in_=in_, **kw)
        slot[1] += 16
        ins.then_inc(slot[0], 16)
        tok = (slot[0], slot[1])
        self.n_instr += 1
        self._mark(tok, reads, writes)
        if is_output:
            self.out_tokens.append(tok)
        return tok

    def finish(self):
        E = self.eng["sp"]
        for q in self.dsems:
            for slot in self.dsems[q]:
                if slot[1] > 0:
                    self._wait(E, (slot[0], slot[1]))
        for n in ("pe", "act", "dve", "pool"):
            e2 = self.eng[n]
            if e2.cnt > 0:
                self._wait(E, (e2.sem, e2.cnt))


def _sb(self, name, shape, dt=F32):
    stack = getattr(self, "_stack", None)
    if stack is None:
        return self.nc.alloc_sbuf_tensor(name, list(shape), dt).ap()
    self._nsb = getattr(self, "_nsb", 0) + 1
    h = stack.enter_context(self.nc.sbuf_tensor(f"{name}_{self._nsb}", list(shape), dt))
    return h.ap()


def _phase_begin(self):
    import contextlib
    self._stack = contextlib.ExitStack()


def _barrier(self):
    toks = []
    for n in ("pe", "act", "dve", "pool", "sp"):
        e2 = self.eng[n]
        if e2.cnt > 0:
            toks.append((e2.sem, e2.cnt))
    for q in self.dsems:
        for slot in self.dsems[q]:
            if slot[1] > 0:
                toks.append((slot[0], slot[1]))
    toks.extend(getattr(self, "cc_tokens", []))
    for n in ("pe", "act", "dve", "pool", "sp"):
        E = self.eng[n]
        for t in toks:
            if t[0] is E.sem:
                continue
            self._wait(E, t)


def _phase_end(self):
    self.barrier()
    self._stack.close()
    self._stack = None


K.phase_begin = _phase_begin
K.phase_end = _phase_end
K.barrier = _barrier


def _ps(self, name, shape, dt=F32):
    return self.nc.alloc_psum_tensor(name, list(shape), dt).ap()


K.sb = _sb
K.ps = _ps


def _mm(self, out, lhsT, rhs, start, stop, reads, writes, inc=None, **kw):
    if inc is None:
        inc = stop
    return self.op("pe", lambda e: e.matmul(out, lhsT=lhsT, rhs=rhs, start=start, stop=stop, **kw),
                   reads, writes, inc=inc)


def _tr(self, out, in_, ident, reads, writes, inc=True):
    return self.op("pe", lambda e: e.transpose(out=out, in_=in_, identity=ident), reads, writes, inc=inc)


def _act(self, out, in_, func, reads, writes, **kw):
    return self.op("act", lambda e: e.activation(out=out, in_=in_, func=func, **kw), reads, writes)


def _tt(self, eng, out, in0, in1, op, reads, writes):
    return self.op(eng, lambda e: e.tensor_tensor(out=out, in0=in0, in1=in1, op=op), reads, writes)


def _ts(self, eng, out, in0, s1, s2, op0, op1, reads, writes):
    if op1 is None and eng == "pool" and op0 == ALU.mult:
        op1, s2 = ALU.mult, 1.0
    if op1 is None:
        return self.op(eng, lambda e: e.tensor_scalar(out=out, in0=in0, scalar1=s1, scalar2=None, op0=op0),
                       reads, writes)
    return self.op(eng, lambda e: e.tensor_scalar(out=out, in0=in0, scalar1=s1, scalar2=s2, op0=op0, op1=op1),
                   reads, writes)


def _stt(self, out, in0, scalar, in1, op0, op1, reads, writes):
    return self.op("dve", lambda e: e.scalar_tensor_tensor(out=out, in0=in0, scalar=scalar, in1=in1,
                                                           op0=op0, op1=op1), reads, writes)


def _cp(self, eng, out, in_, reads, writes):
    if eng == "act":
        return self.op("act", lambda e: e.copy(out=out, in_=in_), reads, writes)
    return self.op(eng, lambda e: e.tensor_copy(out=out, in_=in_), reads, writes)


def _recip(self, out, in_, reads, writes):
    return self.op("dve", lambda e: e.reciprocal(out=out, in_=in_), reads, writes)


def _memset(self, eng, ap, val, writes):
    return self.op(eng, lambda e: e.memset(ap, val), (), writes)


K.mm = _mm
K.tr = _tr
K.act = _act
K.tt = _tt
K.ts = _ts
K.stt = _stt
K.cp = _cp
K.recip = _recip
K.memset = _memset


class Ring:
    def __init__(self, k, name, shape, dt, n):
        self.aps = [k.sb(f"{name}{i}", shape, dt) for i in range(n)]
        self.bufs = [Buf(f"{name}{i}") for i in range(n)]
        self.n = n
        self.i = -1

    def next(self):
        self.i = (self.i + 1) % self.n
        return self.aps[self.i], self.bufs[self.i]

    def cur(self):
        return self.aps[self.i], self.bufs[self.i]


class Psum:
    def __init__(self, k):
        self.banks = [k.ps(f"bank{i}", [128, 512], F32) for i in range(8)]
        self.bufs = [Buf(f"bank{i}", excl=True) for i in range(8)]

    def f32(self, i):
        return self.banks[i]

    def bf16(self, i):
        return self.banks[i].bitcast(BF16)


def _collective(self, kind, in_ap, out_ap, groups, reads=(), writes=()):
    E = self.eng["pool"]
    self._deps(E, reads, writes)
    if not hasattr(self, "cc_sem"):
        self.cc_sem = self.nc.alloc_semaphore("cc_sem")
        self.cc_cnt = 0
        self.cc_tokens = []
    ins = E.e.collective_compute(kind, ALU.bypass, replica_groups=groups, ins=[in_ap.opt()], outs=[out_ap.opt()])
    ins.then_inc(self.cc_sem)
    self.cc_cnt += 1
    tok = (self.cc_sem, self.cc_cnt)
    self.n_instr += 1
    self._mark(tok, reads, writes)
    self.cc_tokens = [tok]
    return tok


K.collective = _collective


S = 16384
D = 1024
NT = S // 128
EPS = 1e-6


def norm_to_T(k, P, xt, xb, gt, gb, ident, identb, n_feat, scr, hnT, hnTb, pbank):
    junk, junkb, ss, ssb, rstd, rstdb, hn, hnb = scr
    k.act(junk, xt, AF.Square, [xb], [junkb, ssb], accum_out=ss)
    k.act(rstd, ss, AF.Sqrt, [ssb], [rstdb], scale=1.0 / n_feat, bias=EPS)
    k.recip(rstd, rstd, [rstdb], [rstdb])
    k.stt(hn, xt, rstd, gt, ALU.mult, ALU.mult, [xb, rstdb, gb], [hnb])
    nch = n_feat // 128
    pT = P.bf16(pbank)
    for c in range(nch):
        k.tr(pT[:, c * 128:(c + 1) * 128], hn[:, c * 128:(c + 1) * 128], ident, [hnb, identb], [P.bufs[pbank]],
             inc=(c == nch - 1))
    k.cp("act", hnT, pT[:, 0:nch * 128].rearrange("p (c t) -> p c t", c=nch), [P.bufs[pbank]], [hnTb])


def make_ident(k):
    io = k.sb("io_id", [128, 128], F32)
    iob = Buf("io")
    idf = k.sb("identf", [128, 128], F32)
    ident = k.sb("ident", [128, 128], BF16)
    identb = Buf("ident")
    k.op("pool", lambda e: e.iota(io, pattern=[[1, 128]], base=0, channel_multiplier=-1,
                                  allow_small_or_imprecise_dtypes=True), (), [iob])
    k.ts("dve", idf, io, 0.0, None, ALU.is_equal, None, [iob], [iob])
    k.cp("dve", ident, idf, [iob], [identb])
    return ident, identb, io, iob


def mem_prep(k, P, nc, mem, mng, wkv, nheads, ident, identb, scr, gt, gb):
    mkT = k.sb("mkT", [128, nheads, 2, 256], BF16)
    mkTb = Buf("mkT")
    mva = k.sb("mva", [128, nheads, 2, 258], BF16)
    mvab = Buf("mva")
    mt_x = k.sb("mem_x", [128, D], F32)
    mt_xb = Buf("mem_x")
    mg = k.sb("mem_g", [128, D], F32)
    mgb = Buf("mem_g")
    mnT = k.sb("mnT", [128, 8, 128], BF16)
    mnTb = Buf("mnT")
    wk = k.sb("wkv_sb", [128, 8, 512], BF16)
    wkb = Buf("wkv_sb")
    mkt = k.sb("mk_tok", [128, 256], BF16)
    mktb = Buf("mk_tok")
    k.dma("sp", mg, mng.partition_broadcast(128), (), [mgb])
    k.memset("dve", mva, 1.0, [mvab])
    for hd in range(nheads):
        k.dma("pool", wk, wkv[:, hd * 512:(hd + 1) * 512].rearrange("(c p) n -> p c n", p=128), (), [wkb])
        for mt in range(2):
            k.dma("sp", mt_x, mem[mt * 128:(mt + 1) * 128, :], (), [mt_xb])
            norm_to_T(k, P, mt_x, mt_xb, mg, mgb, ident, identb, D, scr, mnT, mnTb, 0)
            pp = P.f32(1)
            for c in range(8):
                k.mm(pp, mnT[:, c, :], wk[:, c, :], c == 0, c == 7, [mnTb, wkb], [P.bufs[1]])
            k.cp("act", mkt, pp[:, 0:256], [P.bufs[1]], [mktb])
            k.cp("dve", mva[:, hd, mt, 0:256], pp[:, 256:512], [P.bufs[1]], [mvab])
            pT = P.bf16(0)
            for dc in range(2):
                k.tr(pT[:, dc * 128:(dc + 1) * 128], mkt[:, dc * 128:(dc + 1) * 128], ident, [mktb, identb],
                     [P.bufs[0]], inc=(dc == 1))
            k.cp("act", mkT[:, hd, :, mt * 128:(mt + 1) * 128],
                 pT[:, 0:256].rearrange("p (c t) -> p c t", c=2), [P.bufs[0]], [mkTb])
    return mkT, mkTb, mva, mvab


def mem_attn(k, P, qm_src, qm_srcb, sgm, sgmb, mkT_h, mkTb, mva_h, mvab, ident, identb, tmp, banks, out, outb):
    qmT, qmTb, E, Eb, rs, rsb = tmp
    bT, bS, bO = banks
    pT = P.bf16(bT)
    for dc in range(2):
        k.tr(pT[:, dc * 128:(dc + 1) * 128], qm_src[:, dc * 128:(dc + 1) * 128], ident, [qm_srcb, identb],
             [P.bufs[bT]], inc=(dc == 1))
    k.cp("act", qmT, pT[:, 0:256].rearrange("p (c t) -> p c t", c=2), [P.bufs[bT]], [qmTb])
    pS = P.f32(bS)
    for mt in range(2):
        for dc in range(2):
            k.mm(pS[:, mt * 128:(mt + 1) * 128], mkT_h[:, dc, mt * 128:(mt + 1) * 128], qmT[:, dc, :],
                 dc == 0, dc == 1, [mkTb, qmTb], [P.bufs[bS]], inc=(mt == 1 and dc == 1))
    k.act(E, pS[:, 0:256].rearrange("p (c t) -> p c t", c=2), AF.Exp, [P.bufs[bS]], [Eb], scale=1.0 / 16.0)
    pO = P.f32(bO)
    for mt in range(2):
        k.mm(pO[:, 0:257], E[:, mt, :], mva_h[:, mt, 0:257], mt == 0, mt == 1, [Eb, mvab], [P.bufs[bO]])
    k.recip(rs, pO[:, 256:257], [P.bufs[bO]], [rsb])
    k.stt(out, pO[:, 0:256], rs, sgm, ALU.mult, ALU.mult, [P.bufs[bO], rsb, sgmb], [outb])


def phase_A(k, P, nc, ident, identb, io, iob, T, n_tiles):
    x, ng, w, mem, mng, wkv, dec, rcs = (T[n] for n in ("xA", "ngA", "wA", "memA", "mngA", "wkvA", "decA", "rcsA"))
    yg, ygm, scrd, sbs = T["yg"], T["ygm"], T["scrA"], T["sbsA"]
    gather_ygm = T["gather_ygm"]
    yg_mb, yg_yb = T["yg_mb"], T["yg_yb"]
    gather_yg = T["gather_yg"]
    scr_b = [Buf(f"scr{i}") for i in range(n_tiles)]
    sbs_b = [Buf(f"sbs{i}") for i in range(n_tiles)]
    stop = 99
    wt = k.sb("wt", [128, 8, 2048], BF16)
    wtb = Buf("wt")
    for c4 in range(4):
        k.dma("pool", wt[:, :, c4 * 512:(c4 + 1) * 512],
              w[:, c4 * 512:(c4 + 1) * 512].rearrange("(c p) n -> p c n", p=128), (), [wtb])
    gt = k.sb("gt", [128, D], F32)
    gb = Buf("gt")
    k.dma("sp", gt, ng.partition_broadcast(128), (), [gb])

    def mkscr(tag, nf):
        return (k.sb("junk" + tag, [128, nf], F32), Buf(), k.sb("ss" + tag, [128, 1], F32), Buf(),
                k.sb("rstd" + tag, [128, 1], F32), Buf(), k.sb("hn" + tag, [128, nf], BF16), Buf())

    scr = mkscr("a", D)
    mkT, mkTb, mva, mvab = mem_prep(k, P, nc, mem, mng, wkv, 1, ident, identb, scr, gt, gb)

    lg = k.sb("lg", [128, 4], F32)
    lgb = Buf("lg")
    k.dma("sp", lg, dec.partition_broadcast(128), (), [lgb])
    k.act(lg, lg, AF.Exp, [lgb], [lgb], scale=-1.0)
    k.act(lg, lg, AF.Ln, [lgb], [lgb], bias=1.0)
    k.ts("dve", lg, lg, -1.0, None, ALU.mult, None, [lgb], [lgb])
    cb = Buf("consts")
    tmpc = k.sb("tmpc", [128, 128], F32)
    tmpm = k.sb("tmpm", [128, 128], F32)
    tmpe = k.sb("tmpe", [128, 128], F32)
    DT = k.sb("DT", [128, 2, 128], F32)
    QDF = k.sb("QDF", [128, 2, 128], F32)
    QDB = k.sb("QDB", [128, 2, 128], F32)
    kd = k.sb("kd", [128, 4], F32)
    cd = k.sb("cd", [128, 4], F32)
    ci = k.sb("ci", [128, 128], F32)
    cbk = k.sb("cbk", [128, 128], F32)
    pi = k.sb("pi", [128, 2], F32)
    for h in range(2):
        k.ts("dve", tmpc, io, 0.0, None, ALU.max, None, [iob], [cb])
        k.act(tmpe, tmpc, AF.Exp, [cb, lgb], [cb], scale=lg[:, h:h + 1])
        k.ts("dve", tmpm, io, 0.0, None, ALU.is_ge, None, [iob], [cb])
        k.tt("dve", DT[:, h, :], tmpe, tmpm, ALU.mult, [cb], [cb])
        k.ts("dve", tmpc, io, -1.0, 0.0, ALU.mult, ALU.max, [iob], [cb])
        k.act(tmpe, tmpc, AF.Exp, [cb, lgb], [cb], scale=lg[:, 2 + h:3 + h])
        k.ts("dve", tmpm, io, 0.0, None, ALU.is_lt, None, [iob], [cb])
        k.tt("dve", tmpe, tmpe, tmpm, ALU.mult, [cb], [cb])
        k.tt("dve", DT[:, h, :], DT[:, h, :], tmpe, ALU.add, [cb], [cb])
    k.op("pool", lambda e: e.iota(ci, pattern=[[1, 128]], base=1, channel_multiplier=0,
                                  allow_small_or_imprecise_dtypes=True), (), [cb])
    k.op("pool", lambda e: e.iota(cbk, pattern=[[-1, 128]], base=128, channel_multiplier=0,
                                  allow_small_or_imprecise_dtypes=True), (), [cb])
    k.op("pool", lambda e: e.iota(pi[:, 0:1], pattern=[[0, 1]], base=127, channel_multiplier=-1,
                                  allow_small_or_imprecise_dtypes=True), (), [cb])
    k.op("pool", lambda e: e.iota(pi[:, 1:2], pattern=[[0, 1]], base=0, channel_multiplier=1,
                                  allow_small_or_imprecise_dtypes=True), (), [cb])
    for h in range(2):
        k.act(QDF[:, h, :], ci, AF.Exp, [cb, lgb], [cb], scale=lg[:, h:h + 1])
        k.act(QDB[:, h, :], cbk, AF.Exp, [cb, lgb], [cb], scale=lg[:, 2 + h:3 + h])
        k.act(kd[:, h:h + 1], pi[:, 0:1], AF.Exp, [cb, lgb], [cb], scale=lg[:, h:h + 1])
        k.act(kd[:, 2 + h:3 + h], pi[:, 1:2], AF.Exp, [cb, lgb], [cb], scale=lg[:, 2 + h:3 + h])
    k.act(cd, lg, AF.Exp, [lgb], [cb], scale=128.0)

    NRING = 8
    rg = dict(
        x=Ring(k, "pxt", [128, D], F32, 3), rc=Ring(k, "prc", [128, 128], F32, NRING),
        hn=Ring(k, "phn", [128, D], BF16, 3), hnT=Ring(k, "phnT", [128, 8, 128], BF16, 2),
        qk=Ring(k, "pqk", [128, 4, 128], F32, 3), qkr=Ring(k, "pqkr", [128, 4, 128], BF16, 4),
        st=Ring(k, "pst", [128, 2048], BF16, 5), qm=Ring(k, "pqm", [128, 256], BF16, 3),
        sgm=Ring(k, "psgm", [128, 256], F32, NRING), qmT=Ring(k, "pqmT", [128, 2, 128], BF16, 3),
        E=Ring(k, "pE", [128, 2, 128], BF16, 3), mgt=Ring(k, "pmgt", [128, 256], BF16, 3),
        mgT=Ring(k, "pmgT", [128, 2, 128], BF16, 3), ss=Ring(k, "pss", [128, 2], F32, 3),
        rs=Ring(k, "prs", [128, 1], F32, 3), t1=Ring(k, "pt1", [128, 4, 64], F32, 2),
        t2=Ring(k, "pt2", [128, 4, 64], F32, 2), junk=Ring(k, "pjunk", [128, D], BF16, 2),
    )
    tl = {}

    def g(i, name):
        d = tl.setdefault(i, {})
        if name not in d:
            d[name] = rg[name].next()
        return d[name]

    def f_load(i):
        xt, xb = g(i, "x")
        rc, rcb = g(i, "rc")
        k.dma("sp", xt, x[i * 128:(i + 1) * 128, :], (), [xb])
        k.dma("sp", rc, rcs[i * 128:(i + 1) * 128, :], (), [rcb])

    def f_norm(i):
        xt, xb = g(i, "x")
        ss, ssb = g(i, "ss")
        hn, hnb = g(i, "hn")
        junk, junkb = g(i, "junk")
        k.act(junk, xt, AF.Square, [xb], [junkb, ssb], accum_out=ss[:, 0:1])
        k.act(ss[:, 1:2], ss[:, 0:1], AF.Sqrt, [ssb], [ssb], scale=1.0 / D, bias=EPS)
        k.recip(ss[:, 1:2], ss[:, 1:2], [ssb], [ssb])
        k.stt(hn, xt, ss[:, 1:2], gt, ALU.mult, ALU.mult, [xb, ssb, gb], [hnb])

    def f_tr(i):
        hn, hnb = g(i, "hn")
        pT = P.bf16(0)
        for c in range(8):
            k.tr(pT[:, c * 128:(c + 1) * 128], hn[:, c * 128:(c + 1) * 128], ident, [hnb, identb], [P.bufs[0]],
                 inc=(c == 7))

    def f_hcopy(i):
        hnT, hnTb = g(i, "hnT")
        k.cp("act", hnT, P.bf16(0)[:, 0:1024].rearrange("p (c t) -> p c t", c=8), [P.bufs[0]], [hnTb])

    def f_proj(i):
        hnT, hnTb = g(i, "hnT")
        for gi in range(4):
            pp = P.f32(1 + gi)
            for c in range(8):
                k.mm(pp, hnT[:, c, :], wt[:, c, gi * 512:(gi + 1) * 512], c == 0, c == 7, [hnTb, wtb],
                     [P.bufs[1 + gi]])

    def f_evac(i):
        st, stb = g(i, "st")
        qk, qkb = g(i, "qk")
        qm, qmb = g(i, "qm")
        sgm, sgmb = g(i, "sgm")
        p0 = P.f32(1)
        k.cp("act", qk[:, 0:2, :], p0[:, 0:256].rearrange("p (u d) -> p u d", u=2), [P.bufs[1]], [qkb])
        k.act(qk[:, 2:4, :], p0[:, 256:512].rearrange("p (u d) -> p u d", u=2), AF.Copy, [P.bufs[1]], [qkb],
              scale=128.0 ** -0.5)
        k.cp("act", st[:, 1024:1536], P.f32(2), [P.bufs[2]], [stb])
        k.act(st[:, 1536:2048], P.f32(3), AF.Silu, [P.bufs[3]], [stb])
        p3 = P.f32(4)
        k.cp("dve", qm, p3[:, 0:256], [P.bufs[4]], [qmb])
        k.act(sgm, p3[:, 256:512], AF.Silu, [P.bufs[4]], [sgmb])

    def f_rot(i):
        qk, qkb = g(i, "qk")
        qkr, qkrb = g(i, "qkr")
        rc, rcb = g(i, "rc")
        t1, t1b = rg["t1"].next()
        t2, t2b = rg["t2"].next()
        x1 = qk[:, :, 0:64]
        x2 = qk[:, :, 64:128]
        cosb = rc[:, 0:64].unsqueeze(1).to_broadcast([128, 4, 64])
        sinb = rc[:, 64:128].unsqueeze(1).to_broadcast([128, 4, 64])
        k.tt("dve", t1, x1, cosb, ALU.mult, [qkb, rcb], [t1b])
        k.tt("dve", t2, x2, sinb, ALU.mult, [qkb, rcb], [t2b])
        k.tt("dve", qkr[:, :, 0:64], t1, t2, ALU.subtract, [t1b, t2b], [qkrb])
        k.tt("dve", t1, x1, sinb, ALU.mult, [qkb, rcb], [t1b])
        k.tt("dve", t2, x2, cosb, ALU.mult, [qkb, rcb], [t2b])
        k.tt("dve", qkr[:, :, 64:128], t1, t2, ALU.add, [t1b, t2b], [qkrb])
        qm, qmb = g(i, "qm")
        pT = P.bf16(6)
        for dc in range(2):
            k.tr(pT[:, 512 + dc * 128:512 + (dc + 1) * 128], qm[:, dc * 128:(dc + 1) * 128], ident, [qmb, identb],
                 [P.bufs[6]], inc=(dc == 1))

    def f_kfb(i):
        qkr, qkrb = g(i, "qkr")
        st, stb = g(i, "st")
        for h in range(2):
            k.ts("pool", st[:, 512 + h * 128:512 + (h + 1) * 128], qkr[:, 2 + h, :], kd[:, h:h + 1], None,
                 ALU.mult, None, [qkrb, cb], [stb])
            k.ts("pool", st[:, 768 + h * 128:768 + (h + 1) * 128], qkr[:, 2 + h, :], kd[:, 2 + h:3 + h], None,
                 ALU.mult, None, [qkrb, cb], [stb])
        qmT, qmTb = g(i, "qmT")
        k.cp("act", qmT, P.bf16(6)[:, 512:768].rearrange("p (c t) -> p c t", c=2), [P.bufs[6]], [qmTb])
        pT = P.bf16(6)
        for u in range(4):
            k.tr(pT[:, u * 128:(u + 1) * 128], qkr[:, u, :], ident, [qkrb, identb], [P.bufs[6]], inc=(u == 3))

    def f_store(i):
        st, stb = g(i, "st")
        k.cp("act", st[:, 0:512], P.bf16(6)[:, 0:512], [P.bufs[6]], [stb])
        k.dma("sp", scrd[i], st, [stb], [scr_b[i]])
        qmT, qmTb = g(i, "qmT")
        pS = P.f32(7)
        for mt in range(2):
            for dc in range(2):
                k.mm(pS[:, mt * 128:(mt + 1) * 128], mkT[:, 0][:, dc, mt * 128:(mt + 1) * 128], qmT[:, dc, :],
                     dc == 0, dc == 1, [mkTb, qmTb], [P.bufs[7]], inc=(mt == 1 and dc == 1))

    def f_exp(i):
        E, Eb = g(i, "E")
        k.act(E, P.f32(7)[:, 0:256].rearrange("p (c t) -> p c t", c=2), AF.Exp, [P.bufs[7]], [Eb], scale=1.0 / 16.0)

    def f_av(i):
        E, Eb = g(i, "E")
        pO = P.f32(5)
        for mt in range(2):
            k.mm(pO[:, 0:257], E[:, mt, :], mva[:, 0][:, mt, 0:257], mt == 0, mt == 1, [Eb, mvab], [P.bufs[5]])

    def f_mnorm(i):
        rs, rsb = g(i, "rs")
        sgm, sgmb = g(i, "sgm")
        mgt, mgtb = g(i, "mgt")
        pO = P.f32(5)
        k.recip(rs, pO[:, 256:257], [P.bufs[5]], [rsb])
        k.stt(mgt, pO[:, 0:256], rs, sgm, ALU.mult, ALU.mult, [P.bufs[5], rsb, sgmb], [mgtb])

    def f_mtr(i):
        mgt, mgtb = g(i, "mgt")
        pT = P.bf16(7)
        for ec in range(2):
            k.tr(pT[:, 512 + ec * 128:512 + (ec + 1) * 128], mgt[:, ec * 128:(ec + 1) * 128], ident,
                 [mgtb, identb], [P.bufs[7]], inc=(ec == 1))

    def f_mout(i):
        mgT, mgTb = g(i, "mgT")
        k.cp("act", mgT, P.bf16(7)[:, 512:768].rearrange("p (c t) -> p c t", c=2), [P.bufs[7]], [mgTb])
        k.dma("sp", ygm[i], mgT.rearrange("p c t -> p (c t)"), [mgTb], [yg_mb[i]])
        gather_ygm(i)
        tl.pop(i, None)

    stages = [(0, f_load), (1, f_norm), (2, f_tr), (3, f_proj), (4, f_evac), (5, f_rot), (6, f_kfb), (7, f_store),
              (8, f_exp), (9, f_av), (10, f_mnorm), (11, f_mtr), (12, f_mout)]
    stages = stages[::-1]
    stages = [(3, f_hcopy)] + stages
    maxoff = max(o for o, _ in stages)
    for t in range(n_tiles + maxoff):
        for off, fn in stages:
            i = t - off
            if 0 <= i < n_tiles:
                fn(i)

    gather_ygm(0, final=True)

    Sb = k.sb("Sb", [128, 2, 256], F32)
    Sbb = Buf("Sb")
    Sf = k.sb("Sf", [128, 2, 256], F32)
    Sfb = Buf("Sf")
    Sf16 = k.sb("Sf16", [128, 2, 256], BF16)
    Sf16b = Buf("Sf16")
    k.memset("dve", Sb, 0.0, [Sbb])
    k.memset("dve", Sf, 0.0, [Sfb])
    k.memset("dve", Sf16, 0.0, [Sf16b])
    kvr = Ring(k, "kvr", [128, 1024], BF16, 3)
    sbst = Ring(k, "sbst", [128, 512], BF16, 3)
    ldB = {}

    def loadBk(n):
        if n >= 0:
            kv, kvb = kvr.next()
            k.dma("sp", kv, scrd[n][:, 512:1536], [scr_b[n]], [kvb])
            ldB[n] = (kv, kvb)

    loadBk(n_tiles - 1)
    loadBk(n_tiles - 2)
    for n in range(n_tiles - 1, -1, -1):
        loadBk(n - 2)
        kv, kvb = ldB.pop(n)
        so, sob = sbst.next()
        k.cp("act", so, Sb.rearrange("p h e -> p (h e)"), [Sbb], [sob])
        k.dma("sp", sbs[n], so, [sob], [sbs_b[n]])
        if n == 0:
            break
        for h in range(2):
            bk = 1 + h
            k.mm(P.f32(bk)[:, 0:256], kv[:, 256 + h * 128:256 + (h + 1) * 128], kv[:, 512 + h * 256:512 + (h + 1) * 256],
                 True, True, [kvb], [P.bufs[bk]])
            k.stt(Sb[:, h, :], Sb[:, h, :], cd[:, 2 + h:3 + h], P.f32(bk)[:, 0:256], ALU.mult, ALU.add,
                  [Sbb, cb, P.bufs[bk]], [Sbb])

    Sf16p = [k.sb("Sf16p0", [128, 2, 256], BF16), k.sb("Sf16p1", [128, 2, 256], BF16)]
    Sf16pb = [Buf("Sf16p0"), Buf("Sf16p1")]
    k.memset("dve", Sf16p[0], 0.0, [Sf16pb[0]])
    k.memset("dve", Sf16p[1], 0.0, [Sf16pb[1]])
    fr = dict(
        ch=Ring(k, "fch", [128, 2048], BF16, 6), sbn=Ring(k, "fsbn", [128, 512], BF16, 4),
        AT=Ring(k, "fAT", [128, 2, 128], BF16, 3), qsf=Ring(k, "fqsf", [128, 2, 128], BF16, 3),
        qsb=Ring(k, "fqsb", [128, 2, 128], BF16, 3), ss=Ring(k, "fss", [128, 2], F32, 3),
        junk=Ring(k, "fjunk", [128, 256], BF16, 2), ygt=Ring(k, "fygt", [128, 2, 256], BF16, 3),
        yT=Ring(k, "fyT", [128, 4, 128], BF16, 3),
    )
    ft = {}

    def fg(n, name):
        d = ft.setdefault(n, {})
        if name not in d:
            d[name] = fr[name].next()
        return d[name]

    def F_load(n):
        ch, chb = fg(n, "ch")
        sbn, sbnb = fg(n, "sbn")
        k.dma("sp", ch, scrd[n], [scr_b[n]], [chb])
        k.dma("sp", sbn, sbs[n], [sbs_b[n]], [sbnb])

    def F_s(n):
        ch, chb = fg(n, "ch")
        for h in range(2):
            qT = ch[:, h * 128:(h + 1) * 128]
            kT = ch[:, 256 + h * 128:256 + (h + 1) * 128]
            k.mm(P.f32(1)[:, h * 128:(h + 1) * 128], kT, qT, True, True, [chb], [P.bufs[1]])
        if n < n_tiles - 1:
            for h in range(2):
                kf = ch[:, 512 + h * 128:512 + (h + 1) * 128]
                v = ch[:, 1024 + h * 256:1024 + (h + 1) * 256]
                k.mm(P.f32(5)[:, h * 256:(h + 1) * 256], kf, v, True, True, [chb], [P.bufs[5]])

    def F_dec(n):
        ch, chb = fg(n, "ch")
        a, ab = fg(n, "AT")
        f, fb = fg(n, "qsf")
        bq, bqb = fg(n, "qsb")
        k.tt("dve", a, P.f32(1)[:, 0:256].rearrange("p (h c) -> p h c", h=2), DT, ALU.mult, [P.bufs[1], cb], [ab])
        qT2 = ch[:, 0:256].rearrange("p (h c) -> p h c", h=2)
        k.tt("pool", f, qT2, QDF, ALU.mult, [chb, cb], [fb])
        k.tt("pool", bq, qT2, QDB, ALU.mult, [chb, cb], [bqb])
        if n < n_tiles - 1:
            nxt = (n + 1) % 2
            for h in range(2):
                k.stt(Sf[:, h, :], Sf[:, h, :], cd[:, h:h + 1], P.f32(5)[:, h * 256:(h + 1) * 256], ALU.mult, ALU.add,
                      [Sfb, cb, P.bufs[5]], [Sfb])
            k.cp("act", Sf16p[nxt], Sf, [Sfb], [Sf16pb[nxt]])

    def F_y(n):
        ch, chb = fg(n, "ch")
        sbn, sbnb = fg(n, "sbn")
        a, ab = fg(n, "AT")
        f, fb = fg(n, "qsf")
        bq, bqb = fg(n, "qsb")
        bY = 3 + (n % 2)
        cur = n % 2
        for h in range(2):
            v = ch[:, 1024 + h * 256:1024 + (h + 1) * 256]
            pY = P.f32(bY)[:, h * 256:(h + 1) * 256]
            k.mm(pY, a[:, h, :], v, True, False, [ab, chb], [P.bufs[bY]])
            k.mm(pY, f[:, h, :], Sf16p[cur][:, h, :], False, False, [fb, Sf16pb[cur]], [P.bufs[bY]])
            k.mm(pY, bq[:, h, :], sbn[:, h * 256:(h + 1) * 256], False, True, [bqb, sbnb], [P.bufs[bY]])

    def F_ss(n):
        ss, ssb = fg(n, "ss")
        bY = 3 + (n % 2)
        for h in range(2):
            junk, junkb = fr["junk"].next()
            k.act(junk, P.f32(bY)[:, h * 256:(h + 1) * 256], AF.Square, [P.bufs[bY]], [junkb, ssb],
                  accum_out=ss[:, h:h + 1])
        k.act(ss, ss, AF.Sqrt, [ssb], [ssb], scale=1.0 / 256, bias=EPS)

    def F_gate(n):
        ss, ssb = fg(n, "ss")
        ch, chb = fg(n, "ch")
        ygt, ygb = fg(n, "ygt")
        bY = 3 + (n % 2)
        k.recip(ss, ss, [ssb], [ssb])
        for h in range(2):
            sg = ch[:, 1536 + h * 256:1536 + (h + 1) * 256]
            k.stt(ygt[:, h, :], P.f32(bY)[:, h * 256:(h + 1) * 256], ss[:, h:h + 1], sg, ALU.mult, ALU.mult,
                  [P.bufs[bY], ssb, chb], [ygb])

    def F_tr(n):
        ygt, ygb = fg(n, "ygt")
        pT = P.bf16(7)
        for h in range(2):
            for ec in range(2):
                k.tr(pT[:, (h * 2 + ec) * 128:(h * 2 + ec + 1) * 128], ygt[:, h, ec * 128:(ec + 1) * 128], ident,
                     [ygb, identb], [P.bufs[7]], inc=(h == 1 and ec == 1))

    def F_out(n):
        yT, yTb = fg(n, "yT")
        k.cp("act", yT, P.bf16(7)[:, 0:512].rearrange("p (c t) -> p c t", c=4), [P.bufs[7]], [yTb])
        k.dma("sp", yg[n], yT.rearrange("p c t -> p (c t)"), [yTb], [yg_yb[n]])
        gather_yg(n)
        ft.pop(n, None)

    fstages = [(0, F_load), (1, F_s), (2, F_dec), (3, F_y), (4, F_ss), (5, F_gate), (6, F_tr), (7, F_out)][::-1]
    for t in range(n_tiles + 7):
        for off, fn in fstages:
            n = t - off
            if 0 <= n < n_tiles:
                fn(n)


import math

LAMBDA_INIT = 0.8 - 0.6 * math.exp(-0.3 * 1)
def sumsq_gather(k, P, nc, ssq, ssqb, dr_in, dr_in_b, dr_all, dr_all_b, groups, n_feat, tag):
    nt = ssq.shape[1]
    k.dma("sp", dr_in, ssq, [ssqb], [dr_in_b])
    k.collective("AllGather", dr_in, dr_all, groups, [dr_in_b], [dr_all_b])
    g4 = k.sb("ssq4" + tag, [128, 4, nt], F32)
    g4b = Buf("ssq4" + tag)
    k.dma("sp", g4, dr_all.rearrange("(r p) n -> p r n", p=128), [dr_all_b], [g4b])
    rstd = k.sb("rstd" + tag, [128, nt], F32)
    rstdb = Buf("rstd" + tag)
    k.tt("dve", rstd, g4[:, 0, :], g4[:, 1, :], ALU.add, [g4b], [rstdb])
    k.tt("dve", rstd, rstd, g4[:, 2, :], ALU.add, [g4b, rstdb], [rstdb])
    k.tt("dve", rstd, rstd, g4[:, 3, :], ALU.add, [g4b, rstdb], [rstdb])
    k.act(rstd, rstd, AF.Sqrt, [rstdb], [rstdb], scale=1.0 / n_feat, bias=EPS)
    k.recip(rstd, rstd, [rstdb], [rstdb])
    return rstd, rstdb


def colproj(k, P, src, srcb, srcm, srcmb, wot, wotb, bank):
    pp = P.f32(bank)[:, 0:256]
    n = 0
    for g in range(4):
        for c in (4, 5):
            k.mm(pp, srcm[:, g, (c - 4) * 128:(c - 3) * 128], wot[:, g * 6 + c, :], n == 0, n == 23,
                 [srcmb, wotb], [P.bufs[bank]])
            n += 1
    for g in range(4):
        for c in range(4):
            k.mm(pp, src[:, g, c * 128:(c + 1) * 128], wot[:, g * 6 + c, :], n == 0, n == 23, [srcb, wotb],
                 [P.bufs[bank]])
            n += 1
    return pp


def phase_B(k, P, nc, ident, identb, T, n_tiles):
    ygall, ygall_cb = T["ygall"], T["ygall_cb"]
    ygmall, ygmall_cb = T["ygmall"], T["ygmall_cb"]
    xc, wo, ngc = T["xcB"], T["woB"], T["ngcB"]
    h1s, h1s_b = T["h1s"], T["h1s_b"]
    h1gt, h1gt_tb = T["h1gt"], T["h1gt_tb"]
    gather_h1 = T["gather_h1"]
    wot = k.sb("wotB", [128, 24, 256], BF16)
    wotb = Buf("wotB")
    for c in range(2):
        k.dma("pool", wot[:, c * 12:(c + 1) * 12, :],
              wo[c * 1536:(c + 1) * 1536, :].rearrange("(c p) n -> p c n", p=128), (), [wotb])
    gc = k.sb("gcB", [128, 256], F32)
    gcb = Buf("gcB")
    k.dma("sp", gc, ngc.partition_broadcast(128), (), [gcb])
    ssq = k.sb("ssqB", [128, n_tiles], F32)
    ssqb = Buf("ssqB")
    k.memset("dve", ssq, 0.0, [ssqb])
    ygr = Ring(k, "ygtB", [128, 4, 512], BF16, 3)
    ygmr = Ring(k, "ygmB", [128, 4, 256], BF16, 3)
    xr = Ring(k, "xcB", [128, 256], F32, 3)
    hr = Ring(k, "h1sB", [128, 256], F32, 2)
    hgr = Ring(k, "h1gB", [128, 256], BF16, 2)
    hTr = Ring(k, "h1gTB", [128, 2, 128], BF16, 2)
    junk = k.sb("junkB", [128, 256], BF16)
    junkb = Buf("junkB")
    yv = ygall.rearrange("(c g ii) p n -> c ii p g n", g=4, ii=4)
    ymv = ygmall.rearrange("(c g ii) p n -> c ii p g n", g=4, ii=4)
    ld = {}

    def load(i):
        if i < n_tiles:
            yt, yb = ygr.next()
            xt, xb = xr.next()
            ym, ymb = ygmr.next()
            k.dma("sp", ym, ymv[i // 4][i % 4], [ygmall_cb[i // 4]], [ymb])
            k.dma("sp", yt, yv[i // 4][i % 4], [ygall_cb[i // 4]], [yb])
            k.dma("sp", xt, xc[i * 128:(i + 1) * 128, :], (), [xb])
            ld[i] = (yt, yb, xt, xb, ym, ymb)

    pps = {}

    def proj(i):
        if i < n_tiles:
            yt, yb, xt, xb, ym, ymb = ld[i]
            bank = 1 + (i % 2)
            pps[i] = (colproj(k, P, yt, yb, ym, ymb, wot, wotb, bank), bank)

    load(0)
    load(1)
    proj(0)
    for i in range(n_tiles):
        load(i + 2)
        proj(i + 1)
        yt, yb, xt, xb, ym, ymb = ld.pop(i)
        pp, bank = pps.pop(i)
        h, hb = hr.next()
        k.tt("dve", h, pp, xt, ALU.add, [P.bufs[bank], xb], [hb])
        k.dma("sp", h1s[i], h, [hb], [h1s_b[i]])
        k.act(junk, h, AF.Square, [hb], [junkb, ssqb], accum_out=ssq[:, i:i + 1])
        hg, hgb = hgr.next()
        k.tt("dve", hg, h, gc, ALU.mult, [hb, gcb], [hgb])
        hT, hTb = hTr.next()
        pT = P.bf16(3 + (i % 2))
        for c in range(2):
            k.tr(pT[:, c * 128:(c + 1) * 128], hg[:, c * 128:(c + 1) * 128], ident, [hgb, identb],
                 [P.bufs[3 + (i % 2)]], inc=(c == 1))
        k.cp("act", hT, pT[:, 0:256].rearrange("p (c t) -> p c t", c=2), [P.bufs[3 + (i % 2)]], [hTb])
        k.dma("sp", h1gt[i], hT.rearrange("p c t -> p (c t)"), [hTb], [h1gt_tb[i]])
        gather_h1(i)
    return ssq, ssqb


def phase_B2(k, P, nc, ident, identb, T, n_tiles, rstd, rstdb):
    hall, hall_cb = T["h1gtall"], T["h1gtall_cb"]
    w, mem, mng, wkv, dcs = T["wB"], T["memB"], T["mngB"], T["wkvB"], T["dcsB"]
    scr1, scr1_b = T["scr1"], T["scr1_b"]
    ogm, ogt_mb = T["ogm"], T["ogt_mb"]
    gather_ogm = T["gather_ogm"]
    wt = k.sb("wtB2", [128, 8, 2560], BF16)
    wtb = Buf("wtB2")
    for c in range(5):
        k.dma("pool", wt[:, :, c * 512:(c + 1) * 512],
              w[:, c * 512:(c + 1) * 512].rearrange("(c p) n -> p c n", p=128), (), [wtb])
    gt = k.sb("gtB2", [128, D], F32)
    gb = Buf("gtB2")
    scr = (k.sb("junkB2", [128, D], BF16), Buf(), k.sb("ssB2", [128, 1], F32), Buf(),
           k.sb("rstdB2", [128, 1], F32), Buf(), k.sb("hnB2", [128, D], BF16), Buf())
    mkT, mkTb, mva, mvab = mem_prep(k, P, nc, mem, mng, wkv, 1, ident, identb, scr, gt, gb)
    rg = dict(
        hn=Ring(k, "qhn", [128, 4, 256], BF16, 3), rp=Ring(k, "qrp", [128, 32], F32, 5),
        rot=Ring(k, "qrot", [128, 8, 32], F32, 3), qkr=Ring(k, "qqkr", [128, 8, 128], BF16, 4),
        st=Ring(k, "qst", [128, 2048], BF16, 5), qm=Ring(k, "qqm", [128, 256], BF16, 3),
        sgm=Ring(k, "qsgm", [128, 256], F32, 8), qmT=Ring(k, "qqmT", [128, 2, 128], BF16, 3),
        E=Ring(k, "qE", [128, 2, 128], BF16, 3), mgt=Ring(k, "qmgt", [128, 256], BF16, 3),
        mgT=Ring(k, "qmgT", [128, 2, 128], BF16, 3), rs=Ring(k, "qrs", [128, 1], F32, 3),
        t1=Ring(k, "qt1", [128, 8, 16], F32, 2), t2=Ring(k, "qt2", [128, 8, 16], F32, 2),
    )
    hv = hall.rearrange("(c g ii) p n -> c ii p g n", g=4, ii=4)
    tl = {}

    def g(i, name):
        d = tl.setdefault(i, {})
        if name not in d:
            d[name] = rg[name].next()
        return d[name]

    def f_load(i):
        hn, hnb = g(i, "hn")
        rp, rpb = g(i, "rp")
        k.dma("sp", hn, hv[i // 4][i % 4], [hall_cb[i // 4]], [hnb])
        k.dma("sp", rp, dcs[i * 128:(i + 1) * 128, :], (), [rpb])

    def f_proj(i):
        hn, hnb = g(i, "hn")
        for gi in range(5):
            pp = P.f32(gi)
            for c in range(8):
                k.mm(pp, hn[:, c // 2, (c % 2) * 128:(c % 2 + 1) * 128], wt[:, c, gi * 512:(gi + 1) * 512],
                     c == 0, c == 7, [hnb, wtb], [P.bufs[gi]])

    def f_evac(i):
        r = rstd[:, i:i + 1]
        st, stb = g(i, "st")
        rot, rotb = g(i, "rot")
        qkr, qkrb = g(i, "qkr")
        qm, qmb = g(i, "qm")
        sgm, sgmb = g(i, "sgm")
        for gi in range(2):
            p3 = P.f32(gi).rearrange("p (u d) -> p u d", u=4)
            k.act(rot[:, gi * 4:(gi + 1) * 4, :], p3[:, :, 0:32], AF.Copy, [P.bufs[gi], rstdb], [rotb], scale=r)
            k.act(qkr[:, gi * 4:(gi + 1) * 4, :], p3, AF.Copy, [P.bufs[gi], rstdb], [qkrb], scale=r)
        k.act(st[:, 1024:1536], P.f32(2), AF.Copy, [P.bufs[2], rstdb], [stb], scale=r)
        k.act(st[:, 1536:2048], P.f32(3), AF.Silu, [P.bufs[3], rstdb], [stb], scale=r)
        p4 = P.f32(4)
        k.ts("dve", qm, p4[:, 0:256], r, None, ALU.mult, None, [P.bufs[4], rstdb], [qmb])
        k.act(sgm, p4[:, 256:512], AF.Silu, [P.bufs[4], rstdb], [sgmb], scale=r)

    def f_rot(i):
        rot, rotb = g(i, "rot")
        qkr, qkrb = g(i, "qkr")
        rp, rpb = g(i, "rp")
        t1, t1b = rg["t1"].next()
        t2, t2b = rg["t2"].next()
        x1 = rot[:, :, 0:16]
        x2 = rot[:, :, 16:32]
        cosb = rp[:, 0:16].unsqueeze(1).to_broadcast([128, 8, 16])
        sinb = rp[:, 16:32].unsqueeze(1).to_broadcast([128, 8, 16])
        k.tt("dve", t1, x1, cosb, ALU.mult, [rotb, rpb], [t1b])
        k.tt("dve", t2, x2, sinb, ALU.mult, [rotb, rpb], [t2b])
        k.tt("dve", qkr[:, :, 0:16], t1, t2, ALU.subtract, [t1b, t2b], [qkrb])
        k.tt("dve", t1, x1, sinb, ALU.mult, [rotb, rpb], [t1b])
        k.tt("dve", t2, x2, cosb, ALU.mult, [rotb, rpb], [t2b])
        k.tt("dve", qkr[:, :, 16:32], t1, t2, ALU.add, [t1b, t2b], [qkrb])
        qm, qmb = g(i, "qm")
        pT = P.bf16(6)
        for dc in range(2):
            k.tr(pT[:, 768 + dc * 128:768 + (dc + 1) * 128], qm[:, dc * 128:(dc + 1) * 128], ident, [qmb, identb],
                 [P.bufs[6]], inc=(dc == 1))

    def f_qktr(i):
        qkr, qkrb = g(i, "qkr")
        qmT, qmTb = g(i, "qmT")
        k.cp("act", qmT, P.bf16(6)[:, 768:1024].rearrange("p (c t) -> p c t", c=2), [P.bufs[6]], [qmTb])
        pT = P.bf16(5)
        for u in range(8):
            k.tr(pT[:, u * 128:(u + 1) * 128], qkr[:, u, :], ident, [qkrb, identb], [P.bufs[5]], inc=(u == 7))

    def f_store(i):
        st, stb = g(i, "st")
        k.cp("dve", st[:, 0:1024], P.bf16(5)[:, 0:1024], [P.bufs[5]], [stb])
        k.dma("sp", scr1[i], st, [stb], [scr1_b[i]])
        qmT, qmTb = g(i, "qmT")
        pS = P.f32(7)
        for mt in range(2):
            for dc in range(2):
                k.mm(pS[:, mt * 128:(mt + 1) * 128], mkT[:, 0][:, dc, mt * 128:(mt + 1) * 128], qmT[:, dc, :],
                     dc == 0, dc == 1, [mkTb, qmTb], [P.bufs[7]], inc=(mt == 1 and dc == 1))

    def f_exp(i):
        E, Eb = g(i, "E")
        k.act(E, P.f32(7)[:, 0:256].rearrange("p (c t) -> p c t", c=2), AF.Exp, [P.bufs[7]], [Eb], scale=1.0 / 16.0)

    def f_av(i):
        E, Eb = g(i, "E")
        pO = P.f32(6)
        for mt in range(2):
            k.mm(pO[:, 0:257], E[:, mt, :], mva[:, 0][:, mt, 0:257], mt == 0, mt == 1, [Eb, mvab], [P.bufs[6]])

    def f_mnorm(i):
        rs, rsb = g(i, "rs")
        sgm, sgmb = g(i, "sgm")
        mgt, mgtb = g(i, "mgt")
        pO = P.f32(6)
        k.recip(rs, pO[:, 256:257], [P.bufs[6]], [rsb])
        k.stt(mgt, pO[:, 0:256], rs, sgm, ALU.mult, ALU.mult, [P.bufs[6], rsb, sgmb], [mgtb])

    def f_mtr(i):
        mgt, mgtb = g(i, "mgt")
        pT = P.bf16(7)
        for ec in range(2):
            k.tr(pT[:, 512 + ec * 128:512 + (ec + 1) * 128], mgt[:, ec * 128:(ec + 1) * 128], ident,
                 [mgtb, identb], [P.bufs[7]], inc=(ec == 1))

    def f_mout(i):
        mgT, mgTb = g(i, "mgT")
        k.cp("act", mgT, P.bf16(7)[:, 512:768].rearrange("p (c t) -> p c t", c=2), [P.bufs[7]], [mgTb])
        k.dma("sp", ogm[i], mgT.rearrange("p c t -> p (c t)"), [mgTb], [ogt_mb[i]])
        gather_ogm(i)
        tl.pop(i, None)

    stages = [(0, f_load), (1, f_proj), (2, f_evac), (3, f_rot), (4, f_qktr), (5, f_store), (6, f_exp), (7, f_av),
              (8, f_mnorm), (9, f_mtr), (10, f_mout)][::-1]
    maxoff = 10
    for t in range(n_tiles + maxoff):
        for off, fn in stages:
            i = t - off
            if 0 <= i < n_tiles:
                fn(i)


def phase_C(k, P, nc, ident, identb, T, n_tiles):
    scr1, scr1_b = T["scr1"], T["scr1_b"]
    ogt, ogt_yb = T["ogt"], T["ogt_yb"]
    gather_og = T["gather_og"]
    lam4, slg = T["lamC"], T["slgC"]
    n_all = n_tiles
    nq = n_tiles // 4
    lv = k.sb("lv", [128, 4, 128], F32)
    lvb = Buf("lv")
    k.dma("sp", lv, lam4.rearrange("a d -> (a d)").partition_broadcast(128).rearrange("p (a d) -> p a d", a=4), (), [lvb])
    lp = k.sb("lp", [128, 2, 128], F32)
    k.tt("dve", lp[:, 0, :], lv[:, 0, :], lv[:, 1, :], ALU.mult, [lvb], [lvb])
    k.tt("dve", lp[:, 1, :], lv[:, 2, :], lv[:, 3, :], ALU.mult, [lvb], [lvb])
    ls = k.sb("ls", [128, 2], F32)
    k.op("dve", lambda e: e.tensor_reduce(out=ls, in_=lp, axis=AX.X, op=ALU.add), [lvb], [lvb])
    k.act(ls, ls, AF.Exp, [lvb], [lvb])
    nlam = k.sb("nlam", [128, 1], F32)
    nlamb = Buf("nlam")
    k.tt("dve", nlam, ls[:, 1:2], ls[:, 0:1], ALU.subtract, [lvb], [nlamb])
    k.ts("dve", nlam, nlam, -LAMBDA_INIT, None, ALU.add, None, [nlamb], [nlamb])
    sgain = k.sb("sgain", [128, 256], F32)
    sgainb = Buf("sgain")
    k.dma("sp", sgain, slg.partition_broadcast(128), (), [sgainb])
    k.ts("dve", sgain, sgain, 1.0 - LAMBDA_INIT, None, ALU.mult, None, [sgainb], [sgainb])

    kc = [k.sb(f"kc{c}", [128, n_all, 128], BF16) for c in range(2)]
    kcb = [Buf("kc0"), Buf("kc1")]
    va = k.sb("va", [128, n_all, 258], BF16)
    vab = Buf("va")
    k.memset("dve", va, 1.0, [vab])
    qtr = Ring(k, "qtile", [128, 2, 4, 128], BF16, 2)
    sgr = Ring(k, "sgt", [128, 4, 256], BF16, 2)
    er = Ring(k, "E", [128, 512], BF16, 3)
    on = k.sb("on", [128, 2, 4, 256], F32)
    onb = Buf("on")
    rs = k.sb("rs", [128, 4], F32)
    rsb = Buf("rs")
    ssq = k.sb("ssq", [128, 4], F32)
    ssqb = Buf("ssq")
    junk = k.sb("junkc", [128, 256], BF16)
    junkb = Buf("junkc")
    ogtile = k.sb("ogtile", [128, 4, 256], BF16)
    ogtileb = Buf("ogtile")
    ogTr = Ring(k, "ogT", [128, 4, 2, 128], BF16, 2)
    scale = 128.0 ** -0.5
    sbank = 0
    allscr = list(scr1_b)
    for hh in range(2):
        for c in range(2):
            u = hh * 2 + c
            for i0 in range(0, n_all, 16):
                i1 = min(n_all, i0 + 16)
                k.dma("sp", kc[c][:, i0:i1, :],
                      scr1[i0:i1, :, 512 + u * 128:512 + (u + 1) * 128].rearrange("i p t -> p i t"),
                      allscr[i0:i1], [kcb[c]])
        for i0 in range(0, n_all, 16):
            i1 = min(n_all, i0 + 16)
            k.dma("sp", va[:, i0:i1, 0:256],
                  scr1[i0:i1, :, 1024 + hh * 256:1024 + (hh + 1) * 256].rearrange("i p e -> p i e"),
                  allscr[i0:i1], [vab])
        ld = {}

        def load(q):
            if q < nq:
                qt, qtb = qtr.next()
                sgt, sgtb = sgr.next()
                for c in range(2):
                    u = hh * 2 + c
                    k.dma("sp", qt[:, c], scr1[q * 4:(q + 1) * 4, :, u * 128:(u + 1) * 128].rearrange("i p t -> p i t"),
                          allscr[q * 4:(q + 1) * 4], [qtb])
                k.dma("sp", sgt, scr1[q * 4:(q + 1) * 4, :, 1536 + hh * 256:1536 + (hh + 1) * 256].rearrange("i p e -> p i e"),
                      allscr[q * 4:(q + 1) * 4], [sgtb])
                ld[q] = (qt, qtb, sgt, sgtb)

        load(0)
        for q in range(nq):
            load(q + 1)
            qt, qtb, sgt, sgtb = ld.pop(q)
            items = [(c, kt_i) for c in range(2) for kt_i in range(n_all)]
            pend = {}

            def emit_S(c, kt_i):
                nonlocal sbank
                sbank = (sbank + 1) % 3
                pS = P.f32(sbank)
                k.mm(pS, kc[c][:, kt_i, :], qt[:, c].rearrange("p a t -> p (a t)"), True, True, [kcb[c], qtb],
                     [P.bufs[sbank]])
                pend[(c, kt_i)] = sbank

            emit_S(*items[0])
            emit_S(*items[1])
            for idx, (c, kt_i) in enumerate(items):
                if idx + 2 < len(items):
                    emit_S(*items[idx + 2])
                sb_ = pend.pop((c, kt_i))
                E, Eb = er.next()
                k.act(E, P.f32(sb_), AF.Exp, [P.bufs[sb_]], [Eb], scale=scale)
                for qs in range(4):
                    k.mm(P.f32(3 + qs)[:, 0:257], E[:, qs * 128:(qs + 1) * 128], va[:, kt_i, 0:257],
                         kt_i == 0, kt_i == n_all - 1, [Eb, vab], [P.bufs[3 + qs]], inc=(qs == 3))
                if kt_i == n_all - 1:
                    for qs in range(4):
                        pO = P.f32(3 + qs)
                        k.recip(rs[:, qs:qs + 1], pO[:, 256:257], [P.bufs[3 + qs]], [rsb])
                        k.ts("dve", on[:, c, qs, :], pO[:, 0:256], rs[:, qs:qs + 1], None, ALU.mult, None,
                             [P.bufs[3 + qs], rsb], [onb])
            on0 = on[:, 0].rearrange("p a e -> p (a e)")
            on1 = on[:, 1].rearrange("p a e -> p (a e)")
            k.stt(on0, on1, nlam, on0, ALU.mult, ALU.add, [onb, nlamb], [onb])
            for qs in range(4):
                k.act(junk, on[:, 0, qs, :], AF.Square, [onb], [junkb, ssqb], accum_out=ssq[:, qs:qs + 1])
            k.act(ssq, ssq, AF.Sqrt, [ssqb], [ssqb], scale=1.0 / 256, bias=EPS)
            k.recip(ssq, ssq, [ssqb], [ssqb])
            for qs in range(4):
                k.stt(on[:, 0, qs, :], on[:, 0, qs, :], ssq[:, qs:qs + 1], sgain, ALU.mult, ALU.mult,
                      [onb, ssqb, sgainb], [onb])
            k.tt("dve", ogtile.rearrange("p a e -> p (a e)"), on0, sgt.rearrange("p a e -> p (a e)"), ALU.mult,
                 [onb, sgtb], [ogtileb])
            ogT, ogTb = ogTr.next()
            pT = P.bf16(7)
            for qs in range(4):
                for ec in range(2):
                    k.tr(pT[:, (qs * 2 + ec) * 128:(qs * 2 + ec + 1) * 128], ogtile[:, qs, ec * 128:(ec + 1) * 128],
                         ident, [ogtileb, identb], [P.bufs[7]], inc=(qs == 3 and ec == 1))
            k.cp("act", ogT.rearrange("p a c t -> p (a c t)"), pT[:, 0:1024], [P.bufs[7]], [ogTb])
            k.dma("sp", ogt[q * 4:(q + 1) * 4, :, hh * 256:(hh + 1) * 256].rearrange("i p n -> p i n"),
                  ogT.rearrange("p a c t -> p a (c t)"), [ogTb], [ogt_yb[q][hh]])
            if hh == 1:
                gather_og(q)


def phase_D(k, P, nc, ident, identb, T, n_tiles, GROUPS):
    ogall, ogall_cb = T["ogtall"], T["ogtall_cb"]
    ogmall, ogmall_cb = T["ogmall"], T["ogmall_cb"]
    wo, fgc = T["woD"], T["fgcD"]
    h1s, h1s_b = T["h1s"], T["h1s_b"]
    out = T["out"]
    wot = k.sb("wotD", [128, 24, 256], BF16)
    wotb = Buf("wotD")
    for c in range(2):
        k.dma("pool", wot[:, c * 12:(c + 1) * 12, :],
              wo[c * 1536:(c + 1) * 1536, :].rearrange("(c p) n -> p c n", p=128), (), [wotb])
    gc = k.sb("gcD", [128, 256], F32)
    gcb = Buf("gcD")
    k.dma("sp", gc, fgc.partition_broadcast(128), (), [gcb])
    ssq = k.sb("ssqD", [128, n_tiles], F32)
    ssqb = Buf("ssqD")
    k.memset("dve", ssq, 0.0, [ssqb])
    h2 = k.sb("h2D", [128, n_tiles, 256], F32)
    h2b = [Buf(f"h2_{i}") for i in range(n_tiles)]
    ogr = Ring(k, "ogD", [128, 4, 512], BF16, 3)
    ogmr = Ring(k, "ogmD", [128, 4, 256], BF16, 3)
    hr = Ring(k, "h1D", [128, 256], F32, 3)
    junk = k.sb("junkD", [128, 256], BF16)
    junkb = Buf("junkD")
    ov = ogall.rearrange("(c g ii) p n -> c ii p g n", g=4, ii=4)
    omv = ogmall.rearrange("(c g ii) p n -> c ii p g n", g=4, ii=4)
    ld = {}

    def load(i):
        if i < n_tiles:
            og, ogb = ogr.next()
            h, hb = hr.next()
            om, omb = ogmr.next()
            k.dma("sp", om, omv[i // 4][i % 4], [ogmall_cb[i // 4]], [omb])
            k.dma("sp", og, ov[i // 4][i % 4], [ogall_cb[i // 4]], [ogb])
            k.dma("sp", h, h1s[i], [h1s_b[i]], [hb])
            ld[i] = (og, ogb, h, hb, om, omb)

    pps = {}

    def proj(i):
        if i < n_tiles:
            og, ogb, h, hb, om, omb = ld[i]
            bank = 1 + (i % 2)
            pps[i] = (colproj(k, P, og, ogb, om, omb, wot, wotb, bank), bank)

    load(0)
    load(1)
    proj(0)
    for i in range(n_tiles):
        load(i + 2)
        proj(i + 1)
        og, ogb, h, hb, om, omb = ld.pop(i)
        pp, bank = pps.pop(i)
        k.tt("dve", h2[:, i, :], pp, h, ALU.add, [P.bufs[bank], hb], [h2b[i]])
        k.act(junk, h2[:, i, :], AF.Square, [h2b[i]], [junkb, ssqb], accum_out=ssq[:, i:i + 1])
    rstd, rstdb = sumsq_gather(k, P, nc, ssq, ssqb, T["ssq2"], T["ssq2_b"], T["ssq2all"], T["ssq2all_b"], GROUPS,
                               D, "D")
    outr = Ring(k, "outD", [128, 256], F32, 3)
    for i in range(n_tiles):
        o, ob = outr.next()
        k.stt(o, h2[:, i, :], rstd[:, i:i + 1], gc, ALU.mult, ALU.mult, [h2b[i], rstdb, gcb], [ob])
        k.dma("sp", out[i * 128:(i + 1) * 128, :], o, [ob], (), is_output=True)


def build_fused(n_tiles=128, groups=None):
    GROUPS = groups or [[0, 1, 2, 3], [4, 5, 6, 7]]
    nc = bass.Bass("TRN2", target_bir_lowering=False)
    S_ = n_tiles * 128
    T = {}

    def ext(name, shape, dt=F32):
        T[name] = nc.dram_tensor(name, list(shape), dt, kind="ExternalInput").ap()

    def scratch(name, shape, dt):
        T[name] = nc.dram_tensor(name, list(shape), dt).ap()
        T[name + "_b"] = Buf(name)

    ext("xA", [S_, D]); ext("ngA", [D]); ext("wA", [D, 2048]); ext("memA", [256, D]); ext("mngA", [D])
    ext("wkvA", [D, 512]); ext("decA", [4]); ext("rcsA", [S_, 128])
    ext("xcB", [S_, 256]); ext("woB", [3072, 256]); ext("ngcB", [256])
    ext("wB", [D, 2560]); ext("memB", [256, D]); ext("mngB", [D]); ext("wkvB", [D, 512]); ext("dcsB", [S_, 32])
    ext("lamC", [4, 128]); ext("slgC", [256]); ext("woD", [3072, 256]); ext("fgcD", [256])
    T["out"] = nc.dram_tensor("out", [S_, 256], F32, kind="ExternalOutput").ap()
    scratch("yg", [n_tiles, 128, 512], BF16)
    scratch("ygall", [4 * n_tiles, 128, 512], BF16)
    scratch("ygm", [n_tiles, 128, 256], BF16)
    scratch("ygmall", [4 * n_tiles, 128, 256], BF16)
    T["scrA"] = nc.dram_tensor("scrA", [n_tiles, 128, 2048], BF16).ap()
    T["sbsA"] = nc.dram_tensor("sbsA", [n_tiles, 128, 512], BF16).ap()
    T["h1s"] = nc.dram_tensor("h1s", [n_tiles, 128, 256], F32).ap()
    T["h1s_b"] = [Buf(f"h1s{i}") for i in range(n_tiles)]
    scratch("h1gt", [n_tiles, 128, 256], BF16)
    scratch("h1gtall", [4 * n_tiles, 128, 256], BF16)
    scratch("ssq1", [128, n_tiles], F32)
    scratch("ssq1all", [4 * 128, n_tiles], F32)
    T["scr1"] = nc.dram_tensor("scr1", [n_tiles, 128, 2048], BF16).ap()
    T["scr1_b"] = [Buf(f"scr1_{i}") for i in range(n_tiles)]
    scratch("ogt", [n_tiles, 128, 512], BF16)
    scratch("ogtall", [4 * n_tiles, 128, 512], BF16)
    scratch("ogm", [n_tiles, 128, 256], BF16)
    scratch("ogmall", [4 * n_tiles, 128, 256], BF16)
    scratch("ssq2", [128, n_tiles], F32)
    scratch("ssq2all", [4 * 128, n_tiles], F32)

    k = K(nc)
    P = Psum(k)
    ident, identb, io, iob = make_ident(k)

    def flat(ap):
        return ap.rearrange("i p n -> (i p) n")

    nch = n_tiles // 4

    def mk_gather(src, dst, dst_cb, rd, lag):
        done = set()

        def emit(c):
            if c in done or c >= nch or c < 0:
                return
            done.add(c)
            k.collective("AllGather", flat(T[src][c * 4:(c + 1) * 4]), flat(T[dst][c * 16:(c + 1) * 16]), GROUPS,
                         rd(c), [dst_cb[c]])

        def f(i, final=False):
            if final:
                for c in range(nch):
                    emit(c)
            else:
                c = (i - 3 - lag) // 4
                if (i - 3 - lag) % 4 == 0:
                    emit(c)
        return f

    T["yg_mb"] = [Buf() for _ in range(n_tiles)]
    T["yg_yb"] = [Buf() for _ in range(n_tiles)]
    T["ygall_cb"] = [Buf() for _ in range(nch)]
    T["gather_yg"] = mk_gather("yg", "ygall", T["ygall_cb"], lambda c: T["yg_yb"][c * 4:(c + 1) * 4], 1)
    T["ygmall_cb"] = [Buf() for _ in range(nch)]
    T["gather_ygm"] = mk_gather("ygm", "ygmall", T["ygmall_cb"], lambda c: T["yg_mb"][c * 4:(c + 1) * 4], 2)
    T["h1gt_tb"] = [Buf() for _ in range(n_tiles)]
    T["h1gtall_cb"] = [Buf() for _ in range(nch)]
    T["gather_h1"] = mk_gather("h1gt", "h1gtall", T["h1gtall_cb"], lambda c: T["h1gt_tb"][c * 4:(c + 1) * 4], 2)
    T["ogt_mb"] = [Buf() for _ in range(n_tiles)]
    T["ogt_yb"] = [[Buf(), Buf()] for _ in range(nch)]
    T["ogtall_cb"] = [Buf() for _ in range(nch)]
    g_og = mk_gather("ogt", "ogtall", T["ogtall_cb"], lambda c: T["ogt_yb"][c], 0)
    T["ogmall_cb"] = [Buf() for _ in range(nch)]
    T["gather_ogm"] = mk_gather("ogm", "ogmall", T["ogmall_cb"], lambda c: T["ogt_mb"][c * 4:(c + 1) * 4], 2)
    T["gather_og"] = lambda q: g_og(q * 4 + 3 - 4) if q > 0 else None

    k.phase_begin()
    phase_A(k, P, nc, ident, identb, io, iob, T, n_tiles)
    T["gather_ygm"](0, final=True)
    T["gather_yg"](0, final=True)
    k.phase_end()

    k.phase_begin()
    ssq, ssqb = phase_B(k, P, nc, ident, identb, T, n_tiles)
    T["gather_h1"](0, final=True)
    rstd, rstdb = sumsq_gather(k, P, nc, ssq, ssqb, T["ssq1"], T["ssq1_b"], T["ssq1all"], T["ssq1all_b"], GROUPS,
                               D, "B")
    phase_B2(k, P, nc, ident, identb, T, n_tiles, rstd, rstdb)
    T["gather_ogm"](0, final=True)
    k.phase_end()

    k.phase_begin()
    phase_C(k, P, nc, ident, identb, T, n_tiles)
    g_og(0, final=True)
    k.phase_end()

    k.phase_begin()
    phase_D(k, P, nc, ident, identb, T, n_tiles, GROUPS)
    k.finish()
    return nc, k


S = 16384


def rope_tab(seq, dim, theta):
    inv = (1.0 / (np.float32(theta) ** (np.arange(0, dim, 2, dtype=np.float32) / np.float32(dim)))).astype(np.float32)
    ang = (np.arange(seq, dtype=np.float32)[:, None] * inv[None, :]).astype(np.float32)
    return np.concatenate([np.cos(ang), np.sin(ang)], -1).astype(np.float32)


def prep_A(inp, b, g, S_=S):
    w = inp["ret_w_in"][0]
    hs = [2 * g, 2 * g + 1]
    cols = []
    for base, wd in ((0, 128), (1024, 128), (2048, 256), (4096, 256)):
        for h in hs:
            cols.append(np.arange(base + h * wd, base + (h + 1) * wd))
    cols.append(np.arange(6144 + g * 256, 6144 + (g + 1) * 256))
    cols.append(np.arange(7168 + g * 256, 7168 + (g + 1) * 256))
    cols = np.concatenate(cols)
    wkv = inp["mem_w_kv"][0]
    wkvc = np.concatenate([np.arange(g * 256, (g + 1) * 256), np.arange(1024 + g * 256, 1024 + (g + 1) * 256)])
    dec = np.array([inp["ret_decay_fwd"][0, hs[0]], inp["ret_decay_fwd"][0, hs[1]],
                    inp["ret_decay_bwd"][0, hs[0]], inp["ret_decay_bwd"][0, hs[1]]], np.float32)
    return {
        "xA": np.ascontiguousarray(inp["x"][b, :S_]),
        "ngA": np.ascontiguousarray(inp["norm_g"][0]),
        "wA": np.ascontiguousarray(w[:, cols]),
        "memA": np.ascontiguousarray(inp["mem"][b]),
        "mngA": np.ascontiguousarray(inp["mem_norm_g"][0]),
        "wkvA": np.ascontiguousarray(wkv[:, wkvc]),
        "decA": dec,
        "rcsA": rope_tab(S_, 128, 10000.0),
    }


def prep_fused(inp, core, S_=16384):
    b, j = core // 4, core % 4
    d = prep_A(inp, b, j, S_)
    rows = []
    for g in range(4):
        rows.append(np.arange(g * 512, (g + 1) * 512))
        rows.append(np.arange(2048 + g * 256, 2048 + (g + 1) * 256))
    rows = np.concatenate(rows)
    cs = slice(j * 256, (j + 1) * 256)
    w = inp["diff_w_in"][0]
    hs = [2 * j, 2 * j + 1]
    cols = []
    for base in (0, 2048, 4096, 6144):
        for h in hs:
            cols.append(np.arange(base + h * 256, base + (h + 1) * 256))
    cols.append(np.arange(8192 + j * 256, 8192 + (j + 1) * 256))
    cols.append(np.arange(9216 + j * 256, 9216 + (j + 1) * 256))
    cols = np.concatenate(cols)
    wkv = inp["mem_w_kv"][1]
    kvc = np.concatenate([np.arange(j * 256, (j + 1) * 256), np.arange(1024 + j * 256, 1024 + (j + 1) * 256)])
    d.update({
        "xcB": np.ascontiguousarray(inp["x"][b, :S_, cs]),
        "woB": np.ascontiguousarray(inp["ret_w_out"][0][rows][:, cs]),
        "ngcB": np.ascontiguousarray(inp["norm_g"][1][cs]),
        "wB": np.ascontiguousarray(w[:, cols]),
        "memB": np.ascontiguousarray(inp["mem"][b]),
        "mngB": np.ascontiguousarray(inp["mem_norm_g"][1]),
        "wkvB": np.ascontiguousarray(wkv[:, kvc]),
        "dcsB": np.ascontiguousarray(rope_tab(S_, 32, 500000.0)),
        "lamC": np.ascontiguousarray(np.stack([inp["diff_lambda_q1"][0], inp["diff_lambda_k1"][0],
                                               inp["diff_lambda_q2"][0], inp["diff_lambda_k2"][0]])),
        "slgC": np.ascontiguousarray(inp["diff_subln_g"][0]),
        "woD": np.ascontiguousarray(inp["diff_w_out"][0][rows][:, cs]),
        "fgcD": np.ascontiguousarray(inp["final_norm_g"][cs]),
    })
    return d


_CACHE = {}


def kernel(**inputs):
    inp = {k_: np.asarray(v) for k_, v in inputs.items()}
    cores = list(range(8))
    if "nc" not in _CACHE:
        _CACHE["nc"] = build_fused(128)[0]
    res = run_bass_kernel_spmd(_CACHE["nc"], [prep_fused(inp, c) for c in cores], core_ids=cores).results
    out = np.empty((2, 16384, 1024), np.float32)
    for c in cores:
        out[c // 4, :, (c % 4) * 256:(c % 4 + 1) * 256] = np.asarray(res[c]["out"])
    return out
```

```python
import numpy as np
import concourse.bass as bass
import concourse.mybir as mybir
from concourse.bass_utils import run_bass_kernel_spmd

F32 = mybir.dt.float32
BF16 = mybir.dt.bfloat16
I32 = mybir.dt.int32
AF = mybir.ActivationFunctionType
ALU = mybir.AluOpType
AX = mybir.AxisListType

SEM_ROLL = 30000


class Buf:
    __slots__ = ("name", "w", "r", "excl")

    def __init__(self, name="", excl=False):
        self.name = name
        self.excl = excl
        self.w = None
        self.r = {}


class Eng:
    def __init__(self, K, name, e):
        self.K = K
        self.name = name
        self.e = e
        self.sem = K.nc.alloc_semaphore(f"s_{name}_0")
        self.nsem = 1
        self.cnt = 0
        self.waited = {}

    def roll(self):
        if self.cnt >= SEM_ROLL:
            self.sem = self.K.nc.alloc_semaphore(f"s_{self.name}_{self.nsem}")
            self.nsem += 1
            self.cnt = 0


class K:
    def __init__(self, nc, n_dma_sems=12):
        self.nc = nc
        self.eng = {
            "pe": Eng(self, "pe", nc.tensor),
            "act": Eng(self, "act", nc.scalar),
            "dve": Eng(self, "dve", nc.vector),
            "pool": Eng(self, "pool", nc.gpsimd),
            "sp": Eng(self, "sp", nc.sync),
        }
        self.dsems = {}
        for q in ("sp", "act", "pool"):
            self.dsems[q] = [[nc.alloc_semaphore(f"d_{q}_{i}"), 0] for i in range(n_dma_sems)]
        self.dnext = {"sp": 0, "act": 0, "pool": 0}
        self.out_tokens = []
        self.n_instr = 0

    def _wait(self, E, tok):
        sem, val = tok
        key = id(sem)
        if E.waited.get(key, 0) < val:
            E.e.wait_ge(sem, val)
            E.waited[key] = val

    def _deps(self, E, reads, writes, skip_own=False):
        toks = []
        for b in reads:
            if b.w is not None:
                toks.append(b.w)
        for b in writes:
            if b.w is not None:
                toks.append(b.w)
            toks.extend(b.r.values())
        for t in toks:
            if skip_own and t[0] is E.sem:
                continue
            self._wait(E, t)

    def _mark(self, tok, reads, writes):
        for b in reads:
            cur = b.r.get(id(tok[0]))
            if cur is None or cur[1] < tok[1]:
                b.r[id(tok[0])] = tok
        for b in writes:
            b.w = tok
            b.r = {}

    def op(self, eng, fn, reads=(), writes=(), inc=True):
        E = self.eng[eng]
        if any(b.excl for b in reads):
            writes = list(writes) + [b for b in reads if b.excl]
            reads = [b for b in reads if not b.excl]
        self._deps(E, reads, writes, skip_own=(eng == "pe"))
        ins = fn(E.e)
        self.n_instr += 1
        if inc:
            E.cnt += 1
            ins.then_inc(E.sem, 1)
            tok = (E.sem, E.cnt)
        else:
            tok = (E.sem, E.cnt + 1)
        self._mark(tok, reads, writes)
        if inc:
            E.roll()
        return tok

    def dma(self, q, out, in_, reads=(), writes=(), is_output=False, **kw):
        E = self.eng[q]
        self._deps(E, reads, writes)
        pool = self.dsems[q]
        i = self.dnext[q]
        self.dnext[q] = (i + 1) % len(pool)
        slot = pool[i]
        if slot[1] > 0:
            self._wait(E, (slot[0], slot[1]))
        ins = E.e.dma_start(out=out, in_=in_, **kw)
        slot[1] += 16
        ins.then_inc(slot[0], 16)
        tok = (slot[0], slot[1])
        self.n_instr += 1
        self._mark(tok, reads, writes)
        if is_output:
            self.out_tokens.append(tok)
        return tok

    def finish(self):
        E = self.eng["sp"]
        for q in self.dsems:
            for slot in self.dsems[q]:
                if slot[1] > 0:
                    self._wait(E, (slot[0], slot[1]))
        for n in ("pe", "act", "dve", "pool"):
            e2 = self.eng[n]
            if e2.cnt > 0:
                self._wait(E, (e2.sem, e2.cnt))


def _sb(self, name, shape, dt=F32):
    stack = getattr(self, "_stack", None)
    if stack is None:
        return self.nc.alloc_sbuf_tensor(name, list(shape), dt).ap()
    self._nsb = getattr(self, "_nsb", 0) + 1
    h = stack.enter_context(self.nc.sbuf_tensor(f"{name}_{self._nsb}", list(shape), dt))
    return h.ap()


def _phase_begin(self):
    import contextlib
    self._stack = contextlib.ExitStack()


def _barrier(self):
    toks = []
    for n in ("pe", "act", "dve", "pool", "sp"):
        e2 = self.eng[n]
        if e2.cnt > 0:
            toks.append((e2.sem, e2.cnt))
    for q in self.dsems:
        for slot in self.dsems[q]:
            if slot[1] > 0:
                toks.append((slot[0], slot[1]))
    toks.extend(getattr(self, "cc_tokens", []))
    for n in ("pe", "act", "dve", "pool", "sp"):
        E = self.eng[n]
        for t in toks:
            if t[0] is E.sem:
                continue
            self._wait(E, t)


def _phase_end(self):
    self.barrier()
    self._stack.close()
    self._stack = None


K.phase_begin = _phase_begin
K.phase_end = _phase_end
K.barrier = _barrier


def _ps(self, name, shape, dt=F32):
    return self.nc.alloc_psum_tensor(name, list(shape), dt).ap()


K.sb = _sb
K.ps = _ps


def _mm(self, out, lhsT, rhs, start, stop, reads, writes, inc=None, **kw):
    if inc is None:
        inc = stop
    return self.op("pe", lambda e: e.matmul(out, lhsT=lhsT, rhs=rhs, start=start, stop=stop, **kw),
                   reads, writes, inc=inc)


def _tr(self, out, in_, ident, reads, writes, inc=True):
    return self.op("pe", lambda e: e.transpose(out=out, in_=in_, identity=ident), reads, writes, inc=inc)


def _act(self, out, in_, func, reads, writes, **kw):
    return self.op("act", lambda e: e.activation(out=out, in_=in_, func=func, **kw), reads, writes)


def _tt(self, eng, out, in0, in1, op, reads, writes):
    return self.op(eng, lambda e: e.tensor_tensor(out=out, in0=in0, in1=in1, op=op), reads, writes)


def _ts(self, eng, out, in0, s1, s2, op0, op1, reads, writes):
    if op1 is None and eng == "pool" and op0 == ALU.mult:
        op1, s2 = ALU.mult, 1.0
    if op1 is None:
        return self.op(eng, lambda e: e.tensor_scalar(out=out, in0=in0, scalar1=s1, scalar2=None, op0=op0),
                       reads, writes)
    return self.op(eng, lambda e: e.tensor_scalar(out=out, in0=in0, scalar1=s1, scalar2=s2, op0=op0, op1=op1),
                   reads, writes)


def _stt(self, out, in0, scalar, in1, op0, op1, reads, writes):
    return self.op("dve", lambda e: e.scalar_tensor_tensor(out=out, in0=in0, scalar=scalar, in1=in1,
                                                           op0=op0, op1=op1), reads, writes)


def _cp(self, eng, out, in_, reads, writes):
    if eng == "act":
        return self.op("act", lambda e: e.copy(out=out, in_=in_), reads, writes)
    return self.op(eng, lambda e: e.tensor_copy(out=out, in_=in_), reads, writes)


def _recip(self, out, in_, reads, writes):
    return self.op("dve", lambda e: e.reciprocal(out=out, in_=in_), reads, writes)


def _memset(self, eng, ap, val, writes):
    return self.op(eng, lambda e: e.memset(ap, val), (), writes)


K.mm = _mm
K.tr = _tr
K.act = _act
K.tt = _tt
K.ts = _ts
K.stt = _stt
K.cp = _cp
K.recip = _recip
K.memset = _memset


class Ring:
    def __init__(self, k, name, shape, dt, n):
        self.aps = [k.sb(f"{name}{i}", shape, dt) for i in range(n)]
        self.bufs = [Buf(f"{name}{i}") for i in range(n)]
        self.n = n
        self.i = -1

    def next(self):
        self.i = (self.i + 1) % self.n
        return self.aps[self.i], self.bufs[self.i]

    def cur(self):
        return self.aps[self.i], self.bufs[self.i]


class Psum:
    def __init__(self, k):
        self.banks = [k.ps(f"bank{i}", [128, 512], F32) for i in range(8)]
        self.bufs = [Buf(f"bank{i}", excl=True) for i in range(8)]

    def f32(self, i):
        return self.banks[i]

    def bf16(self, i):
        return self.banks[i].bitcast(BF16)


def _collective(self, kind, in_ap, out_ap, groups, reads=(), writes=()):
    E = self.eng["pool"]
    self._deps(E, reads, writes)
    if not hasattr(self, "cc_sem"):
        self.cc_sem = self.nc.alloc_semaphore("cc_sem")
        self.cc_cnt = 0
        self.cc_tokens = []
    ins = E.e.collective_compute(kind, ALU.bypass, replica_groups=groups, ins=[in_ap.opt()], outs=[out_ap.opt()])
    ins.then_inc(self.cc_sem)
    self.cc_cnt += 1
    tok = (self.cc_sem, self.cc_cnt)
    self.n_instr += 1
    self._mark(tok, reads, writes)
    self.cc_tokens = [tok]
    return tok


K.collective = _collective


S = 16384
D = 1024
NT = S // 128
EPS = 1e-6


def norm_to_T(k, P, xt, xb, gt, gb, ident, identb, n_feat, scr, hnT, hnTb, pbank):
    junk, junkb, ss, ssb, rstd, rstdb, hn, hnb = scr
    k.act(junk, xt, AF.Square, [xb], [junkb, ssb], accum_out=ss)
    k.act(rstd, ss, AF.Sqrt, [ssb], [rstdb], scale=1.0 / n_feat, bias=EPS)
    k.recip(rstd, rstd, [rstdb], [rstdb])
    k.stt(hn, xt, rstd, gt, ALU.mult, ALU.mult, [xb, rstdb, gb], [hnb])
    nch = n_feat // 128
    pT = P.bf16(pbank)
    for c in range(nch):
        k.tr(pT[:, c * 128:(c + 1) * 128], hn[:, c * 128:(c + 1) * 128], ident, [hnb, identb], [P.bufs[pbank]],
             inc=(c == nch - 1))
    k.cp("act", hnT, pT[:, 0:nch * 128].rearrange("p (c t) -> p c t", c=nch), [P.bufs[pbank]], [hnTb])


def make_ident(k):
    io = k.sb("io_id", [128, 128], F32)
    iob = Buf("io")
    idf = k.sb("identf", [128, 128], F32)
    ident = k.sb("ident", [128, 128], BF16)
    identb = Buf("ident")
    k.op("pool", lambda e: e.iota(io, pattern=[[1, 128]], base=0, channel_multiplier=-1,
                                  allow_small_or_imprecise_dtypes=True), (), [iob])
    k.ts("dve", idf, io, 0.0, None, ALU.is_equal, None, [iob], [iob])
    k.cp("dve", ident, idf, [iob], [identb])
    return ident, identb, io, iob


def mem_prep(k, P, nc, mem, mng, wkv, nheads, ident, identb, scr, gt, gb):
    mkT = k.sb("mkT", [128, nheads, 2, 256], BF16)
    mkTb = Buf("mkT")
    mva = k.sb("mva", [128, nheads, 2, 258], BF16)
    mvab = Buf("mva")
    mt_x = k.sb("mem_x", [128, D], F32)
    mt_xb = Buf("mem_x")
    mg = k.sb("mem_g", [128, D], F32)
    mgb = Buf("mem_g")
    mnT = k.sb("mnT", [128, 8, 128], BF16)
    mnTb = Buf("mnT")
    wk = k.sb("wkv_sb", [128, 8, 512], BF16)
    wkb = Buf("wkv_sb")
    mkt = k.sb("mk_tok", [128, 256], BF16)
    mktb = Buf("mk_tok")
    k.dma("sp", mg, mng.partition_broadcast(128), (), [mgb])
    k.memset("dve", mva, 1.0, [mvab])
    for hd in range(nheads):
        k.dma("pool", wk, wkv[:, hd * 512:(hd + 1) * 512].rearrange("(c p) n -> p c n", p=128), (), [wkb])
        for mt in range(2):
            k.dma("sp", mt_x, mem[mt * 128:(mt + 1) * 128, :], (), [mt_xb])
            norm_to_T(k, P, mt_x, mt_xb, mg, mgb, ident, identb, D, scr, mnT, mnTb, 0)
            pp = P.f32(1)
            for c in range(8):
                k.mm(pp, mnT[:, c, :], wk[:, c, :], c == 0, c == 7, [mnTb, wkb], [P.bufs[1]])
            k.cp("act", mkt, pp[:, 0:256], [P.bufs[1]], [mktb])
            k.cp("dve", mva[:, hd, mt, 0:256], pp[:, 256:512], [P.bufs[1]], [mvab])
            pT = P.bf16(0)
            for dc in range(2):
                k.tr(pT[:, dc * 128:(dc + 1) * 128], mkt[:, dc * 128:(dc + 1) * 128], ident, [mktb, identb],
                     [P.bufs[0]], inc=(dc == 1))
            k.cp("act", mkT[:, hd, :, mt * 128:(mt + 1) * 128],
                 pT[:, 0:256].rearrange("p (c t) -> p c t", c=2), [P.bufs[0]], [mkTb])
    return mkT, mkTb, mva, mvab


def mem_attn(k, P, qm_src, qm_srcb, sgm, sgmb, mkT_h, mkTb, mva_h, mvab, ident, identb, tmp, banks, out, outb):
    qmT, qmTb, E, Eb, rs, rsb = tmp
    bT, bS, bO = banks
    pT = P.bf16(bT)
    for dc in range(2):
        k.tr(pT[:, dc * 128:(dc + 1) * 128], qm_src[:, dc * 128:(dc + 1) * 128], ident, [qm_srcb, identb],
             [P.bufs[bT]], inc=(dc == 1))
    k.cp("act", qmT, pT[:, 0:256].rearrange("p (c t) -> p c t", c=2), [P.bufs[bT]], [qmTb])
    pS = P.f32(bS)
    for mt in range(2):
        for dc in range(2):
            k.mm(pS[:, mt * 128:(mt + 1) * 128], mkT_h[:, dc, mt * 128:(mt + 1) * 128], qmT[:, dc, :],
                 dc == 0, dc == 1, [mkTb, qmTb], [P.bufs[bS]], inc=(mt == 1 and dc == 1))
    k.act(E, pS[:, 0:256].rearrange("p (c t) -> p c t", c=2), AF.Exp, [P.bufs[bS]], [Eb], scale=1.0 / 16.0)
    pO = P.f32(bO)
    for mt in range(2):
        k.mm(pO[:, 0:257], E[:, mt, :], mva_h[:, mt, 0:257], mt == 0, mt == 1, [Eb, mvab], [P.bufs[bO]])
    k.recip(rs, pO[:, 256:257], [P.bufs[bO]], [rsb])
    k.stt(out, pO[:, 0:256], rs, sgm, ALU.mult, ALU.mult, [P.bufs[bO], rsb, sgmb], [outb])


def phase_A(k, P, nc, ident, identb, io, iob, T, n_tiles):
    x, ng, w, mem, mng, wkv, dec, rcs = (T[n] for n in ("xA", "ngA", "wA", "memA", "mngA", "wkvA", "decA", "rcsA"))
    yg, ygm, scrd, sbs = T["yg"], T["ygm"], T["scrA"], T["sbsA"]
    gather_ygm = T["gather_ygm"]
    yg_mb, yg_yb = T["yg_mb"], T["yg_yb"]
    gather_yg = T["gather_yg"]
    scr_b = [Buf(f"scr{i}") for i in range(n_tiles)]
    sbs_b = [Buf(f"sbs{i}") for i in range(n_tiles)]
    stop = 99
    wt = k.sb("wt", [128, 8, 2048], BF16)
    wtb = Buf("wt")
    for c4 in range(4):
        k.dma("pool", wt[:, :, c4 * 512:(c4 + 1) * 512],
              w[:, c4 * 512:(c4 + 1) * 512].rearrange("(c p) n -> p c n", p=128), (), [wtb])
    gt = k.sb("gt", [128, D], F32)
    gb = Buf("gt")
    k.dma("sp", gt, ng.partition_broadcast(128), (), [gb])

    def mkscr(tag, nf):
        return (k.sb("junk" + tag, [128, nf], F32), Buf(), k.sb("ss" + tag, [128, 1], F32), Buf(),
                k.sb("rstd" + tag, [128, 1], F32), Buf(), k.sb("hn" + tag, [128, nf], BF16), Buf())

    scr = mkscr("a", D)
    mkT, mkTb, mva, mvab = mem_prep(k, P, nc, mem, mng, wkv, 1, ident, identb, scr, gt, gb)

    lg = k.sb("lg", [128, 4], F32)
    lgb = Buf("lg")
    k.dma("sp", lg, dec.partition_broadcast(128), (), [lgb])
    k.act(lg, lg, AF.Exp, [lgb], [lgb], scale=-1.0)
    k.act(lg, lg, AF.Ln, [lgb], [lgb], bias=1.0)
    k.ts("dve", lg, lg, -1.0, None, ALU.mult, None, [lgb], [lgb])
    cb = Buf("consts")
    tmpc = k.sb("tmpc", [128, 128], F32)
    tmpm = k.sb("tmpm", [128, 128], F32)
    tmpe = k.sb("tmpe", [128, 128], F32)
    DT = k.sb("DT", [128, 2, 128], F32)
    QDF = k.sb("QDF", [128, 2, 128], F32)
    QDB = k.sb("QDB", [128, 2, 128], F32)
    kd = k.sb("kd", [128, 4], F32)
    cd = k.sb("cd", [128, 4], F32)
    ci = k.sb("ci", [128, 128], F32)
    cbk = k.sb("cbk", [128, 128], F32)
    pi = k.sb("pi", [128, 2], F32)
    for h in range(2):
        k.ts("dve", tmpc, io, 0.0, None, ALU.max, None, [iob], [cb])
        k.act(tmpe, tmpc, AF.Exp, [cb, lgb], [cb], scale=lg[:, h:h + 1])
        k.ts("dve", tmpm, io, 0.0, None, ALU.is_ge, None, [iob], [cb])
        k.tt("dve", DT[:, h, :], tmpe, tmpm, ALU.mult, [cb], [cb])
        k.ts("dve", tmpc, io, -1.0, 0.0, ALU.mult, ALU.max, [iob], [cb])
        k.act(tmpe, tmpc, AF.Exp, [cb, lgb], [cb], scale=lg[:, 2 + h:3 + h])
        k.ts("dve", tmpm, io, 0.0, None, ALU.is_lt, None, [iob], [cb])
        k.tt("dve", tmpe, tmpe, tmpm, ALU.mult, [cb], [cb])
        k.tt("dve", DT[:, h, :], DT[:, h, :], tmpe, ALU.add, [cb], [cb])
    k.op("pool", lambda e: e.iota(ci, pattern=[[1, 128]], base=1, channel_multiplier=0,
                                  allow_small_or_imprecise_dtypes=True), (), [cb])
    k.op("pool", lambda e: e.iota(cbk, pattern=[[-1, 128]], base=128, channel_multiplier=0,
                                  allow_small_or_imprecise_dtypes=True), (), [cb])
    k.op("pool", lambda e: e.iota(pi[:, 0:1], pattern=[[0, 1]], base=127, channel_multiplier=-1,
                                  allow_small_or_imprecise_dtypes=True), (), [cb])
    k.op("pool", lambda e: e.iota(pi[:, 1:2], pattern=[[0, 1]], base=0, channel_multiplier=1,
                                  allow_small_or_imprecise_dtypes=True), (), [cb])
    for h in range(2):
        k.act(QDF[:, h, :], ci, AF.Exp, [cb, lgb], [cb], scale=lg[:, h:h + 1])
        k.act(QDB[:, h, :], cbk, AF.Exp, [cb, lgb], [cb], scale=lg[:, 2 + h:3 + h])
        k.act(kd[:, h:h + 1], pi[:, 0:1], AF.Exp, [cb, lgb], [cb], scale=lg[:, h:h + 1])
        k.act(kd[:, 2 + h:3 + h], pi[:, 1:2], AF.Exp, [cb, lgb], [cb], scale=lg[:, 2 + h:3 + h])
    k.act(cd, lg, AF.Exp, [lgb], [cb], scale=128.0)

    NRING = 8
    rg = dict(
        x=Ring(k, "pxt", [128, D], F32, 3), rc=Ring(k, "prc", [128, 128], F32, NRING),
        hn=Ring(k, "phn", [128, D], BF16, 3), hnT=Ring(k, "phnT", [128, 8, 128], BF16, 2),
        qk=Ring(k, "pqk", [128, 4, 128], F32, 3), qkr=Ring(k, "pqkr", [128, 4, 128], BF16, 4),
        st=Ring(k, "pst", [128, 2048], BF16, 5), qm=Ring(k, "pqm", [128, 256], BF16, 3),
        sgm=Ring(k, "psgm", [128, 256], F32, NRING), qmT=Ring(k, "pqmT", [128, 2, 128], BF16, 3),
        E=Ring(k, "pE", [128, 2, 128], BF16, 3), mgt=Ring(k, "pmgt", [128, 256], BF16, 3),
        mgT=Ring(k, "pmgT", [128, 2, 128], BF16, 3), ss=Ring(k, "pss", [128, 2], F32, 3),
        rs=Ring(k, "prs", [128, 1], F32, 3), t1=Ring(k, "pt1", [128, 4, 64], F32, 2),
        t2=Ring(k, "pt2", [128, 4, 64], F32, 2), junk=Ring(k, "pjunk", [128, D], BF16, 2),
    )
    tl = {}

    def g(i, name):
        d = tl.setdefault(i, {})
        if name not in d:
            d[name] = rg[name].next()
        return d[name]

    def f_load(i):
        xt, xb = g(i, "x")
        rc, rcb = g(i, "rc")
        k.dma("sp", xt, x[i * 128:(i + 1) * 128, :], (), [xb])
        k.dma("sp", rc, rcs[i * 128:(i + 1) * 128, :], (), [rcb])

    def f_norm(i):
        xt, xb = g(i, "x")
        ss, ssb = g(i, "ss")
        hn, hnb = g(i, "hn")
        junk, junkb = g(i, "junk")
        k.act(junk, xt, AF.Square, [xb], [junkb, ssb], accum_out=ss[:, 0:1])
        k.act(ss[:, 1:2], ss[:, 0:1], AF.Sqrt, [ssb], [ssb], scale=1.0 / D, bias=EPS)
        k.recip(ss[:, 1:2], ss[:, 1:2], [ssb], [ssb])
        k.stt(hn, xt, ss[:, 1:2], gt, ALU.mult, ALU.mult, [xb, ssb, gb], [hnb])

    def f_tr(i):
        hn, hnb = g(i, "hn")
        pT = P.bf16(0)
        for c in range(8):
            k.tr(pT[:, c * 128:(c + 1) * 128], hn[:, c * 128:(c + 1) * 128], ident, [hnb, identb], [P.bufs[0]],
                 inc=(c == 7))

    def f_hcopy(i):
        hnT, hnTb = g(i, "hnT")
        k.cp("act", hnT, P.bf16(0)[:, 0:1024].rearrange("p (c t) -> p c t", c=8), [P.bufs[0]], [hnTb])

    def f_proj(i):
        hnT, hnTb = g(i, "hnT")
        for gi in range(4):
            pp = P.f32(1 + gi)
            for c in range(8):
                k.mm(pp, hnT[:, c, :], wt[:, c, gi * 512:(gi + 1) * 512], c == 0, c == 7, [hnTb, wtb],
                     [P.bufs[1 + gi]])

    def f_evac(i):
        st, stb = g(i, "st")
        qk, qkb = g(i, "qk")
        qm, qmb = g(i, "qm")
        sgm, sgmb = g(i, "sgm")
        p0 = P.f32(1)
        k.cp("act", qk[:, 0:2, :], p0[:, 0:256].rearrange("p (u d) -> p u d", u=2), [P.bufs[1]], [qkb])
        k.act(qk[:, 2:4, :], p0[:, 256:512].rearrange("p (u d) -> p u d", u=2), AF.Copy, [P.bufs[1]], [qkb],
              scale=128.0 ** -0.5)
        k.cp("act", st[:, 1024:1536], P.f32(2), [P.bufs[2]], [stb])
        k.act(st[:, 1536:2048], P.f32(3), AF.Silu, [P.bufs[3]], [stb])
        p3 = P.f32(4)
        k.cp("dve", qm, p3[:, 0:256], [P.bufs[4]], [qmb])
        k.act(sgm, p3[:, 256:512], AF.Silu, [P.bufs[4]], [sgmb])

    def f_rot(i):
        qk, qkb = g(i, "qk")
        qkr, qkrb = g(i, "qkr")
        rc, rcb = g(i, "rc")
        t1, t1b = rg["t1"].next()
        t2, t2b = rg["t2"].next()
        x1 = qk[:, :, 0:64]
        x2 = qk[:, :, 64:128]
        cosb = rc[:, 0:64].unsqueeze(1).to_broadcast([128, 4, 64])
        sinb = rc[:, 64:128].unsqueeze(1).to_broadcast([128, 4, 64])
        k.tt("dve", t1, x1, cosb, ALU.mult, [qkb, rcb], [t1b])
        k.tt("dve", t2, x2, sinb, ALU.mult, [qkb, rcb], [t2b])
        k.tt("dve", qkr[:, :, 0:64], t1, t2, ALU.subtract, [t1b, t2b], [qkrb])
        k.tt("dve", t1, x1, sinb, ALU.mult, [qkb, rcb], [t1b])
        k.tt("dve", t2, x2, cosb, ALU.mult, [qkb, rcb], [t2b])
        k.tt("dve", qkr[:, :, 64:128], t1, t2, ALU.add, [t1b, t2b], [qkrb])
        qm, qmb = g(i, "qm")
        pT = P.bf16(6)
        for dc in range(2):
            k.tr(pT[:, 512 + dc * 128:512 + (dc + 1) * 128], qm[:, dc * 128:(dc + 1) * 128], ident, [qmb, identb],
                 [P.bufs[6]], inc=(dc == 1))

    def f_kfb(i):
        qkr, qkrb = g(i, "qkr")
        st, stb = g(i, "st")
        for h in range(2):
            k.ts("pool", st[:, 512 + h * 128:512 + (h + 1) * 128], qkr[:, 2 + h, :], kd[:, h:h + 1], None,
                 ALU.mult, None, [qkrb, cb], [stb])
            k.ts("pool", st[:, 768 + h * 128:768 + (h + 1) * 128], qkr[:, 2 + h, :], kd[:, 2 + h:3 + h], None,
                 ALU.mult, None, [qkrb, cb], [stb])
        qmT, qmTb = g(i, "qmT")
        k.cp("act", qmT, P.bf16(6)[:, 512:768].rearrange("p (c t) -> p c t", c=2), [P.bufs[6]], [qmTb])
        pT = P.bf16(6)
        for u in range(4):
            k.tr(pT[:, u * 128:(u + 1) * 128], qkr[:, u, :], ident, [qkrb, identb], [P.bufs[6]], inc=(u == 3))

    def f_store(i):
        st, stb = g(i, "st")
        k.cp("act", st[:, 0:512], P.bf16(6)[:, 0:512], [P.bufs[6]], [stb])
        k.dma("sp", scrd[i], st, [stb], [scr_b[i]])
        qmT, qmTb = g(i, "qmT")
        pS = P.f32(7)
        for mt in range(2):
            for dc in range(2):
                k.mm(pS[:, mt * 128:(mt + 1) * 128], mkT[:, 0][:, dc, mt * 128:(mt + 1) * 128], qmT[:, dc, :],
                     dc == 0, dc == 1, [mkTb, qmTb], [P.bufs[7]], inc=(mt == 1 and dc == 1))

    def f_exp(i):
        E, Eb = g(i, "E")
        k.act(E, P.f32(7)[:, 0:256].rearrange("p (c t) -> p c t", c=2), AF.Exp, [P.bufs[7]], [Eb], scale=1.0 / 16.0)

    def f_av(i):
        E, Eb = g(i, "E")
        pO = P.f32(5)
        for mt in range(2):
            k.mm(pO[:, 0:257], E[:, mt, :], mva[:, 0][:, mt, 0:257], mt == 0, mt == 1, [Eb, mvab], [P.bufs[5]])

    def f_mnorm(i):
        rs, rsb = g(i, "rs")
        sgm, sgmb = g(i, "sgm")
        mgt, mgtb = g(i, "mgt")
        pO = P.f32(5)
        k.recip(rs, pO[:, 256:257], [P.bufs[5]], [rsb])
        k.stt(mgt, pO[:, 0:256], rs, sgm, ALU.mult, ALU.mult, [P.bufs[5], rsb, sgmb], [mgtb])

    def f_mtr(i):
        mgt, mgtb = g(i, "mgt")
        pT = P.bf16(7)
        for ec in range(2):
            k.tr(pT[:, 512 + ec * 128:512 + (ec + 1) * 128], mgt[:, ec * 128:(ec + 1) * 128], ident,
                 [mgtb, identb], [P.bufs[7]], inc=(ec == 1))

    def f_mout(i):
        mgT, mgTb = g(i, "mgT")
        k.cp("act", mgT, P.bf16(7)[:, 512:768].rearrange("p (c t) -> p c t", c=2), [P.bufs[7]], [mgTb])
        k.dma("sp", ygm[i], mgT.rearrange("p c t -> p (c t)"), [mgTb], [yg_mb[i]])
        gather_ygm(i)
        tl.pop(i, None)

    stages = [(0, f_load), (1, f_norm), (2, f_tr), (3, f_proj), (4, f_evac), (5, f_rot), (6, f_kfb), (7, f_store),
              (8, f_exp), (9, f_av), (10, f_mnorm), (11, f_mtr), (12, f_mout)]
    stages = stages[::-1]
    stages = [(3, f_hcopy)] + stages
    maxoff = max(o for o, _ in stages)
    for t in range(n_tiles + maxoff):
        for off, fn in stages:
            i = t - off
            if 0 <= i < n_tiles:
                fn(i)

    gather_ygm(0, final=True)

    Sb = k.sb("Sb", [128, 2, 256], F32)
    Sbb = Buf("Sb")
    Sf = k.sb("Sf", [128, 2, 256], F32)
    Sfb = Buf("Sf")
    Sf16 = k.sb("Sf16", [128, 2, 256], BF16)
    Sf16b = Buf("Sf16")
    k.memset("dve", Sb, 0.0, [Sbb])
    k.memset("dve", Sf, 0.0, [Sfb])
    k.memset("dve", Sf16, 0.0, [Sf16b])
    kvr = Ring(k, "kvr", [128, 1024], BF16, 3)
    sbst = Ring(k, "sbst", [128, 512], BF16, 3)
    ldB = {}

    def loadBk(n):
        if n >= 0:
            kv, kvb = kvr.next()
            k.dma("sp", kv, scrd[n][:, 512:1536], [scr_b[n]], [kvb])
            ldB[n] = (kv, kvb)

    loadBk(n_tiles - 1)
    loadBk(n_tiles - 2)
    for n in range(n_tiles - 1, -1, -1):
        loadBk(n - 2)
        kv, kvb = ldB.pop(n)
        so, sob = sbst.next()
        k.cp("act", so, Sb.rearrange("p h e -> p (h e)"), [Sbb], [sob])
        k.dma("sp", sbs[n], so, [sob], [sbs_b[n]])
        if n == 0:
            break
        for h in range(2):
            bk = 1 + h
            k.mm(P.f32(bk)[:, 0:256], kv[:, 256 + h * 128:256 + (h + 1) * 128], kv[:, 512 + h * 256:512 + (h + 1) * 256],
                 True, True, [kvb], [P.bufs[bk]])
            k.stt(Sb[:, h, :], Sb[:, h, :], cd[:, 2 + h:3 + h], P.f32(bk)[:, 0:256], ALU.mult, ALU.add,
                  [Sbb, cb, P.bufs[bk]], [Sbb])

    Sf16p = [k.sb("Sf16p0", [128, 2, 256], BF16), k.sb("Sf16p1", [128, 2, 256], BF16)]
    Sf16pb = [Buf("Sf16p0"), Buf("Sf16p1")]
    k.memset("dve", Sf16p[0], 0.0, [Sf16pb[0]])
    k.memset("dve", Sf16p[1], 0.0, [Sf16pb[1]])
    fr = dict(
        ch=Ring(k, "fch", [128, 2048], BF16, 6), sbn=Ring(k, "fsbn", [128, 512], BF16, 4),
        AT=Ring(k, "fAT", [128, 2, 128], BF16, 3), qsf=Ring(k, "fqsf", [128, 2, 128], BF16, 3),
        qsb=Ring(k, "fqsb", [128, 2, 128], BF16, 3), ss=Ring(k, "fss", [128, 2], F32, 3),
        junk=Ring(k, "fjunk", [128, 256], BF16, 2), ygt=Ring(k, "fygt", [128, 2, 256], BF16, 3),
        yT=Ring(k, "fyT", [128, 4, 128], BF16, 3),
    )
    ft = {}

    def fg(n, name):
        d = ft.setdefault(n, {})
        if name not in d:
            d[name] = fr[name].next()
        return d[name]

    def F_load(n):
        ch, chb = fg(n, "ch")
        sbn, sbnb = fg(n, "sbn")
        k.dma("sp", ch, scrd[n], [scr_b[n]], [chb])
        k.dma("sp", sbn, sbs[n], [sbs_b[n]], [sbnb])

    def F_s(n):
        ch, chb = fg(n, "ch")
        for h in range(2):
            qT = ch[:, h * 128:(h + 1) * 128]
            kT = ch[:, 256 + h * 128:256 + (h + 1) * 128]
            k.mm(P.f32(1)[:, h * 128:(h + 1) * 128], kT, qT, True, True, [chb], [P.bufs[1]])
        if n < n_tiles - 1:
            for h in range(2):
                kf = ch[:, 512 + h * 128:512 + (h + 1) * 128]
                v = ch[:, 1024 + h * 256:1024 + (h + 1) * 256]
                k.mm(P.f32(5)[:, h * 256:(h + 1) * 256], kf, v, True, True, [chb], [P.bufs[5]])

    def F_dec(n):
        ch, chb = fg(n, "ch")
        a, ab = fg(n, "AT")
        f, fb = fg(n, "qsf")
        bq, bqb = fg(n, "qsb")
        k.tt("dve", a, P.f32(1)[:, 0:256].rearrange("p (h c) -> p h c", h=2), DT, ALU.mult, [P.bufs[1], cb], [ab])
        qT2 = ch[:, 0:256].rearrange("p (h c) -> p h c", h=2)
        k.tt("pool", f, qT2, QDF, ALU.mult, [chb, cb], [fb])
        k.tt("pool", bq, qT2, QDB, ALU.mult, [chb, cb], [bqb])
        if n < n_tiles - 1:
            nxt = (n + 1) % 2
            for h in range(2):
                k.stt(Sf[:, h, :], Sf[:, h, :], cd[:, h:h + 1], P.f32(5)[:, h * 256:(h + 1) * 256], ALU.mult, ALU.add,
                      [Sfb, cb, P.bufs[5]], [Sfb])
            k.cp("act", Sf16p[nxt], Sf, [Sfb], [Sf16pb[nxt]])

    def F_y(n):
        ch, chb = fg(n, "ch")
        sbn, sbnb = fg(n, "sbn")
        a, ab = fg(n, "AT")
        f, fb = fg(n, "qsf")
        bq, bqb = fg(n, "qsb")
        bY = 3 + (n % 2)
        cur = n % 2
        for h in range(2):
            v = ch[:, 1024 + h * 256:1024 + (h + 1) * 256]
            pY = P.f32(bY)[:, h * 256:(h + 1) * 256]
            k.mm(pY, a[:, h, :], v, True, False, [ab, chb], [P.bufs[bY]])
            k.mm(pY, f[:, h, :], Sf16p[cur][:, h, :], False, False, [fb, Sf16pb[cur]], [P.bufs[bY]])
            k.mm(pY, bq[:, h, :], sbn[:, h * 256:(h + 1) * 256], False, True, [bqb, sbnb], [P.bufs[bY]])

    def F_ss(n):
        ss, ssb = fg(n, "ss")
        bY = 3 + (n % 2)
        for h in range(2):
            junk, junkb = fr["junk"].next()
            k.act(junk, P.f32(bY)[:, h * 256:(h + 1) * 256], AF.Square, [P.bufs[bY]], [junkb, ssb],
                  accum_out=ss[:, h:h + 1])
        k.act(ss, ss, AF.Sqrt, [ssb], [ssb], scale=1.0 / 256, bias=EPS)

    def F_gate(n):
        ss, ssb = fg(n, "ss")
        ch, chb = fg(n, "ch")
        ygt, ygb = fg(n, "ygt")
        bY = 3 + (n % 2)
        k.recip(ss, ss, [ssb], [ssb])
        for h in range(2):
            sg = ch[:, 1536 + h * 256:1536 + (h + 1) * 256]
            k.stt(ygt[:, h, :], P.f32(bY)[:, h * 256:(h + 1) * 256], ss[:, h:h + 1], sg, ALU.mult, ALU.mult,
                  [P.bufs[bY], ssb, chb], [ygb])

    def F_tr(n):
        ygt, ygb = fg(n, "ygt")
        pT = P.bf16(7)
        for h in range(2):
            for ec in range(2):
                k.tr(pT[:, (h * 2 + ec) * 128:(h * 2 + ec + 1) * 128], ygt[:, h, ec * 128:(ec + 1) * 128], ident,
                     [ygb, identb], [P.bufs[7]], inc=(h == 1 and ec == 1))

    def F_out(n):
        yT, yTb = fg(n, "yT")
        k.cp("act", yT, P.bf16(7)[:, 0:512].rearrange("p (c t) -> p c t", c=4), [P.bufs[7]], [yTb])
        k.dma("sp", yg[n], yT.rearrange("p c t -> p (c t)"), [yTb], [yg_yb[n]])
        gather_yg(n)
        ft.pop(n, None)

    fstages = [(0, F_load), (1, F_s), (2, F_dec), (3, F_y), (4, F_ss), (5, F_gate), (6, F_tr), (7, F_out)][::-1]
    for t in range(n_tiles + 7):
        for off, fn in fstages:
            n = t - off
            if 0 <= n < n_tiles:
                fn(n)


import math

LAMBDA_INIT = 0.8 - 0.6 * math.exp(-0.3 * 1)
def sumsq_gather(k, P, nc, ssq, ssqb, dr_in, dr_in_b, dr_all, dr_all_b, groups, n_feat, tag):
    nt = ssq.shape[1]
    k.dma("sp", dr_in, ssq, [ssqb], [dr_in_b])
    k.collective("AllGather", dr_in, dr_all, groups, [dr_in_b], [dr_all_b])
    g4 = k.sb("ssq4" + tag, [128, 4, nt], F32)
    g4b = Buf("ssq4" + tag)
    k.dma("sp", g4, dr_all.rearrange("(r p) n -> p r n", p=128), [dr_all_b], [g4b])
    rstd = k.sb("rstd" + tag, [128, nt], F32)
    rstdb = Buf("rstd" + tag)
    k.tt("dve", rstd, g4[:, 0, :], g4[:, 1, :], ALU.add, [g4b], [rstdb])
    k.tt("dve", rstd, rstd, g4[:, 2, :], ALU.add, [g4b, rstdb], [rstdb])
    k.tt("dve", rstd, rstd, g4[:, 3, :], ALU.add, [g4b, rstdb], [rstdb])
    k.act(rstd, rstd, AF.Sqrt, [rstdb], [rstdb], scale=1.0 / n_feat, bias=EPS)
    k.recip(rstd, rstd, [rstdb], [rstdb])
    return rstd, rstdb


def colproj(k, P, src, srcb, srcm, srcmb, wot, wotb, bank):
    pp = P.f32(bank)[:, 0:256]
    n = 0
    for g in range(4):
        for c in (4, 5):
            k.mm(pp, srcm[:, g, (c - 4) * 128:(c - 3) * 128], wot[:, g * 6 + c, :], n == 0, n == 23,
                 [srcmb, wotb], [P.bufs[bank]])
            n += 1
    for g in range(4):
        for c in range(4):
            k.mm(pp, src[:, g, c * 128:(c + 1) * 128], wot[:, g * 6 + c, :], n == 0, n == 23, [srcb, wotb],
                 [P.bufs[bank]])
            n += 1
    return pp


def phase_B(k, P, nc, ident, identb, T, n_tiles):
    ygall, ygall_cb = T["ygall"], T["ygall_cb"]
    ygmall, ygmall_cb = T["ygmall"], T["ygmall_cb"]
    xc, wo, ngc = T["xcB"], T["woB"], T["ngcB"]
    h1s, h1s_b = T["h1s"], T["h1s_b"]
    h1gt, h1gt_tb = T["h1gt"], T["h1gt_tb"]
    gather_h1 = T["gather_h1"]
    wot = k.sb("wotB", [128, 24, 256], BF16)
    wotb = Buf("wotB")
    for c in range(2):
        k.dma("pool", wot[:, c * 12:(c + 1) * 12, :],
              wo[c * 1536:(c + 1) * 1536, :].rearrange("(c p) n -> p c n", p=128), (), [wotb])
    gc = k.sb("gcB", [128, 256], F32)
    gcb = Buf("gcB")
    k.dma("sp", gc, ngc.partition_broadcast(128), (), [gcb])
    ssq = k.sb("ssqB", [128, n_tiles], F32)
    ssqb = Buf("ssqB")
    k.memset("dve", ssq, 0.0, [ssqb])
    ygr = Ring(k, "ygtB", [128, 4, 512], BF16, 3)
    ygmr = Ring(k, "ygmB", [128, 4, 256], BF16, 3)
    xr = Ring(k, "xcB", [128, 256], F32, 3)
    hr = Ring(k, "h1sB", [128, 256], F32, 2)
    hgr = Ring(k, "h1gB", [128, 256], BF16, 2)
    hTr = Ring(k, "h1gTB", [128, 2, 128], BF16, 2)
    junk = k.sb("junkB", [128, 256], BF16)
    junkb = Buf("junkB")
    yv = ygall.rearrange("(c g ii) p n -> c ii p g n", g=4, ii=4)
    ymv = ygmall.rearrange("(c g ii) p n -> c ii p g n", g=4, ii=4)
    ld = {}

    def load(i):
        if i < n_tiles:
            yt, yb = ygr.next()
            xt, xb = xr.next()
            ym, ymb = ygmr.next()
            k.dma("sp", ym, ymv[i // 4][i % 4], [ygmall_cb[i // 4]], [ymb])
            k.dma("sp", yt, yv[i // 4][i % 4], [ygall_cb[i // 4]], [yb])
            k.dma("sp", xt, xc[i * 128:(i + 1) * 128, :], (), [xb])
            ld[i] = (yt, yb, xt, xb, ym, ymb)

    pps = {}

    def proj(i):
        if i < n_tiles:
            yt, yb, xt, xb, ym, ymb = ld[i]
            bank = 1 + (i % 2)
            pps[i] = (colproj(k, P, yt, yb, ym, ymb, wot, wotb, bank), bank)

    load(0)
    load(1)
    proj(0)
    for i in range(n_tiles):
        load(i + 2)
        proj(i + 1)
        yt, yb, xt, xb, ym, ymb = ld.pop(i)
        pp, bank = pps.pop(i)
        h, hb = hr.next()
        k.tt("dve", h, pp, xt, ALU.add, [P.bufs[bank], xb], [hb])
        k.dma("sp", h1s[i], h, [hb], [h1s_b[i]])
        k.act(junk, h, AF.Square, [hb], [junkb, ssqb], accum_out=ssq[:, i:i + 1])
        hg, hgb = hgr.next()
        k.tt("dve", hg, h, gc, ALU.mult, [hb, gcb], [hgb])
        hT, hTb = hTr.next()
        pT = P.bf16(3 + (i % 2))
        for c in range(2):
            k.tr(pT[:, c * 128:(c + 1) * 128], hg[:, c * 128:(c + 1) * 128], ident, [hgb, identb],
                 [P.bufs[3 + (i % 2)]], inc=(c == 1))
        k.cp("act", hT, pT[:, 0:256].rearrange("p (c t) -> p c t", c=2), [P.bufs[3 + (i % 2)]], [hTb])
        k.dma("sp", h1gt[i], hT.rearrange("p c t -> p (c t)"), [hTb], [h1gt_tb[i]])
        gather_h1(i)
    return ssq, ssqb


def phase_B2_pre(k, P, nc, ident, identb, T):
    w, mem, mng, wkv = T["wB"], T["memB"], T["mngB"], T["wkvB"]
    wt = k.sb("wtB2", [128, 8, 2560], BF16)
    wtb = Buf("wtB2")
    for c in range(5):
        k.dma("pool", wt[:, :, c * 512:(c + 1) * 512],
              w[:, c * 512:(c + 1) * 512].rearrange("(c p) n -> p c n", p=128), (), [wtb])
    gt = k.sb("gtB2", [128, D], F32)
    gb = Buf("gtB2")
    scr = (k.sb("junkB2", [128, D], BF16), Buf(), k.sb("ssB2", [128, 1], F32), Buf(),
           k.sb("rstdB2", [128, 1], F32), Buf(), k.sb("hnB2", [128, D], BF16), Buf())
    mkT, mkTb, mva, mvab = mem_prep(k, P, nc, mem, mng, wkv, 1, ident, identb, scr, gt, gb)
    T["B2pre"] = (wt, wtb, mkT, mkTb, mva, mvab)


def phase_B2(k, P, nc, ident, identb, T, n_tiles, rstd, rstdb):
    hall, hall_cb = T["h1gtall"], T["h1gtall_cb"]
    w, mem, mng, wkv, dcs = T["wB"], T["memB"], T["mngB"], T["wkvB"], T["dcsB"]
    scr1, scr1_b = T["scr1"], T["scr1_b"]
    ogm, ogt_mb = T["ogm"], T["ogt_mb"]
    gather_ogm = T["gather_ogm"]
    wt, wtb, mkT, mkTb, mva, mvab = T["B2pre"]
    rg = dict(
        hn=Ring(k, "qhn", [128, 4, 256], BF16, 3), rp=Ring(k, "qrp", [128, 32], F32, 5),
        rot=Ring(k, "qrot", [128, 8, 32], F32, 3), qkr=Ring(k, "qqkr", [128, 8, 128], BF16, 4),
        st=Ring(k, "qst", [128, 2048], BF16, 5), qm=Ring(k, "qqm", [128, 256], BF16, 3),
        sgm=Ring(k, "qsgm", [128, 256], F32, 8), qmT=Ring(k, "qqmT", [128, 2, 128], BF16, 3),
        E=Ring(k, "qE", [128, 2, 128], BF16, 3), mgt=Ring(k, "qmgt", [128, 256], BF16, 3),
        mgT=Ring(k, "qmgT", [128, 2, 128], BF16, 3), rs=Ring(k, "qrs", [128, 1], F32, 3),
        t1=Ring(k, "qt1", [128, 8, 16], F32, 2), t2=Ring(k, "qt2", [128, 8, 16], F32, 2),
    )
    hv = hall.rearrange("(c g ii) p n -> c ii p g n", g=4, ii=4)
    tl = {}

    def g(i, name):
        d = tl.setdefault(i, {})
        if name not in d:
            d[name] = rg[name].next()
        return d[name]

    def f_load(i):
        hn, hnb = g(i, "hn")
        rp, rpb = g(i, "rp")
        k.dma("sp", hn, hv[i // 4][i % 4], [hall_cb[i // 4]], [hnb])
        k.dma("sp", rp, dcs[i * 128:(i + 1) * 128, :], (), [rpb])

    def f_proj(i):
        hn, hnb = g(i, "hn")
        for gi in range(5):
            pp = P.f32(gi)
            for c in range(8):
                k.mm(pp, hn[:, c // 2, (c % 2) * 128:(c % 2 + 1) * 128], wt[:, c, gi * 512:(gi + 1) * 512],
                     c == 0, c == 7, [hnb, wtb], [P.bufs[gi]])

    def f_evac(i):
        r = rstd[:, i:i + 1]
        st, stb = g(i, "st")
        rot, rotb = g(i, "rot")
        qkr, qkrb = g(i, "qkr")
        qm, qmb = g(i, "qm")
        sgm, sgmb = g(i, "sgm")
        for gi in range(2):
            p3 = P.f32(gi).rearrange("p (u d) -> p u d", u=4)
            k.act(rot[:, gi * 4:(gi + 1) * 4, :], p3[:, :, 0:32], AF.Copy, [P.bufs[gi], rstdb], [rotb], scale=r)
            k.act(qkr[:, gi * 4:(gi + 1) * 4, :], p3, AF.Copy, [P.bufs[gi], rstdb], [qkrb], scale=r)
        k.act(st[:, 1024:1536], P.f32(2), AF.Copy, [P.bufs[2], rstdb], [stb], scale=r)
        k.act(st[:, 1536:2048], P.f32(3), AF.Silu, [P.bufs[3], rstdb], [stb], scale=r)
        p4 = P.f32(4)
        k.ts("dve", qm, p4[:, 0:256], r, None, ALU.mult, None, [P.bufs[4], rstdb], [qmb])
        k.act(sgm, p4[:, 256:512], AF.Silu, [P.bufs[4], rstdb], [sgmb], scale=r)

    def f_rot(i):
        rot, rotb = g(i, "rot")
        qkr, qkrb = g(i, "qkr")
        rp, rpb = g(i, "rp")
        t1, t1b = rg["t1"].next()
        t2, t2b = rg["t2"].next()
        x1 = rot[:, :, 0:16]
        x2 = rot[:, :, 16:32]
        cosb = rp[:, 0:16].unsqueeze(1).to_broadcast([128, 8, 16])
        sinb = rp[:, 16:32].unsqueeze(1).to_broadcast([128, 8, 16])
        k.tt("dve", t1, x1, cosb, ALU.mult, [rotb, rpb], [t1b])
        k.tt("dve", t2, x2, sinb, ALU.mult, [rotb, rpb], [t2b])
        k.tt("dve", qkr[:, :, 0:16], t1, t2, ALU.subtract, [t1b, t2b], [qkrb])
        k.tt("dve", t1, x1, sinb, ALU.mult, [rotb, rpb], [t1b])
        k.tt("dve", t2, x2, cosb, ALU.mult, [rotb, rpb], [t2b])
        k.tt("dve", qkr[:, :, 16:32], t1, t2, ALU.add, [t1b, t2b], [qkrb])
        qm, qmb = g(i, "qm")
        pT = P.bf16(6)
        for dc in range(2):
            k.tr(pT[:, 768 + dc * 128:768 + (dc + 1) * 128], qm[:, dc * 128:(dc + 1) * 128], ident, [qmb, identb],
                 [P.bufs[6]], inc=(dc == 1))

    def f_qktr(i):
        qkr, qkrb = g(i, "qkr")
        qmT, qmTb = g(i, "qmT")
        k.cp("act", qmT, P.bf16(6)[:, 768:1024].rearrange("p (c t) -> p c t", c=2), [P.bufs[6]], [qmTb])
        pT = P.bf16(5)
        for u in range(8):
            k.tr(pT[:, u * 128:(u + 1) * 128], qkr[:, u, :], ident, [qkrb, identb], [P.bufs[5]], inc=(u == 7))

    def f_store(i):
        st, stb = g(i, "st")
        k.cp("dve", st[:, 0:1024], P.bf16(5)[:, 0:1024], [P.bufs[5]], [stb])
        k.dma("sp", scr1[i], st, [stb], [scr1_b[i]])
        qmT, qmTb = g(i, "qmT")
        pS = P.f32(7)
        for mt in range(2):
            for dc in range(2):
                k.mm(pS[:, mt * 128:(mt + 1) * 128], mkT[:, 0][:, dc, mt * 128:(mt + 1) * 128], qmT[:, dc, :],
                     dc == 0, dc == 1, [mkTb, qmTb], [P.bufs[7]], inc=(mt == 1 and dc == 1))

    def f_exp(i):
        E, Eb = g(i, "E")
        k.act(E, P.f32(7)[:, 0:256].rearrange("p (c t) -> p c t", c=2), AF.Exp, [P.bufs[7]], [Eb], scale=1.0 / 16.0)

    def f_av(i):
        E, Eb = g(i, "E")
        pO = P.f32(6)
        for mt in range(2):
            k.mm(pO[:, 0:257], E[:, mt, :], mva[:, 0][:, mt, 0:257], mt == 0, mt == 1, [Eb, mvab], [P.bufs[6]])

    def f_mnorm(i):
        rs, rsb = g(i, "rs")
        sgm, sgmb = g(i, "sgm")
        mgt, mgtb = g(i, "mgt")
        pO = P.f32(6)
        k.recip(rs, pO[:, 256:257], [P.bufs[6]], [rsb])
        k.stt(mgt, pO[:, 0:256], rs, sgm, ALU.mult, ALU.mult, [P.bufs[6], rsb, sgmb], [mgtb])

    def f_mtr(i):
        mgt, mgtb = g(i, "mgt")
        pT = P.bf16(7)
        for ec in range(2):
            k.tr(pT[:, 512 + ec * 128:512 + (ec + 1) * 128], mgt[:, ec * 128:(ec + 1) * 128], ident,
                 [mgtb, identb], [P.bufs[7]], inc=(ec == 1))

    def f_mout(i):
        mgT, mgTb = g(i, "mgT")
        k.cp("act", mgT, P.bf16(7)[:, 512:768].rearrange("p (c t) -> p c t", c=2), [P.bufs[7]], [mgTb])
        k.dma("sp", ogm[i], mgT.rearrange("p c t -> p (c t)"), [mgTb], [ogt_mb[i]])
        gather_ogm(i)
        tl.pop(i, None)

    stages = [(0, f_load), (1, f_proj), (2, f_evac), (3, f_rot), (4, f_qktr), (5, f_store), (6, f_exp), (7, f_av),
              (8, f_mnorm), (9, f_mtr), (10, f_mout)][::-1]
    maxoff = 10
    for t in range(n_tiles + maxoff):
        for off, fn in stages:
            i = t - off
            if 0 <= i < n_tiles:
                fn(i)


def phase_C(k, P, nc, ident, identb, T, n_tiles):
    scr1, scr1_b = T["scr1"], T["scr1_b"]
    ogt, ogt_yb = T["ogt"], T["ogt_yb"]
    gather_og = T["gather_og"]
    lam4, slg = T["lamC"], T["slgC"]
    n_all = n_tiles
    nq = n_tiles // 4
    lv = k.sb("lv", [128, 4, 128], F32)
    lvb = Buf("lv")
    k.dma("sp", lv, lam4.rearrange("a d -> (a d)").partition_broadcast(128).rearrange("p (a d) -> p a d", a=4), (), [lvb])
    lp = k.sb("lp", [128, 2, 128], F32)
    k.tt("dve", lp[:, 0, :], lv[:, 0, :], lv[:, 1, :], ALU.mult, [lvb], [lvb])
    k.tt("dve", lp[:, 1, :], lv[:, 2, :], lv[:, 3, :], ALU.mult, [lvb], [lvb])
    ls = k.sb("ls", [128, 2], F32)
    k.op("dve", lambda e: e.tensor_reduce(out=ls, in_=lp, axis=AX.X, op=ALU.add), [lvb], [lvb])
    k.act(ls, ls, AF.Exp, [lvb], [lvb])
    nlam = k.sb("nlam", [128, 1], F32)
    nlamb = Buf("nlam")
    k.tt("dve", nlam, ls[:, 1:2], ls[:, 0:1], ALU.subtract, [lvb], [nlamb])
    k.ts("dve", nlam, nlam, -LAMBDA_INIT, None, ALU.add, None, [nlamb], [nlamb])
    sgain = k.sb("sgain", [128, 256], F32)
    sgainb = Buf("sgain")
    k.dma("sp", sgain, slg.partition_broadcast(128), (), [sgainb])
    k.ts("dve", sgain, sgain, 1.0 - LAMBDA_INIT, None, ALU.mult, None, [sgainb], [sgainb])

    kc = [k.sb(f"kc{c}", [128, n_all, 128], BF16) for c in range(2)]
    kcb = [Buf("kc0"), Buf("kc1")]
    va = k.sb("va", [128, n_all, 258], BF16)
    vab = Buf("va")
    k.memset("dve", va, 1.0, [vab])
    qtr = Ring(k, "qtile", [128, 2, 4, 128], BF16, 2)
    sgr = Ring(k, "sgt", [128, 4, 256], BF16, 2)
    er = Ring(k, "E", [128, 512], BF16, 3)
    on = k.sb("on", [128, 2, 4, 256], F32)
    onq = [Buf(f"on{q_}") for q_ in range(4)]
    rs = k.sb("rs", [128, 4], F32)
    rsb = Buf("rs")
    ssq = k.sb("ssq", [128, 4], F32)
    ssqb = Buf("ssq")
    junk = k.sb("junkc", [128, 256], BF16)
    junkb = Buf("junkc")
    ogtile = k.sb("ogtile", [128, 4, 256], BF16)
    ogtileb = Buf("ogtile")
    ogTr = Ring(k, "ogT", [128, 4, 2, 128], BF16, 2)
    scale = 128.0 ** -0.5
    sbank = 0
    allscr = list(scr1_b)
    for hh in range(2):
        for c in range(2):
            u = hh * 2 + c
            for i0 in range(0, n_all, 16):
                i1 = min(n_all, i0 + 16)
                k.dma("sp", kc[c][:, i0:i1, :],
                      scr1[i0:i1, :, 512 + u * 128:512 + (u + 1) * 128].rearrange("i p t -> p i t"),
                      allscr[i0:i1], [kcb[c]])
        for i0 in range(0, n_all, 16):
            i1 = min(n_all, i0 + 16)
            k.dma("sp", va[:, i0:i1, 0:256],
                  scr1[i0:i1, :, 1024 + hh * 256:1024 + (hh + 1) * 256].rearrange("i p e -> p i e"),
                  allscr[i0:i1], [vab])
        ld = {}

        def load(q):
            if q < nq:
                qt, qtb = qtr.next()
                sgt, sgtb = sgr.next()
                for c in range(2):
                    u = hh * 2 + c
                    k.dma("sp", qt[:, c], scr1[q * 4:(q + 1) * 4, :, u * 128:(u + 1) * 128].rearrange("i p t -> p i t"),
                          allscr[q * 4:(q + 1) * 4], [qtb])
                k.dma("sp", sgt, scr1[q * 4:(q + 1) * 4, :, 1536 + hh * 256:1536 + (hh + 1) * 256].rearrange("i p e -> p i e"),
                      allscr[q * 4:(q + 1) * 4], [sgtb])
                ld[q] = (qt, qtb, sgt, sgtb)

        load(0)
        for q in range(nq):
            load(q + 1)
            qt, qtb, sgt, sgtb = ld.pop(q)
            items = [(c, kt_i) for c in range(2) for kt_i in range(n_all)]
            pend = {}

            def emit_S(c, kt_i):
                nonlocal sbank
                sbank = (sbank + 1) % 3
                pS = P.f32(sbank)
                k.mm(pS, kc[c][:, kt_i, :], qt[:, c].rearrange("p a t -> p (a t)"), True, True, [kcb[c], qtb],
                     [P.bufs[sbank]])
                pend[(c, kt_i)] = sbank

            emit_S(*items[0])
            emit_S(*items[1])
            for idx, (c, kt_i) in enumerate(items):
                if idx + 2 < len(items):
                    emit_S(*items[idx + 2])
                sb_ = pend.pop((c, kt_i))
                E, Eb = er.next()
                k.act(E, P.f32(sb_), AF.Exp, [P.bufs[sb_]], [Eb], scale=scale)
                for qs in range(4):
                    k.mm(P.f32(3 + qs)[:, 0:257], E[:, qs * 128:(qs + 1) * 128], va[:, kt_i, 0:257],
                         kt_i == 0, kt_i == n_all - 1, [Eb, vab], [P.bufs[3 + qs]], inc=(qs == 3))
                if kt_i == n_all - 1:
                    for qs in range(4):
                        pO = P.f32(3 + qs)
                        k.recip(rs[:, qs:qs + 1], pO[:, 256:257], [P.bufs[3 + qs]], [rsb])
                    for qs in range(4):
                        pO = P.f32(3 + qs)
                        if qs % 2 == 0:
                            k.ts("dve", on[:, c, qs, :], pO[:, 0:256], rs[:, qs:qs + 1], None, ALU.mult, None,
                                 [P.bufs[3 + qs], rsb], [onq[qs]])
                        else:
                            k.act(on[:, c, qs, :], pO[:, 0:256], AF.Copy, [P.bufs[3 + qs], rsb], [onq[qs]],
                                  scale=rs[:, qs:qs + 1])
            on0 = on[:, 0].rearrange("p a e -> p (a e)")
            on1 = on[:, 1].rearrange("p a e -> p (a e)")
            k.stt(on0, on1, nlam, on0, ALU.mult, ALU.add, onq + [nlamb], onq)
            for qs in range(4):
                k.act(junk, on[:, 0, qs, :], AF.Square, [onq[qs]], [junkb, ssqb], accum_out=ssq[:, qs:qs + 1])
            k.act(ssq, ssq, AF.Sqrt, [ssqb], [ssqb], scale=1.0 / 256, bias=EPS)
            k.recip(ssq, ssq, [ssqb], [ssqb])
            for qs in range(4):
                k.stt(on[:, 0, qs, :], on[:, 0, qs, :], ssq[:, qs:qs + 1], sgain, ALU.mult, ALU.mult,
                      [onq[qs], ssqb, sgainb], [onq[qs]])
            k.tt("dve", ogtile.rearrange("p a e -> p (a e)"), on0, sgt.rearrange("p a e -> p (a e)"), ALU.mult,
                 onq + [sgtb], [ogtileb])
            ogT, ogTb = ogTr.next()
            pT = P.bf16(7)
            for qs in range(4):
                for ec in range(2):
                    k.tr(pT[:, (qs * 2 + ec) * 128:(qs * 2 + ec + 1) * 128], ogtile[:, qs, ec * 128:(ec + 1) * 128],
                         ident, [ogtileb, identb], [P.bufs[7]], inc=(qs == 3 and ec == 1))
            k.cp("act", ogT.rearrange("p a c t -> p (a c t)"), pT[:, 0:1024], [P.bufs[7]], [ogTb])
            k.dma("sp", ogt[q * 4:(q + 1) * 4, :, hh * 256:(hh + 1) * 256].rearrange("i p n -> p i n"),
                  ogT.rearrange("p a c t -> p a (c t)"), [ogTb], [ogt_yb[q][hh]])
            if hh == 1:
                gather_og(q)


def phase_D(k, P, nc, ident, identb, T, n_tiles, GROUPS):
    ogall, ogall_cb = T["ogtall"], T["ogtall_cb"]
    ogmall, ogmall_cb = T["ogmall"], T["ogmall_cb"]
    wo, fgc = T["woD"], T["fgcD"]
    h1s, h1s_b = T["h1s"], T["h1s_b"]
    out = T["out"]
    wot = k.sb("wotD", [128, 24, 256], BF16)
    wotb = Buf("wotD")
    for c in range(2):
        k.dma("pool", wot[:, c * 12:(c + 1) * 12, :],
              wo[c * 1536:(c + 1) * 1536, :].rearrange("(c p) n -> p c n", p=128), (), [wotb])
    gc = k.sb("gcD", [128, 256], F32)
    gcb = Buf("gcD")
    k.dma("sp", gc, fgc.partition_broadcast(128), (), [gcb])
    ssq = k.sb("ssqD", [128, n_tiles], F32)
    ssqb = Buf("ssqD")
    k.memset("dve", ssq, 0.0, [ssqb])
    h2 = k.sb("h2D", [128, n_tiles, 256], F32)
    h2b = [Buf(f"h2_{i}") for i in range(n_tiles)]
    ogr = Ring(k, "ogD", [128, 4, 512], BF16, 3)
    ogmr = Ring(k, "ogmD", [128, 4, 256], BF16, 3)
    hr = Ring(k, "h1D", [128, 256], F32, 3)
    junk = k.sb("junkD", [128, 256], BF16)
    junkb = Buf("junkD")
    ov = ogall.rearrange("(c g ii) p n -> c ii p g n", g=4, ii=4)
    omv = ogmall.rearrange("(c g ii) p n -> c ii p g n", g=4, ii=4)
    ld = {}

    def load(i):
        if i < n_tiles:
            og, ogb = ogr.next()
            h, hb = hr.next()
            om, omb = ogmr.next()
            k.dma("sp", om, omv[i // 4][i % 4], [ogmall_cb[i // 4]], [omb])
            k.dma("sp", og, ov[i // 4][i % 4], [ogall_cb[i // 4]], [ogb])
            k.dma("sp", h, h1s[i], [h1s_b[i]], [hb])
            ld[i] = (og, ogb, h, hb, om, omb)

    pps = {}

    def proj(i):
        if i < n_tiles:
            og, ogb, h, hb, om, omb = ld[i]
            bank = 1 + (i % 2)
            pps[i] = (colproj(k, P, og, ogb, om, omb, wot, wotb, bank), bank)

    load(0)
    load(1)
    proj(0)
    for i in range(n_tiles):
        load(i + 2)
        proj(i + 1)
        og, ogb, h, hb, om, omb = ld.pop(i)
        pp, bank = pps.pop(i)
        k.tt("dve", h2[:, i, :], pp, h, ALU.add, [P.bufs[bank], hb], [h2b[i]])
        k.act(junk, h2[:, i, :], AF.Square, [h2b[i]], [junkb, ssqb], accum_out=ssq[:, i:i + 1])
    rstd, rstdb = sumsq_gather(k, P, nc, ssq, ssqb, T["ssq2"], T["ssq2_b"], T["ssq2all"], T["ssq2all_b"], GROUPS,
                               D, "D")
    outr = Ring(k, "outD", [128, 256], F32, 3)
    for i in range(n_tiles):
        o, ob = outr.next()
        k.stt(o, h2[:, i, :], rstd[:, i:i + 1], gc, ALU.mult, ALU.mult, [h2b[i], rstdb, gcb], [ob])
        k.dma("sp", out[i * 128:(i + 1) * 128, :], o, [ob], (), is_output=True)


def build_fused(n_tiles=128, groups=None):
    GROUPS = groups or [[0, 1, 2, 3], [4, 5, 6, 7]]
    nc = bass.Bass("TRN2", target_bir_lowering=False)
    S_ = n_tiles * 128
    T = {}

    def ext(name, shape, dt=F32):
        T[name] = nc.dram_tensor(name, list(shape), dt, kind="ExternalInput").ap()

    def scratch(name, shape, dt):
        T[name] = nc.dram_tensor(name, list(shape), dt).ap()
        T[name + "_b"] = Buf(name)

    ext("xA", [S_, D]); ext("ngA", [D]); ext("wA", [D, 2048]); ext("memA", [256, D]); ext("mngA", [D])
    ext("wkvA", [D, 512]); ext("decA", [4]); ext("rcsA", [S_, 128])
    ext("xcB", [S_, 256]); ext("woB", [3072, 256]); ext("ngcB", [256])
    ext("wB", [D, 2560]); ext("memB", [256, D]); ext("mngB", [D]); ext("wkvB", [D, 512]); ext("dcsB", [S_, 32])
    ext("lamC", [4, 128]); ext("slgC", [256]); ext("woD", [3072, 256]); ext("fgcD", [256])
    T["out"] = nc.dram_tensor("out", [S_, 256], F32, kind="ExternalOutput").ap()
    scratch("yg", [n_tiles, 128, 512], BF16)
    scratch("ygall", [4 * n_tiles, 128, 512], BF16)
    scratch("ygm", [n_tiles, 128, 256], BF16)
    scratch("ygmall", [4 * n_tiles, 128, 256], BF16)
    T["scrA"] = nc.dram_tensor("scrA", [n_tiles, 128, 2048], BF16).ap()
    T["sbsA"] = nc.dram_tensor("sbsA", [n_tiles, 128, 512], BF16).ap()
    T["h1s"] = nc.dram_tensor("h1s", [n_tiles, 128, 256], F32).ap()
    T["h1s_b"] = [Buf(f"h1s{i}") for i in range(n_tiles)]
    scratch("h1gt", [n_tiles, 128, 256], BF16)
    scratch("h1gtall", [4 * n_tiles, 128, 256], BF16)
    scratch("ssq1", [128, n_tiles], F32)
    scratch("ssq1all", [4 * 128, n_tiles], F32)
    T["scr1"] = nc.dram_tensor("scr1", [n_tiles, 128, 2048], BF16).ap()
    T["scr1_b"] = [Buf(f"scr1_{i}") for i in range(n_tiles)]
    scratch("ogt", [n_tiles, 128, 512], BF16)
    scratch("ogtall", [4 * n_tiles, 128, 512], BF16)
    scratch("ogm", [n_tiles, 128, 256], BF16)
    scratch("ogmall", [4 * n_tiles, 128, 256], BF16)
    scratch("ssq2", [128, n_tiles], F32)
    scratch("ssq2all", [4 * 128, n_tiles], F32)

    k = K(nc)
    P = Psum(k)
    ident, identb, io, iob = make_ident(k)

    def flat(ap):
        return ap.rearrange("i p n -> (i p) n")

    nch = n_tiles // 4

    def mk_gather(src, dst, dst_cb, rd, lag):
        done = set()

        def emit(c):
            if c in done or c >= nch or c < 0:
                return
            done.add(c)
            k.collective("AllGather", flat(T[src][c * 4:(c + 1) * 4]), flat(T[dst][c * 16:(c + 1) * 16]), GROUPS,
                         rd(c), [dst_cb[c]])

        def f(i, final=False):
            if final:
                for c in range(nch):
                    emit(c)
            else:
                c = (i - 3 - lag) // 4
                if (i - 3 - lag) % 4 == 0:
                    emit(c)
        return f

    T["yg_mb"] = [Buf() for _ in range(n_tiles)]
    T["yg_yb"] = [Buf() for _ in range(n_tiles)]
    T["ygall_cb"] = [Buf() for _ in range(nch)]
    T["gather_yg"] = mk_gather("yg", "ygall", T["ygall_cb"], lambda c: T["yg_yb"][c * 4:(c + 1) * 4], 1)
    T["ygmall_cb"] = [Buf() for _ in range(nch)]
    T["gather_ygm"] = mk_gather("ygm", "ygmall", T["ygmall_cb"], lambda c: T["yg_mb"][c * 4:(c + 1) * 4], 2)
    T["h1gt_tb"] = [Buf() for _ in range(n_tiles)]
    T["h1gtall_cb"] = [Buf() for _ in range(nch)]
    T["gather_h1"] = mk_gather("h1gt", "h1gtall", T["h1gtall_cb"], lambda c: T["h1gt_tb"][c * 4:(c + 1) * 4], 2)
    T["ogt_mb"] = [Buf() for _ in range(n_tiles)]
    T["ogt_yb"] = [[Buf(), Buf()] for _ in range(nch)]
    T["ogtall_cb"] = [Buf() for _ in range(nch)]
    g_og = mk_gather("ogt", "ogtall", T["ogtall_cb"], lambda c: T["ogt_yb"][c], 0)
    T["ogmall_cb"] = [Buf() for _ in range(nch)]
    T["gather_ogm"] = mk_gather("ogm", "ogmall", T["ogmall_cb"], lambda c: T["ogt_mb"][c * 4:(c + 1) * 4], 2)
    T["gather_og"] = lambda q: g_og(q * 4 + 3 - 4) if q > 0 else None

    k.phase_begin()
    phase_A(k, P, nc, ident, identb, io, iob, T, n_tiles)
    T["gather_ygm"](0, final=True)
    T["gather_yg"](0, final=True)
    k.phase_end()

    k.phase_begin()
    phase_B2_pre(k, P, nc, ident, identb, T)
    ssq, ssqb = phase_B(k, P, nc, ident, identb, T, n_tiles)
    T["gather_h1"](0, final=True)
    rstd, rstdb = sumsq_gather(k, P, nc, ssq, ssqb, T["ssq1"], T["ssq1_b"], T["ssq1all"], T["ssq1all_b"], GROUPS,
                               D, "B")
    phase_B2(k, P, nc, ident, identb, T, n_tiles, rstd, rstdb)
    T["gather_ogm"](0, final=True)
    k.phase_end()

    k.phase_begin()
    phase_C(k, P, nc, ident, identb, T, n_tiles)
    g_og(0, final=True)
    k.phase_end()

    k.phase_begin()
    phase_D(k, P, nc, ident, identb, T, n_tiles, GROUPS)
    k.finish()
    return nc, k


S = 16384


def rope_tab(seq, dim, theta):
    inv = (1.0 / (np.float32(theta) ** (np.arange(0, dim, 2, dtype=np.float32) / np.float32(dim)))).astype(np.float32)
    ang = (np.arange(seq, dtype=np.float32)[:, None] * inv[None, :]).astype(np.float32)
    return np.concatenate([np.cos(ang), np.sin(ang)], -1).astype(np.float32)


def prep_A(inp, b, g, S_=S):
    w = inp["ret_w_in"][0]
    hs = [2 * g, 2 * g + 1]
    cols = []
    for base, wd in ((0, 128), (1024, 128), (2048, 256), (4096, 256)):
        for h in hs:
            cols.append(np.arange(base + h * wd, base + (h + 1) * wd))
    cols.append(np.arange(6144 + g * 256, 6144 + (g + 1) * 256))
    cols.append(np.arange(7168 + g * 256, 7168 + (g + 1) * 256))
    cols = np.concatenate(cols)
    wkv = inp["mem_w_kv"][0]
    wkvc = np.concatenate([np.arange(g * 256, (g + 1) * 256), np.arange(1024 + g * 256, 1024 + (g + 1) * 256)])
    dec = np.array([inp["ret_decay_fwd"][0, hs[0]], inp["ret_decay_fwd"][0, hs[1]],
                    inp["ret_decay_bwd"][0, hs[0]], inp["ret_decay_bwd"][0, hs[1]]], np.float32)
    return {
        "xA": np.ascontiguousarray(inp["x"][b, :S_]),
        "ngA": np.ascontiguousarray(inp["norm_g"][0]),
        "wA": np.ascontiguousarray(w[:, cols]),
        "memA": np.ascontiguousarray(inp["mem"][b]),
        "mngA": np.ascontiguousarray(inp["mem_norm_g"][0]),
        "wkvA": np.ascontiguousarray(wkv[:, wkvc]),
        "decA": dec,
        "rcsA": rope_tab(S_, 128, 10000.0),
    }


def prep_fused(inp, core, S_=16384):
    b, j = core // 4, core % 4
    d = prep_A(inp, b, j, S_)
    rows = []
    for g in range(4):
        rows.append(np.arange(g * 512, (g + 1) * 512))
        rows.append(np.arange(2048 + g * 256, 2048 + (g + 1) * 256))
    rows = np.concatenate(rows)
    cs = slice(j * 256, (j + 1) * 256)
    w = inp["diff_w_in"][0]
    hs = [2 * j, 2 * j + 1]
    cols = []
    for base in (0, 2048, 4096, 6144):
        for h in hs:
            cols.append(np.arange(base + h * 256, base + (h + 1) * 256))
    cols.append(np.arange(8192 + j * 256, 8192 + (j + 1) * 256))
    cols.append(np.arange(9216 + j * 256, 9216 + (j + 1) * 256))
    cols = np.concatenate(cols)
    wkv = inp["mem_w_kv"][1]
    kvc = np.concatenate([np.arange(j * 256, (j + 1) * 256), np.arange(1024 + j * 256, 1024 + (j + 1) * 256)])
    d.update({
        "xcB": np.ascontiguousarray(inp["x"][b, :S_, cs]),
        "woB": np.ascontiguousarray(inp["ret_w_out"][0][rows][:, cs]),
        "ngcB": np.ascontiguousarray(inp["norm_g"][1][cs]),
        "wB": np.ascontiguousarray(w[:, cols]),
        "memB": np.ascontiguousarray(inp["mem"][b]),
        "mngB": np.ascontiguousarray(inp["mem_norm_g"][1]),
        "wkvB": np.ascontiguousarray(wkv[:, kvc]),
        "dcsB": np.ascontiguousarray(rope_tab(S_, 32, 500000.0)),
        "lamC": np.ascontiguousarray(np.stack([inp["diff_lambda_q1"][0], inp["diff_lambda_k1"][0],
                                               inp["diff_lambda_q2"][0], inp["diff_lambda_k2"][0]])),
        "slgC": np.ascontiguousarray(inp["diff_subln_g"][0]),
        "woD": np.ascontiguousarray(inp["diff_w_out"][0][rows][:, cs]),
        "fgcD": np.ascontiguousarray(inp["final_norm_g"][cs]),
    })
    return d


_CACHE = {}


def kernel(**inputs):
    inp = {k_: np.asarray(v) for k_, v in inputs.items()}
    cores = list(range(8))
    if "nc" not in _CACHE:
        _CACHE["nc"] = build_fused(128)[0]
    res = run_bass_kernel_spmd(_CACHE["nc"], [prep_fused(inp, c) for c in cores], core_ids=cores).results
    out = np.empty((2, 16384, 1024), np.float32)
    for c in cores:
        out[c // 4, :, (c % 4) * 256:(c % 4 + 1) * 256] = np.asarray(res[c]["out"])
    return out
```
